# Optimizing a Trainium2 kernel written in Bass

```python
import math
import jax, jax.numpy as jnp
from jax import lax
import numpy as np

D_MODEL = 2048
BATCH = 4
SEQ = 2048
DEPTH = 2
DEC_BATCH = 32
DEC_SEQ = 8
PAST_LEN = 8192
PAGE_SIZE = 128

GDN_HEADS = 6
GDN_DK = 128
GDN_DV = 128
GDN_CONV = 4
GDN_CHUNK = 64
DIFF_HEADS = 4
DIFF_QK = 64
DIFF_DV = 128
ROPE_THETA = 500000.0
ROPE_DIM = DIFF_QK // 4
Q_BLOCK = 128
GLA_HEADS = 6
GLA_DK = 64
GLA_DV = 128
GLA_RANK = 16
GLA_TAU = 16.0
GLA_CHUNK = 64
D_FF = -(-8 * D_MODEL // (3 * 256)) * 256
DEEPNORM_ALPHA = (2 * DEPTH) ** 0.25
DEEPNORM_BETA = (8 * DEPTH) ** -0.25

GDN_QKV_W = 2 * GDN_HEADS * GDN_DK + GDN_HEADS * GDN_DV
GDN_V_W = GDN_HEADS * GDN_DV
DIFF_QK_W = DIFF_HEADS * 2 * DIFF_QK
DIFF_V_W = DIFF_HEADS * DIFF_DV
GLA_QK_W = GLA_HEADS * GLA_DK
GLA_V_W = GLA_HEADS * GLA_DV
SPLIT_SIZES = (GDN_QKV_W, GDN_HEADS, GDN_HEADS, GDN_V_W,
               DIFF_QK_W, DIFF_QK_W, DIFF_V_W,
               GLA_QK_W, GLA_QK_W, GLA_V_W, GLA_RANK, GLA_V_W)
IN_WIDTH = sum(SPLIT_SIZES)
MIX_WIDTH = GDN_V_W + DIFF_V_W + GLA_V_W

kernel_name = 'hybrid_gdn_diffattn_gla_deepnorm_step'


def _split_points():
    return [int(s) for s in np.cumsum(SPLIT_SIZES)[:-1]]


def _layer_norm(x, g, b, eps=1e-5):
    xf = x.astype(jnp.float32)
    mu = jnp.mean(xf, -1, keepdims=True)
    var = jnp.mean(jnp.square(xf - mu), -1, keepdims=True)
    return ((xf - mu) * lax.rsqrt(var + eps) * g.astype(jnp.float32) + b.astype(jnp.float32)).astype(x.dtype)


def _rms_norm(x, g, eps=1e-6):
    xf = x.astype(jnp.float32)
    return xf * lax.rsqrt(jnp.mean(xf * xf, -1, keepdims=True) + eps) * g.astype(jnp.float32)


def _l2norm(x, eps=1e-6):
    xf = x.astype(jnp.float32)
    return xf * lax.rsqrt(jnp.sum(xf * xf, -1, keepdims=True) + eps)


def _rope_partial(x, pos):
    half = ROPE_DIM // 2
    inv = 1.0 / (ROPE_THETA ** (jnp.arange(half, dtype=jnp.float32) * 2.0 / ROPE_DIM))
    ang = pos.astype(jnp.float32)[:, None] * inv[None, :]
    cos = jnp.cos(ang)[None, :, None, None, :]
    sin = jnp.sin(ang)[None, :, None, None, :]
    x1 = x[..., :half].astype(jnp.float32)
    x2 = x[..., half:ROPE_DIM].astype(jnp.float32)
    rot = jnp.concatenate([x1 * cos - x2 * sin, x2 * cos + x1 * sin], -1).astype(x.dtype)
    return jnp.concatenate([rot, x[..., ROPE_DIM:]], -1)


def _short_conv(u, buf, w):
    T = u.shape[1]
    full = jnp.concatenate([buf, u], axis=1)
    out = full[:, 0:T] * w[0]
    for j in range(1, GDN_CONV):
        out = out + full[:, j:j + T] * w[j]
    return jax.nn.silu(out), full[:, -(GDN_CONV - 1):]


def _pad_chunks(arrs, T, chunk):
    C = min(chunk, T)
    Tp = -(-T // C) * C
    N = Tp // C
    out = []
    for a in arrs:
        a = a.astype(jnp.float32)
        if Tp != T:
            a = jnp.pad(a, [(0, 0), (0, Tp - T)] + [(0, 0)] * (a.ndim - 2))
        B = a.shape[0]
        out.append(jnp.moveaxis(a.reshape((B, N, C) + a.shape[2:]), 1, 0))
    return out, C, Tp


def _gated_delta(q, k, v, beta, g, S0):
    B, T, H, _ = q.shape
    DV = v.shape[-1]
    (qc, kc, vc, bc, gc), C, Tp = _pad_chunks([q, k, v, beta, g], T, GDN_CHUNK)
    tri = jnp.tril(jnp.ones((C, C), bool))
    strict = jnp.tril(jnp.ones((C, C), bool), -1)

    def step(S, inp):
        qi, ki, vi, bi, gi = inp
        gcum = jnp.cumsum(gi, axis=1)
        gh = jnp.moveaxis(gcum, 1, 2)
        bh = jnp.moveaxis(bi, 1, 2)
        dmat = jnp.exp(jnp.where(tri, gh[..., :, None] - gh[..., None, :], -jnp.inf))
        kk = jnp.einsum('bihd,bjhd->bhij', ki, ki)
        A = jnp.where(strict, kk * dmat * bh[..., :, None], 0.0)
        M = jnp.eye(C, dtype=jnp.float32) + A
        vbeta = jnp.moveaxis(vi * bi[..., None], 1, 2)
        kbd = jnp.moveaxis(ki * (bi * jnp.exp(gcum))[..., None], 1, 2)
        sol = lax.linalg.triangular_solve(M, jnp.concatenate([vbeta, kbd], -1),
                                          left_side=True, lower=True, unit_diagonal=True)
        u = sol[..., :DV] - jnp.einsum('bhck,bhkv->bhcv', sol[..., DV:], S)
        qk = jnp.einsum('bihd,bjhd->bhij', qi, ki) * dmat
        qh = jnp.moveaxis(qi, 1, 2)
        o = jnp.einsum('bhck,bhkv->bhcv', qh * jnp.exp(gh)[..., None], S) + jnp.einsum('bhij,bhjv->bhiv', qk, u)
        glast = gh[..., -1]
        kw = jnp.moveaxis(ki, 1, 2) * jnp.exp(glast[..., None] - gh)[..., None]
        S_new = S * jnp.exp(glast)[..., None, None] + jnp.einsum('bhck,bhcv->bhkv', kw, u)
        return S_new, jnp.moveaxis(o, 1, 2)

    S, o = lax.scan(step, S0.astype(jnp.float32), (qc, kc, vc, bc, gc))
    o = jnp.moveaxis(o, 0, 1).reshape(B, Tp, H, DV)[:, :T]
    return o, S


def _gla(q, k, v, alog, S0):
    B, T, H, _ = q.shape
    DV = v.shape[-1]
    (qc, kc, vc, ac), C, Tp = _pad_chunks([q, k, v, alog], T, GLA_CHUNK)
    tri = jnp.tril(jnp.ones((C, C), bool))[None, :, :, None, None]

    def step(S, inp):
        qi, ki, vi, ai = inp
        bc = jnp.cumsum(ai, axis=1)
        dec = jnp.exp(jnp.where(tri, bc[:, :, None] - bc[:, None, :], -jnp.inf))
        attn = jnp.einsum('bihd,bijhd,bjhd->bhij', qi, dec, ki)
        o = jnp.einsum('bihd,bhdv->bihv', qi * jnp.exp(bc), S) + jnp.einsum('bhij,bjhv->bihv', attn, vi)
        blast = bc[:, -1]
        kw = ki * jnp.exp(blast[:, None] - bc)
        S_new = S * jnp.exp(blast)[..., None] + jnp.einsum('bchd,bchv->bhdv', kw, vi)
        return S_new, o

    S, o = lax.scan(step, S0.astype(jnp.float32), (qc, kc, vc, ac))
    o = jnp.moveaxis(o, 0, 1).reshape(B, Tp, H, DV)[:, :T]
    return o, S


def _diff_probs(s, mask, lam):
    p = jax.nn.softmax(jnp.where(mask, s, -jnp.inf), axis=-1)
    return p[:, :, 0] - lam * p[:, :, 1]


def _diff_attn_prompt(q, k, v, lam):
    B, T, H, _, D = q.shape
    nb = T // Q_BLOCK
    scale = DIFF_QK ** -0.5
    qb = jnp.swapaxes(q.reshape(B, nb, Q_BLOCK, H, 2, D), 0, 1)
    kpos = jnp.arange(T)

    def block(args):
        qi, i = args
        s = jnp.einsum('bqhmd,bkhmd->bhmqk', qi, k).astype(jnp.float32) * scale
        qpos = i * Q_BLOCK + jnp.arange(Q_BLOCK)
        pd = _diff_probs(s, kpos[None, :] <= qpos[:, None], lam)
        return jnp.einsum('bhqk,bkhv->bqhv', pd.astype(v.dtype), v)

    o = lax.map(block, (qb, jnp.arange(nb)))
    return jnp.swapaxes(o, 0, 1).reshape(B, T, H, v.shape[-1])


def _diff_attn_sample(q, k, v, k_past, v_past, lam):
    T = q.shape[1]
    P = k_past.shape[1]
    scale = DIFF_QK ** -0.5
    s_past = jnp.einsum('bqhmd,bkhmd->bhmqk', q, k_past).astype(jnp.float32) * scale
    s_new = jnp.einsum('bqhmd,bkhmd->bhmqk', q, k).astype(jnp.float32) * scale
    mask = jnp.concatenate([jnp.ones((T, P), bool), jnp.tril(jnp.ones((T, T), bool))], -1)
    pd = _diff_probs(jnp.concatenate([s_past, s_new], -1), mask, lam).astype(v.dtype)
    return (jnp.einsum('bhqk,bkhv->bqhv', pd[..., :P], v_past)
            + jnp.einsum('bhqk,bkhv->bqhv', pd[..., P:], v))


def _mix(x, pos, lw, lam_init, past):
    B, T, _ = x.shape
    dt = x.dtype
    proj = jnp.einsum('btd,de->bte', x, lw['w_in'])
    (qkv_raw, b_raw, a_raw, z, dq, dk, dv, cq, ck, cv, c_lr, c_r) = jnp.split(proj, _split_points(), axis=-1)

    if past is None:
        buf = jnp.zeros((B, GDN_CONV - 1, GDN_QKV_W), dt)
        S_a0 = jnp.zeros((B, GDN_HEADS, GDN_DK, GDN_DV), jnp.float32)
        S_c0 = jnp.zeros((B, GLA_HEADS, GLA_DK, GLA_DV), jnp.float32)
    else:
        k_past, v_past, S_a0, buf, S_c0 = past
        buf = buf.astype(dt)
    qkv, new_buf = _short_conv(qkv_raw, buf, lw['gdn_conv_w'])
    aq, ak, av = jnp.split(qkv, [GDN_HEADS * GDN_DK, 2 * GDN_HEADS * GDN_DK], axis=-1)
    aq = _l2norm(aq.reshape(B, T, GDN_HEADS, GDN_DK)) * (GDN_DK ** -0.5)
    ak = _l2norm(ak.reshape(B, T, GDN_HEADS, GDN_DK))
    av = av.reshape(B, T, GDN_HEADS, GDN_DV)
    beta = jax.nn.sigmoid(b_raw.astype(jnp.float32))
    glog = -jnp.exp(lw['gdn_a_log'].astype(jnp.float32)) * jax.nn.softplus(
        a_raw.astype(jnp.float32) + lw['gdn_dt_bias'].astype(jnp.float32))
    o_a, S_a = _gated_delta(aq, ak, av, beta, glog, S_a0)
    o_a = (_rms_norm(o_a, lw['gdn_norm_g']) * jax.nn.silu(z.reshape(B, T, GDN_HEADS, GDN_DV).astype(jnp.float32)))
    o_a = o_a.astype(dt).reshape(B, T, GDN_V_W)

    dq = _rope_partial(dq.reshape(B, T, DIFF_HEADS, 2, DIFF_QK), pos)
    dk = _rope_partial(dk.reshape(B, T, DIFF_HEADS, 2, DIFF_QK), pos)
    dv = dv.reshape(B, T, DIFF_HEADS, DIFF_DV)
    lv = lw['diff_lambda'].astype(jnp.float32)
    lam = jnp.exp(jnp.sum(lv[0] * lv[1])) - jnp.exp(jnp.sum(lv[2] * lv[3])) + lam_init
    if past is None:
        o_b = _diff_attn_prompt(dq, dk, dv, lam)
    else:
        o_b = _diff_attn_sample(dq, dk, dv, k_past.astype(dt), v_past.astype(dt), lam)
    o_b = (_rms_norm(o_b, lw['diff_norm_g']) * (1.0 - lam_init)).astype(dt).reshape(B, T, DIFF_V_W)

    cq = cq.reshape(B, T, GLA_HEADS, GLA_DK).astype(jnp.float32) * (GLA_DK ** -0.5)
    ck = ck.reshape(B, T, GLA_HEADS, GLA_DK)
    cv = cv.reshape(B, T, GLA_HEADS, GLA_DV)
    alog = jax.nn.log_sigmoid((jnp.einsum('btr,re->bte', c_lr, lw['gla_wa2']) + lw['gla_ba']).astype(jnp.float32)) / GLA_TAU
    o_c, S_c = _gla(cq, ck, cv, alog.reshape(B, T, GLA_HEADS, GLA_DK), S_c0)
    o_c = (_rms_norm(o_c, lw['gla_norm_g']) * jax.nn.silu(c_r.reshape(B, T, GLA_HEADS, GLA_DV).astype(jnp.float32)))
    o_c = o_c.astype(dt).reshape(B, T, GLA_V_W)

    out = jnp.einsum('bte,ed->btd', jnp.concatenate([o_a, o_b, o_c], -1), lw['w_out'])
    new_state = (dk.reshape(B, T, DIFF_HEADS, 2 * DIFF_QK), dv, S_a.astype(dt), new_buf, S_c.astype(dt))
    return out, new_state


def _layer(x, pos, lw, lam_init, past):
    mix, st = _mix(x, pos, lw, lam_init, past)
    h = _layer_norm(DEEPNORM_ALPHA * x + mix, lw['ln1_g'], lw['ln1_b'])
    gu = jnp.einsum('btd,df->btf', h, lw['w_gate_up'])
    g, u = jnp.split(gu, 2, axis=-1)
    f = jnp.einsum('btf,fd->btd', jax.nn.silu(g) * u, lw['w_down'])
    return _layer_norm(DEEPNORM_ALPHA * h + f, lw['ln2_g'], lw['ln2_b']), st


def setup_inputs(seed: int = 0) -> dict:
    key = jax.random.key(seed)
    ks = jax.random.split(key, 32)
    f32 = jnp.float32
    n_pages = PAST_LEN // PAGE_SIZE
    n_used = DEC_BATCH * n_pages
    n_pool = n_used + max(1, n_used // 4)

    def nrm(k, shape, s):
        return jax.random.normal(k, shape, f32) * s

    page_table = jax.random.permutation(ks[0], n_pool)[:n_used].reshape(DEC_BATCH, n_pages).astype(jnp.int32)
    dtv = jnp.exp(jax.random.uniform(ks[12], (DEPTH, GDN_HEADS), f32) * (math.log(0.1) - math.log(0.001)) + math.log(0.001))
    return {
        'x_prompt': nrm(ks[1], (BATCH, SEQ, D_MODEL), 1.0),
        'x_sample': nrm(ks[2], (DEC_BATCH, DEC_SEQ, D_MODEL), 1.0),
        'cache_k': nrm(ks[3], (DEPTH, n_pool, PAGE_SIZE, DIFF_HEADS, 2 * DIFF_QK), 1.0),
        'cache_v': nrm(ks[4], (DEPTH, n_pool, PAGE_SIZE, DIFF_HEADS, DIFF_DV), 1.0),
        'state_gdn': nrm(ks[5], (DEPTH, DEC_BATCH, GDN_HEADS, GDN_DK, GDN_DV), GDN_DK ** -0.5),
        'state_gdn_conv': nrm(ks[6], (DEPTH, DEC_BATCH, GDN_CONV - 1, GDN_QKV_W), 1.0),
        'state_gla': nrm(ks[7], (DEPTH, DEC_BATCH, GLA_HEADS, GLA_DK, GLA_DV), 1.0),
        'page_table': page_table,
        'w_in': nrm(ks[8], (DEPTH, D_MODEL, IN_WIDTH), D_MODEL ** -0.5),
        'gdn_conv_w': nrm(ks[9], (DEPTH, GDN_CONV, GDN_QKV_W), GDN_CONV ** -0.5),
        'gdn_a_log': jnp.log(jax.random.uniform(ks[10], (DEPTH, GDN_HEADS), f32, 1.0, 16.0)),
        'gdn_dt_bias': dtv + jnp.log(-jnp.expm1(-dtv)),
        'gdn_norm_g': 1.0 + nrm(ks[11], (DEPTH, GDN_DV), 0.02),
        'diff_lambda': nrm(ks[13], (DEPTH, 4, DIFF_QK), 0.1),
        'diff_norm_g': 1.0 + nrm(ks[14], (DEPTH, DIFF_DV), 0.02),
        'gla_wa2': nrm(ks[15], (DEPTH, GLA_RANK, GLA_QK_W), GLA_RANK ** -0.5),
        'gla_ba': nrm(ks[16], (DEPTH, GLA_QK_W), 0.1),
        'gla_norm_g': 1.0 + nrm(ks[17], (DEPTH, GLA_DV), 0.02),
        'w_out': nrm(ks[18], (DEPTH, MIX_WIDTH, D_MODEL), MIX_WIDTH ** -0.5 * DEEPNORM_BETA),
        'ln1_g': 1.0 + nrm(ks[19], (DEPTH, D_MODEL), 0.02),
        'ln1_b': nrm(ks[20], (DEPTH, D_MODEL), 0.02),
        'ln2_g': 1.0 + nrm(ks[21], (DEPTH, D_MODEL), 0.02),
        'ln2_b': nrm(ks[22], (DEPTH, D_MODEL), 0.02),
        'w_gate_up': nrm(ks[23], (DEPTH, D_MODEL, 2 * D_FF), D_MODEL ** -0.5),
        'w_down': nrm(ks[24], (DEPTH, D_FF, D_MODEL), D_FF ** -0.5 * DEEPNORM_BETA),
    }


def reference(x_prompt, x_sample, cache_k, cache_v, state_gdn, state_gdn_conv, state_gla, page_table,
              w_in, gdn_conv_w, gdn_a_log, gdn_dt_bias, gdn_norm_g, diff_lambda, diff_norm_g,
              gla_wa2, gla_ba, gla_norm_g, w_out, ln1_g, ln1_b, ln2_g, ln2_b, w_gate_up, w_down):
    Bs, Ts = x_sample.shape[0], x_sample.shape[1]
    past_len = page_table.shape[1] * cache_k.shape[2]
    pos_p = jnp.arange(x_prompt.shape[1])
    pos_s = past_len + jnp.arange(Ts)
    hp, hs = x_prompt, x_sample
    kp, vp, gp, cp, lp = [], [], [], [], []
    ksm, vsm, gs, cs, ls = [], [], [], [], []
    for li in range(DEPTH):
        lw = dict(w_in=w_in[li], gdn_conv_w=gdn_conv_w[li], gdn_a_log=gdn_a_log[li], gdn_dt_bias=gdn_dt_bias[li],
                  gdn_norm_g=gdn_norm_g[li], diff_lambda=diff_lambda[li], diff_norm_g=diff_norm_g[li],
                  gla_wa2=gla_wa2[li], gla_ba=gla_ba[li], gla_norm_g=gla_norm_g[li], w_out=w_out[li],
                  ln1_g=ln1_g[li], ln1_b=ln1_b[li], ln2_g=ln2_g[li], ln2_b=ln2_b[li],
                  w_gate_up=w_gate_up[li], w_down=w_down[li])
        lam_init = 0.8 - 0.6 * math.exp(-0.3 * li)
        k_past = cache_k[li][page_table].reshape(Bs, past_len, DIFF_HEADS, 2, DIFF_QK)
        v_past = cache_v[li][page_table].reshape(Bs, past_len, DIFF_HEADS, DIFF_DV)
        hp, st_p = _layer(hp, pos_p, lw, lam_init, None)
        hs, st_s = _layer(hs, pos_s, lw, lam_init, (k_past, v_past, state_gdn[li], state_gdn_conv[li], state_gla[li]))
        kp.append(st_p[0]); vp.append(st_p[1]); gp.append(st_p[2]); cp.append(st_p[3]); lp.append(st_p[4])
        ksm.append(st_s[0]); vsm.append(st_s[1]); gs.append(st_s[2]); cs.append(st_s[3]); ls.append(st_s[4])
    return (hp, hs,
            jnp.stack(kp), jnp.stack(vp), jnp.stack(gp), jnp.stack(cp), jnp.stack(lp),
            jnp.stack(ksm), jnp.stack(vsm), jnp.stack(gs), jnp.stack(cs), jnp.stack(ls))
```

```python
import math
from contextlib import ExitStack

import numpy as np
import concourse.bass as bass
import concourse.mybir as mybir
from concourse.bass_utils import run_bass_kernel_spmd

F32 = mybir.dt.float32
BF16 = mybir.dt.bfloat16
I32 = mybir.dt.int32
ALU = mybir.AluOpType
AF = mybir.ActivationFunctionType
AX = mybir.AxisListType

D = 2048
KC = 16
DEPTH = 2
DFF = 5632
FC = 44
FH = 22
NQ = 4
FQ = 11
NB_S = 4
TS = 8
NS = NB_S * TS
ALPHA = (2 * DEPTH) ** 0.25
N_IN_BLK = 57
EPOCH = 30000
NSLOT = {"sp": 24, "pool": 12, "act": 8}


class Buf:
    __slots__ = ("name", "lw", "rd")

    def __init__(self, name=""):
        self.name = name
        self.lw = None
        self.rd = {}


class Op:
    __slots__ = ("eng", "fn", "deps", "is_dma", "slot", "use", "needed", "sig", "uid")


class V:
    __slots__ = ("ap", "bs")

    def __init__(self, ap, bs):
        self.ap = ap
        self.bs = tuple(bs)


class Sched:
    def __init__(self, nc):
        self.nc = nc
        self.ops = {e: [] for e in ("pe", "act", "dve", "pool", "sp")}
        self.slot_rr = {q: 0 for q in NSLOT}
        self.slot_last = {q: [None] * NSLOT[q] for q in NSLOT}
        self.slot_use = {q: [0] * NSLOT[q] for q in NSLOT}
        self.uid = 0

    def rec(self, eng, fn, reads, writes, is_dma=False):
        op = Op()
        op.eng = eng
        op.fn = fn
        op.is_dma = is_dma
        op.needed = is_dma
        op.sig = None
        op.uid = self.uid
        self.uid += 1
        deps = {}
        raw = set()
        for b in reads:
            if b.lw is not None:
                deps[b.lw.uid] = b.lw
                raw.add(b.lw.uid)
        for b in writes:
            if b.lw is not None:
                deps[b.lw.uid] = b.lw
            for r in b.rd.values():
                deps[r.uid] = r
        if is_dma:
            s = self.slot_rr[eng]
            self.slot_rr[eng] = (s + 1) % NSLOT[eng]
            prev = self.slot_last[eng][s]
            if prev is not None:
                deps[prev.uid] = prev
            self.slot_last[eng][s] = op
            self.slot_use[eng][s] += 1
            op.slot = s
            op.use = self.slot_use[eng][s]
        final = []
        for d in deps.values():
            if (not d.is_dma) and (not is_dma) and d.eng == eng:
                if eng == "pe":
                    continue
            d.needed = True
            final.append(d)
        op.deps = final
        for b in reads:
            b.rd[("d", op.uid) if is_dma else eng] = op
        for b in writes:
            b.lw = op
            b.rd = {}
        self.ops[eng].append(op)
        return op

    def emit(self, stack):
        nc = self.nc
        nsig = {}
        for e in ("pe", "act", "dve", "pool"):
            c = 0
            for op in self.ops[e]:
                if (not op.is_dma) and op.needed:
                    op.sig = c
                    c += 1
            nsig[e] = c
        sems = {}
        for e in ("pe", "act", "dve", "pool"):
            for k in range(nsig[e] // EPOCH + 1):
                sems[(e, k)] = stack.enter_context(nc.semaphore(f"s_{e}_{k}"))
        final_waits = {}
        for q in NSLOT:
            for s in range(NSLOT[q]):
                if self.slot_use[q][s] > 0:
                    sems[("d", q, s)] = stack.enter_context(nc.semaphore(f"d_{q}_{s}"))
                    final_waits[("d", q, s)] = 16 * self.slot_use[q][s]

        def tok(d):
            if d.is_dma:
                return ("d", d.eng, d.slot), 16 * d.use
            return (d.eng, d.sig // EPOCH), d.sig % EPOCH + 1

        def run(engname, eng):
            waited = {}
            for op in self.ops[engname]:
                need = {}
                for d in op.deps:
                    k, v = tok(d)
                    if need.get(k, 0) < v:
                        need[k] = v
                for k, v in need.items():
                    if waited.get(k, 0) >= v:
                        continue
                    eng.wait_ge(sems[k], v)
                    waited[k] = v
                ins = op.fn(eng)
                if op.is_dma:
                    ins.then_inc(sems[("d", op.eng, op.slot)], 16)
                elif op.sig is not None:
                    ins.then_inc(sems[(op.eng, op.sig // EPOCH)], 1)
            if engname == "sp":
                for k, v in final_waits.items():
                    if waited.get(k, 0) < v:
                        eng.wait_ge(sems[k], v)

        block = stack.enter_context(nc.Block())

        @block.tensor
        def _(e):
            run("pe", e)

        @block.scalar
        def _(e):
            run("act", e)

        @block.vector
        def _(e):
            run("dve", e)

        @block.gpsimd
        def _(e):
            run("pool", e)

        @block.sync
        def _(e):
            run("sp", e)


class Tile:
    def __init__(self, st, nc, name, shape, dtype, nb=1, psum=False):
        if psum:
            self.t = st.enter_context(nc.psum_tensor(name, shape, dtype))
        else:
            self.t = st.enter_context(nc.sbuf_tensor(name, shape, dtype))
        self.b = [Buf(f"{name}{i}") for i in range(nb)]

    def __call__(self, key=None, bi=0):
        ap = self.t[:] if key is None else self.t[key]
        if bi is None:
            return V(ap, self.b)
        return V(ap, (self.b[bi],))


class DT:
    def __init__(self, nc, name, shape, dtype, kind, nb=1):
        self.t = nc.dram_tensor(name, list(shape), dtype, kind=kind)
        self.ap = self.t.ap()
        self.b = [Buf(f"{name}{i}") for i in range(nb)]

    def __call__(self, ap=None, bi=0):
        return V(self.ap if ap is None else ap, (self.b[bi],))


def _bs(*vs):
    out = []
    for v in vs:
        if isinstance(v, V):
            out.extend(v.bs)
    return out


def _a(x):
    return x.ap if isinstance(x, V) else x


class KB:
    def __init__(self, S):
        self.S = S

    def mm(self, out, lhsT, rhs, start=True, stop=True):
        o, l, r = out.ap, lhsT.ap, rhs.ap
        self.S.rec("pe", lambda e: e.matmul(o, lhsT=l, rhs=r, start=start, stop=stop), _bs(lhsT, rhs), _bs(out))

    def tr(self, out, in_, ident):
        o, i, d = out.ap, in_.ap, ident.ap
        self.S.rec("pe", lambda e: e.transpose(o, i, d), _bs(in_, ident), _bs(out))

    def act(self, out, in_, func, bias=None, scale=None):
        o, i = out.ap, in_.ap
        kw = {}
        if bias is not None:
            kw["bias"] = _a(bias)
        if scale is not None:
            kw["scale"] = _a(scale)
        self.S.rec("act", lambda e: e.activation(out=o, in_=i, func=func, **kw), _bs(in_, bias, scale), _bs(out))

    def ts(self, out, in0, s1, s2=None, op0=ALU.mult, op1=None, eng="dve"):
        o, i = out.ap, in0.ap
        a1, a2 = _a(s1), _a(s2)
        if op1 is None:
            f = lambda e: e.tensor_scalar(out=o, in0=i, scalar1=a1, scalar2=None, op0=op0)
        else:
            f = lambda e: e.tensor_scalar(out=o, in0=i, scalar1=a1, scalar2=a2, op0=op0, op1=op1)
        self.S.rec(eng, f, _bs(in0, s1, s2), _bs(out))

    def tt(self, out, in0, in1, op, eng="dve"):
        o, i0, i1 = out.ap, in0.ap, in1.ap
        self.S.rec(eng, lambda e: e.tensor_tensor(out=o, in0=i0, in1=i1, op=op), _bs(in0, in1), _bs(out))

    def stt(self, out, in0, scalar, in1, op0, op1):
        o, i0, i1, s = out.ap, in0.ap, in1.ap, _a(scalar)
        self.S.rec("dve", lambda e: e.scalar_tensor_tensor(out=o, in0=i0, scalar=s, in1=i1, op0=op0, op1=op1),
                   _bs(in0, in1, scalar), _bs(out))

    def cp(self, out, in_, eng="dve"):
        o, i = out.ap, in_.ap
        if eng == "act":
            self.S.rec("act", lambda e: e.copy(out=o, in_=i), _bs(in_), _bs(out))
        else:
            self.S.rec(eng, lambda e: e.tensor_copy(out=o, in_=i), _bs(in_), _bs(out))

    def memset(self, out, val, eng="pool"):
        o = out.ap
        self.S.rec(eng, lambda e: e.memset(o, val), [], _bs(out))

    def recip(self, out, in_):
        o, i = out.ap, in_.ap
        self.S.rec("dve", lambda e: e.reciprocal(out=o, in_=i), _bs(in_), _bs(out))

    def rsum(self, out, in_):
        o, i = out.ap, in_.ap
        self.S.rec("dve", lambda e: e.reduce_sum(out=o, in_=i, axis=AX.X), _bs(in_), _bs(out))

    def dma(self, out, in_, q="sp"):
        o, i = out.ap, in_.ap
        self.S.rec(q, lambda e: e.dma_start(out=o, in_=i), _bs(in_), _bs(out), is_dma=True)

    def gather(self, out, table, idx):
        o, t, ix = out.ap, table.ap, idx.ap
        self.S.rec("pool", lambda e: e.indirect_dma_start(out=o, out_offset=None, in_=t,
                                                          in_offset=bass.IndirectOffsetOnAxis(ap=ix, axis=0)),
                   _bs(table, idx), _bs(out), is_dma=True)

    def iota_part(self, out):
        o = out.ap
        self.S.rec("pool", lambda e: e.iota(o, pattern=[[0, 1]], base=0, channel_multiplier=1), [], _bs(out))


def in_blocks():
    o_qkv = 0
    o_b = 2304
    o_a = 2310
    o_z = 2316
    o_dq = 3084
    o_dk = 3596
    o_dv = 4108
    o_cq = 4620
    o_ck = 5004
    o_cv = 5388
    o_lr = 6156
    o_cr = 6172
    blks = [("gb", o_b, 6), ("ga", o_a, 6)]
    for h in range(6):
        blks += [(f"aq{h}", o_qkv + h * 128, 128), (f"ak{h}", o_qkv + 768 + h * 128, 128),
                 (f"av{h}", o_qkv + 1536 + h * 128, 128), (f"az{h}", o_z + h * 128, 128)]
    for h in range(4):
        blks += [(f"dq{h}", o_dq + h * 128, 128), (f"dk{h}", o_dk + h * 128, 128), (f"dv{h}", o_dv + h * 128, 128)]
    blks += [("lr", o_lr, 16)]
    for p in range(3):
        blks += [(f"cq{p}", o_cq + p * 128, 128), (f"ck{p}", o_ck + p * 128, 128)]
        for h in (2 * p, 2 * p + 1):
            blks += [(f"cv{h}", o_cv + h * 128, 128), (f"cr{h}", o_cr + h * 128, 128)]
    assert len(blks) == N_IN_BLK
    return blks


BLK = {n: i for i, (n, _, _) in enumerate(in_blocks())}


def tok_tiles(n):
    nt = n // 128
    k = -(-nt // 4)
    out = []
    t = 0
    for i in range(k):
        m = (nt - t + (k - i) - 1) // (k - i)
        out.append((t * 128, m * 128))
        t += m
    return out


def group_tiles(SEQ):
    nt = SEQ // 128
    if nt >= 16:
        a = -(-nt // 3)
        return [a, (nt - a + 1) // 2, (nt - a) // 2]
    return [nt // 2, nt - nt // 2]


class HV:
    def __init__(self, tile, c, rows=None):
        self.ap = tile.t[:, c, :] if rows is None else tile.t[rows, c, :]
        self.bs = (tile.b[c],)

    def __call__(self, key=None):
        return V(self.ap if key is None else self.ap[key], self.bs)


def build_program(SEQ, PAST, NPOOL, dbg=None):
    GT = group_tiles(SEQ)
    NG = len(GT)
    GSZ = [t * 128 for t in GT]
    GOFF = [sum(GSZ[:i]) for i in range(NG)]
    NPGM = max(GSZ)
    NTPM = max(GT)
    NPAGES = PAST // 128
    NTOK = SEQ + NS
    NTmax = NPGM + NS
    nc = bass.Bass("TRN2", target_bir_lowering=False)
    EI, EO = "ExternalInput", "ExternalOutput"
    xpT = DT(nc, "xpT", [D, SEQ], F32, EI)
    xsT = DT(nc, "xsT", [D, NS], F32, EI)
    cache_k = DT(nc, "cache_k", [DEPTH, NPOOL * 128, 512], F32, EI)
    cache_v = DT(nc, "cache_v", [DEPTH, NPOOL * 128, 512], F32, EI)
    ptab = DT(nc, "ptab", [1, NB_S * NPAGES], I32, EI)
    st_gdn = DT(nc, "st_gdn", [DEPTH, NB_S, 6, 128, 128], F32, EI)
    st_conv = DT(nc, "st_conv", [DEPTH, NB_S, 128, 18, 3], F32, EI)
    st_gla = DT(nc, "st_gla", [DEPTH, NB_S, 3, 128, 128], F32, EI)
    w_in = DT(nc, "w_in", [DEPTH, N_IN_BLK, 128, KC, 128], F32, EI)
    w_out = DT(nc, "w_out", [DEPTH, KC, 128, KC, 128], F32, EI)
    w_gu = DT(nc, "w_gu", [DEPTH, 2 * FC, 128, KC, 128], F32, EI)
    w_dn = DT(nc, "w_dn", [DEPTH, NQ, KC, 128, FQ, 128], F32, EI)
    convw = DT(nc, "convw", [DEPTH, 128, 18, 4], F32, EI)
    NPC = 8 + 4 * KC
    pcol = DT(nc, "pcol", [DEPTH, 128, NPC], F32, EI)
    lamv = DT(nc, "lamv", [1, DEPTH * 256], F32, EI)
    wa2b = DT(nc, "wa2b", [DEPTH, 32, 384], F32, EI)
    consts = DT(nc, "consts", [128, 6, 128], F32, EI)
    ropec = DT(nc, "ropec", [2, 128, NTOK], F32, EI)
    yT = DT(nc, "yT", [D, NTOK], F32, EO)
    newk = DT(nc, "newk", [DEPTH, 4, 128, NTOK], F32, EO, nb=DEPTH * 4)
    newv = DT(nc, "newv", [DEPTH, NTOK, 512], F32, EO, nb=DEPTH * 4)
    o_gdn = DT(nc, "o_gdn", [DEPTH, 1 + NB_S, 6, 128, 128], F32, EO)
    o_conv = DT(nc, "o_conv", [DEPTH, 1 + NB_S, 128, 18, 3], F32, EO)
    o_gla = DT(nc, "o_gla", [DEPTH, 1 + NB_S, 3, 128, 128], F32, EO)
    dbg_out = None
    if dbg is not None:
        dbg_out = DT(nc, "dbg", list(dbg["shape"]), F32, EO)

    with ExitStack() as st:
        S = Sched(nc)
        K = KB(S)

        def T(name, shape, dt=F32, nb=1):
            return Tile(st, nc, name, shape, dt, nb=nb)

        A_ = slice(None)
        xT16 = T("xT16", [128, KC, NTmax], BF16, nb=KC)
        mixT = T("mixT", [128, KC, NTmax], BF16, nb=KC)
        hpre = T("hpre", [128, KC, NTmax], F32, nb=KC)
        NWS = 4
        wring = [T(f"wr{i}", [128, KC, 128], BF16) for i in range(NWS)]
        wctr = [0]
        C = T("constsb", [128, 6, 128], F32)
        ident, Utri, mSU, ones32, perm = (C((A_, i, A_)) for i in range(5))
        C16 = T("c16", [128, 2, 128], BF16)
        ones16 = C16((A_, 0, A_))
        Utri16 = C16((A_, 1, A_))
        rope = T("rope", [128, 2, NTmax], F32)
        pc = T("pc", [128, DEPTH, NPC], F32)
        cw = T("cw", [128, DEPTH, 18, 4], F32)
        lam = T("lam", [128, DEPTH, 4], F32)
        lv = T("lv", [128, DEPTH * 256], F32)
        cb = T("cb", [128, 4], F32)
        wab = T("wab", [32, DEPTH, 384], F32)
        Sg = T("Sg", [128, DEPTH, 6, 128], F32, nb=DEPTH * 6)
        Sl = T("Sl", [128, DEPTH, 3, 128], F32, nb=DEPTH * 3)
        ctail = T("ctail", [128, DEPTH, 18, 3], F32, nb=DEPTH * 18)
        KTh = T("KTh", [128, SEQ], BF16)
        Vh = T("Vh", [128, SEQ // 128, 128], BF16)
        IXi = T("IXi", [128, NB_S * NPAGES], I32)
        IXl = [IXi] + [T(f"IXi{l}", [128, NB_S * NPAGES], I32) for l in range(1, DEPTH)]
        IXf = T("IXf", [128, NB_S * NPAGES], F32)
        iot = T("iot", [128, 2], F32)
        ioti = T("ioti", [128, 1], I32)
        PS = [Tile(st, nc, f"ps{i}", [128, 512], F32, nb=1, psum=True) for i in range(8)]
        dctr = [0]

        def dps():
            p = PS[dctr[0] % 2]
            dctr[0] += 1
            return p

        mctr = [0]

        def mps(rows=128, cols=128):
            i = mctr[0] % 6
            mctr[0] += 1
            p = PS[2 + i]
            return V(p.t[0:rows, 0:cols], p.b)

        K.dma(C(), consts())
        K.dma(pc(), V(pcol.ap.rearrange("l p n -> p l n"), pcol.b))
        K.dma(cw(), V(convw.ap.rearrange("l p c j -> p l c j"), convw.b))
        K.dma(lv(), V(lamv.ap.partition_broadcast(128), lamv.b))
        K.dma(wab(), V(wa2b.ap.rearrange("l p n -> p l n"), wa2b.b))
        K.dma(IXi(), V(ptab.ap.partition_broadcast(128), ptab.b))
        K.cp(C16((A_, 0, A_)), ones32)
        K.cp(C16((A_, 1, A_)), Utri)
        for i, v in enumerate((1e-6, 1e-5, 1.0, 0.0)):
            K.memset(cb((A_, slice(i, i + 1))), v)
        eps6, eps5, one_c, zero_c = (cb((A_, slice(i, i + 1))) for i in range(4))
        K.memset(Sg(None, None), 0.0)
        K.memset(Sl(None, None), 0.0)
        K.memset(ctail(None, None), 0.0)
        K.iota_part(ioti())
        K.cp(iot((A_, slice(0, 1))), ioti())
        K.cp(IXf(), IXi())
        K.ts(IXf(), IXf(), 128.0, iot((A_, slice(0, 1))), ALU.mult, ALU.add)
        K.cp(IXi(), IXf())
        for l in range(1, DEPTH):
            K.ts(IXf(), IXf(), float(NPOOL * 128), None, ALU.add)
            K.cp(IXl[l](), IXf())
        lam_init = [0.8 - 0.6 * math.exp(-0.3 * li) for li in range(DEPTH)]
        ltmp = T("ltmp", [128, 64], F32)
        for li in range(DEPTH):
            for j in range(2):
                o = li * 256 + j * 128
                K.tt(ltmp(), lv((A_, slice(o, o + 64))), lv((A_, slice(o + 64, o + 128))), ALU.mult)
                K.rsum(lam((A_, li, slice(2 + j, 3 + j))), ltmp())
            K.act(lam((A_, li, slice(2, 4))), lam((A_, li, slice(2, 4))), AF.Exp)
            K.tt(lam((A_, li, slice(0, 1))), lam((A_, li, slice(2, 3))), lam((A_, li, slice(3, 4))), ALU.subtract)
            K.ts(lam((A_, li, slice(0, 1))), lam((A_, li, slice(0, 1))), lam_init[li], None, ALU.add)
            K.ts(lam((A_, li, slice(1, 2))), lam((A_, li, slice(0, 1))), -1.0, None, ALU.mult)
            K.act(pc((slice(0, 6), li, slice(0, 1))), pc((slice(0, 6), li, slice(0, 1))), AF.Exp)
            K.ts(pc((slice(0, 6), li, slice(0, 1))), pc((slice(0, 6), li, slice(0, 1))), -1.0, None, ALU.mult)

        def pcs(li, col, rows=128):
            return pc((slice(0, rows), li, slice(col, col + 1)))

        def wload(src_ap, src_v, kc):
            w = wring[wctr[0] % NWS]
            wctr[0] += 1
            K.dma(w((A_, slice(0, kc), A_)), V(src_ap, src_v.b), q="pool")
            return w

        def dense(w, kc, rhs_tile, ttiles, evac, M=128):
            for (t0, n) in ttiles:
                p = dps()
                pv = V(p.t[0:M, 0:n], p.b)
                for c in range(kc):
                    K.mm(pv, w((A_, c, slice(0, M))), rhs_tile((A_, c, slice(t0, t0 + n)), c),
                         start=(c == 0), stop=(c == kc - 1))
                evac(t0, n, pv)

        qT, kT, vT, zg, oTh, sq = (HV(hpre, c) for c in range(6))
        raw = [HV(hpre, 6), HV(hpre, 7)]
        vTb, zgb, oThb = HV(hpre, 8), HV(hpre, 9), HV(hpre, 10)
        gbT = HV(hpre, 11, slice(0, 32))
        ggT = HV(hpre, 12, slice(0, 32))
        clr1 = HV(hpre, 6, slice(0, 32))
        at0, at1 = HV(hpre, 12), HV(hpre, 13)
        rz0, rz1 = HV(hpre, 14), HV(hpre, 15)
        raws = [T(f"raws{i}", [128, NB_S, 3 + TS], F32) for i in range(2)]
        rctr = [0]
        rs = T("rs", [128, 512], F32)
        NTL = NTPM + NB_S
        btok = T("btok", [128, NTL, 6], F32)
        nbtok = T("nbtok", [128, NTL, 6], F32)
        gtok = T("gtok", [128, NTL, 6], F32)
        gctok = T("gctok", [128, NTL, 6], F32)
        NSL = 28
        smA = T("smA", [128, NSL, 128], F32, nb=NSL)

        class SV:
            def __init__(self, s0, n=1):
                self.ap = smA.t[:, s0, :] if n == 1 else smA.t[:, s0:s0 + n, :].rearrange("p a b -> p (a b)")
                self.bs = tuple(smA.b[s0:s0 + n])

            def __call__(self, key=None):
                return V(self.ap if key is None else self.ap[key], self.bs)

        names = ("ktok", "vtok", "R", "PT", "qTe", "kTe", "kw", "EG", "E", "ESU", "X", "XT", "X2", "XT2")
        gsm = [{n: SV(par * 14 + i) for i, n in enumerate(names)} for par in range(2)]
        t_r = [T(f"r{i}", [128, 128], F32) for i in range(2)]
        t_u = [T(f"u{i}", [128, 128], F32) for i in range(2)]
        t_Ug = [T(f"Ug{i}", [128, 128], F32) for i in range(2)]
        t_arg = [T(f"arg{i}", [128, 128], F32) for i in range(2)]
        t_gl = [T(f"gl{i}", [128, 2], F32) for i in range(2)]
        sst = [T(f"sst{i}", [128, 128], F32) for i in range(2)]
        sctr = [0]

        def rstd_from(pv, n, eps_v, scale):
            K.act(rs((A_, slice(0, n))), pv, AF.Ln, bias=eps_v, scale=scale)
            K.act(rs((A_, slice(0, n))), rs((A_, slice(0, n))), AF.Exp, scale=-0.5)

        def rmsnorm_gate(o_t, gate_t, gcol_v, ttiles, head, const_gate=None):
            for (t0, n) in ttiles:
                cs = slice(t0, t0 + n)
                K.act(sq((A_, cs)), o_t((A_, cs)), AF.Square)
                pp = dps()
                p = V(pp.t[:, 0:n], pp.b)
                K.mm(p, ones32, sq((A_, cs)))
                rstd_from(p, n, eps6, 1.0 / 128)
                K.tt(rs((A_, slice(0, n))), rs((A_, slice(0, n))), o_t((A_, cs)), ALU.mult)
                dst = mixT((A_, head, cs), head)
                if const_gate is None:
                    K.stt(dst, rs((A_, slice(0, n))), gcol_v, gate_t((A_, cs)), ALU.mult, ALU.mult)
                else:
                    K.ts(dst, rs((A_, slice(0, n))), gcol_v, const_gate, ALU.mult, ALU.mult)

        def l2norm(x_t, ttiles, scale):
            for (t0, n) in ttiles:
                cs = slice(t0, t0 + n)
                K.act(sq((A_, cs)), x_t((A_, cs)), AF.Square)
                pp = dps()
                p = V(pp.t[:, 0:n], pp.b)
                K.mm(p, ones32, sq((A_, cs)))
                rstd_from(p, n, eps6, 1.0)
                K.stt(x_t((A_, cs)), x_t((A_, cs)), scale, rs((A_, slice(0, n))), ALU.mult, ALU.mult)

        def sub(v, rows, cols=None):
            return V(v.ap[rows, :] if cols is None else v.ap[rows, cols], v.bs)

        def gdn_chunk(li, h, col0, Cn, gi, Sv, par):
            cs = slice(col0, col0 + Cn)
            g_ = gsm[par]
            ktok, vtok, R, PT, qTe, kTe, kw, EG, E, ESU, X, XT, X2, XT2 = (g_[n] for n in names)
            r_, u_, Ug, arg, gl = t_r[par], t_u[par], t_Ug[par], t_arg[par], t_gl[par]
            rC = slice(0, Cn)
            gcol = gtok((rC, gi, slice(h, h + 1)))
            gccol = gctok((rC, gi, slice(h, h + 1)))
            p = mps(Cn, 128)
            K.tr(p, kT((A_, cs)), ident)
            K.cp(ktok((rC, A_)), p, "act")
            p = mps(Cn, 128)
            K.tr(p, vT((A_, cs)), ident)
            K.cp(vtok((rC, A_)), p, "act")
            K.ts(Ug((rC, rC)), sub(Utri, rC, rC), gcol, None, ALU.mult, eng="pool")
            pg = mps(128, Cn)
            K.mm(pg, sub(ones32, rC), Ug((rC, rC)))
            K.ts(arg((rC, rC)), sub(pg, rC), gccol, 0.0, ALU.subtract, ALU.min)
            K.act(E((rC, rC)), arg((rC, rC)), AF.Exp)
            K.act(EG((A_, rC)), pg, AF.Exp)
            K.cp(gl((A_, slice(0, 1))), sub(pg, A_, slice(Cn - 1, Cn)))
            K.tt(ESU((rC, rC)), E((rC, rC)), sub(mSU, rC, rC), ALU.mult, eng="pool")
            K.tt(E((rC, rC)), E((rC, rC)), sub(Utri, rC, rC), ALU.mult, eng="pool")
            pk = mps(Cn, Cn)
            K.mm(pk, kT((A_, cs)), kT((A_, cs)))
            K.stt(X((rC, rC)), pk, nbtok((rC, gi, slice(h, h + 1))), ESU((rC, rC)), ALU.mult, ALU.mult)
            pq = mps(Cn, Cn)
            K.mm(pq, kT((A_, cs)), qT((A_, cs)))
            K.tt(PT((rC, rC)), pq, E((rC, rC)), ALU.mult)
            K.tt(qTe((A_, rC)), qT((A_, cs)), EG((A_, rC)), ALU.mult, eng="pool")
            K.tt(kTe((A_, rC)), kT((A_, cs)), EG((A_, rC)), ALU.mult, eng="pool")
            K.act(gl((rC, slice(1, 2))), gccol, AF.Exp, bias=gl((rC, slice(0, 1))), scale=-1.0)
            K.ts(kw((rC, A_)), ktok((rC, A_)), gl((rC, slice(1, 2))), None, ALU.mult, eng="pool")
            p = mps(Cn, Cn)
            K.tr(p, X((rC, rC)), sub(ident, rC, rC))
            K.cp(XT((rC, rC)), p, "act")
            K.tt(R((rC, rC)), X((rC, rC)), sub(ident, rC, rC), ALU.add, eng="pool")
            nst = max(1, int(math.ceil(math.log2(Cn))) - 1)
            Xc, XTc, Xn, XTn = X, XT, X2, XT2
            for k in range(nst):
                p1 = mps(Cn, Cn)
                K.mm(p1, Xc((rC, rC)), XTc((rC, rC)))
                K.cp(XTn((rC, rC)), p1, "act")
                if k < nst - 1:
                    p2 = mps(Cn, Cn)
                    K.mm(p2, XTc((rC, rC)), Xc((rC, rC)))
                    K.cp(Xn((rC, rC)), p2, "dve")
                p3 = mps(Cn, Cn)
                K.mm(p3, XTn((rC, rC)), R((rC, rC)))
                K.tt(R((rC, rC)), R((rC, rC)), p3, ALU.add)
                Xc, XTc, Xn, XTn = Xn, XTn, Xc, XTc
            p = mps(Cn, 128)
            K.mm(p, kTe((A_, rC)), Sv)
            K.tt(r_((rC, A_)), vtok((rC, A_)), p, ALU.subtract)
            p = mps(Cn, 128)
            K.mm(p, R((rC, rC)), r_((rC, A_)))
            K.ts(u_((rC, A_)), p, btok((rC, gi, slice(h, h + 1))), None, ALU.mult)
            p = mps(128, Cn)
            K.mm(p, Sv, qTe((A_, rC)), start=True, stop=False)
            K.mm(p, u_((rC, A_)), PT((rC, rC)), start=False, stop=True)
            K.cp(oTh((A_, cs)), p, "act")
            p = mps(128, 128)
            K.mm(p, kw((rC, A_)), u_((rC, A_)))
            K.stt(Sv, Sv, EG((A_, slice(Cn - 1, Cn))), p, ALU.mult, ALU.add)

        lnames = ("ebc", "enbc", "qtl", "ktl", "ktlt", "vtokA", "PTl")
        lsm = [{n: SV(par * 14 + i) for i, n in enumerate(lnames)} for par in range(2)]
        alogt = [T(f"alogt{i}", [128, 384], F32) for i in range(2)]

        def gla_chunk(li, pr, col0, Cn, Sp, par, vts, ots):
            cs = slice(col0, col0 + Cn)
            rC = slice(0, Cn)
            L = lsm[par]
            ebc, enbc, qtl, ktl, ktlt, vtokA, PTl = (L[n] for n in lnames)
            al = alogt[par]
            pz = dps()
            pzv = V(pz.t[0:Cn, 0:384], pz.b)
            K.mm(pzv, clr1((slice(0, 32), cs)), wab((A_, li, A_)))
            K.act(al((rC, A_)), pzv, AF.Exp, scale=-1.0)
            K.act(al((rC, A_)), al((rC, A_)), AF.Ln, bias=sub(one_c, rC), scale=1.0)
            pb = mps(128, Cn)
            K.mm(pb, al((rC, slice(pr * 128, (pr + 1) * 128))), sub(Utri, rC, rC))
            K.act(ebc((A_, rC)), pb, AF.Exp, scale=-1.0 / 16)
            K.act(enbc((A_, rC)), pb, AF.Exp, scale=1.0 / 16)
            K.tt(qtl((A_, rC)), qT((A_, cs)), ebc((A_, rC)), ALU.mult, eng="pool")
            K.tt(ktl((A_, rC)), kT((A_, cs)), enbc((A_, rC)), ALU.mult, eng="pool")
            p = mps(Cn, 128)
            K.tr(p, ktl((A_, rC)), ident)
            K.cp(ktlt((rC, A_)), p, "act")
            for hh in range(2):
                r64 = slice(hh * 64, hh * 64 + 64)
                p = mps(Cn, 128)
                K.tr(p, vts[hh]((A_, cs)), ident)
                K.cp(vtokA((rC, A_)), p, "act")
                pa = mps(Cn, Cn)
                K.mm(pa, ktl((r64, rC)), qtl((r64, rC)))
                K.tt(PTl((rC, rC)), pa, sub(Utri, rC, rC), ALU.mult)
                po = mps(128, Cn)
                K.mm(po, sub(Sp, r64), qtl((r64, rC)), start=True, stop=False)
                K.mm(po, vtokA((rC, A_)), PTl((rC, rC)), start=False, stop=True)
                K.cp(ots[hh]((A_, cs)), po, "act")
                pf = PS[2 + mctr[0] % 6]
                mctr[0] += 1
                psn = V(pf.t[r64, 0:128], pf.b)
                K.mm(psn, ktlt((rC, r64)), vtokA((rC, A_)))
                K.tt(sub(Sp, r64), sub(Sp, r64), psn, ALU.add)
                K.ts(sub(Sp, r64), sub(Sp, r64), ebc((r64, slice(Cn - 1, Cn))), None, ALU.mult)

        q16 = T("q16", [128, NTmax], BF16)
        eT16 = [T(f"eT16_{i}", [128, 512], BF16) for i in range(2)]
        ks16 = T("ks16", [128, 4, NS], BF16)
        vs32 = T("vs32", [128, 4, NS], F32)
        qblk = T("qblk", [128, NB_S, 4, 16], BF16)
        osamp = T("osamp", [128, 4, NS], F32)
        kpage = [SV(0, 4), SV(4, 4)]
        vpage = [SV(8, 4), SV(12, 4)]
        v16p = [T(f"v16p{i}", [128, 512], BF16) for i in range(2)]
        ktp16 = [T(f"ktp16_{i}", [128, 512], BF16) for i in range(2)]
        eTp16 = [T(f"eTp16_{i}", [128, 64], BF16) for i in range(2)]
        vsn16 = T("vsn16", [8, 4, 128], BF16)
        m8 = T("m8", [8, 64], BF16)
        zs = T("zs", [128, 64], F32)
        for j in range(8):
            K.cp(m8((A_, slice(j * 8, j * 8 + 8))), sub(Utri, slice(0, 8), slice(0, 8)))
        vtokp = [T(f"vtokp{i}", [128, 128], F32) for i in range(2)]

        def attn_prompt(li, h, g, ptiles):
            for (t0, n) in ptiles:
                q0 = GOFF[g] + t0
                nkt = (q0 + n) // 128
                acc = [PS[4], PS[5], PS[6], PS[7]]
                for kt in range(nkt):
                    kabs = kt * 128
                    qoff = max(0, kabs - q0)
                    nq = n - qoff
                    diag = kabs + 128 > q0
                    for m in range(2):
                        r64 = slice(m * 64, m * 64 + 64)
                        pst = PS[2 + (kt * 2 + m) % 2]
                        sv = V(pst.t[:, 0:nq], pst.b)
                        K.mm(sv, KTh((r64, slice(kabs, kabs + 128))), q16((r64, slice(t0 + qoff, t0 + n))))
                        e = eT16[(kt * 2 + m) % 2]
                        K.act(e((A_, slice(0, nq))), sv, AF.Exp)
                        if diag:
                            K.tt(e((A_, slice(0, 128))), e((A_, slice(0, 128))), Utri16, ALU.mult, eng="pool")
                        ov = V(acc[m].t[:, qoff:n], acc[m].b)
                        zv = V(acc[2 + m].t[:, qoff:n], acc[2 + m].b)
                        K.mm(ov, Vh((A_, kt, A_)), e((A_, slice(0, nq))), start=(kt == 0), stop=(kt == nkt - 1))
                        K.mm(zv, ones16, e((A_, slice(0, nq))), start=(kt == 0), stop=(kt == nkt - 1))
                cs = slice(t0, t0 + n)
                nn = slice(0, n)
                K.recip(rz0((A_, nn)), V(acc[2].t[:, 0:n], acc[2].b))
                K.recip(rz1((A_, nn)), V(acc[3].t[:, 0:n], acc[3].b))
                K.tt(at0((A_, nn)), V(acc[0].t[:, 0:n], acc[0].b), rz0((A_, nn)), ALU.mult)
                K.tt(at1((A_, nn)), V(acc[1].t[:, 0:n], acc[1].b), rz1((A_, nn)), ALU.mult)
                K.stt(oTh((A_, cs)), at1((A_, nn)), lam((A_, li, slice(1, 2))), at0((A_, nn)), ALU.mult, ALU.add)

        def attn_sample(li, NPG):
            for b in range(NB_S):
                oacc, zacc = PS[6], PS[7]
                ov = V(oacc.t[:, 0:64], oacc.b)
                zv = V(zacc.t[:, 0:64], zacc.b)
                for h in range(4):
                    p = V(PS[2].t[0:8, 0:128], PS[2].b)
                    K.tr(p, vs32((A_, h, slice(b * TS, (b + 1) * TS))), ident)
                    K.cp(vsn16((A_, h, A_)), p, "act")
                for n in range(NPAGES):
                    par = n % 2
                    col = b * NPAGES + n
                    K.gather(kpage[par](), V(cache_k.ap.rearrange("l n c -> (l n) c"), cache_k.b), IXl[li]((A_, slice(col, col + 1))))
                    K.gather(vpage[par](), V(cache_v.ap.rearrange("l n c -> (l n) c"), cache_v.b), IXl[li]((A_, slice(col, col + 1))))
                    K.cp(v16p[par](), vpage[par](), "pool")
                    pt_ = PS[2 + par]
                    for h in range(4):
                        K.tr(V(pt_.t[:, h * 128:(h + 1) * 128], pt_.b), kpage[par]((A_, slice(h * 128, (h + 1) * 128))), ident)
                    K.cp(ktp16[par](), V(pt_.t[:, :], pt_.b), "act" if par == 0 else "dve")
                    ps_ = PS[4 + par]
                    for h in range(4):
                        K.mm(V(ps_.t[:, h * 16:(h + 1) * 16], ps_.b), ktp16[par]((A_, slice(h * 128, (h + 1) * 128))), qblk((A_, b, h, A_)))
                    K.act(eTp16[par](), V(ps_.t[:, 0:64], ps_.b), AF.Exp)
                    for h in range(4):
                        K.mm(V(oacc.t[:, h * 16:(h + 1) * 16], oacc.b), v16p[par]((A_, slice(h * 128, (h + 1) * 128))),
                             eTp16[par]((A_, slice(h * 16, (h + 1) * 16))), start=(n == 0 and h == 0), stop=False)
                    K.mm(zv, ones16, eTp16[par](), start=(n == 0), stop=False)
                bs_ = slice(b * TS, (b + 1) * TS)
                ps_ = PS[4]
                for h in range(4):
                    K.mm(V(ps_.t[0:8, h * 16:(h + 1) * 16], ps_.b), ks16((A_, h, bs_)), qblk((A_, b, h, A_)))
                e8 = eTp16[0]
                K.act(e8((slice(0, 8), A_)), V(ps_.t[0:8, 0:64], ps_.b), AF.Exp)
                K.tt(e8((slice(0, 8), A_)), e8((slice(0, 8), A_)), m8(), ALU.mult)
                for h in range(4):
                    K.mm(V(oacc.t[:, h * 16:(h + 1) * 16], oacc.b), vsn16((A_, h, A_)), e8((slice(0, 8), slice(h * 16, (h + 1) * 16))),
                         start=False, stop=(h == 3))
                K.mm(zv, sub(ones16, slice(0, 8)), e8((slice(0, 8), A_)), start=False, stop=True)
                K.recip(zs(), zv)
                K.tt(zs(), zs(), ov, ALU.mult)
                for h in range(4):
                    K.stt(osamp((A_, h, bs_)), zs((A_, slice(h * 16 + 8, h * 16 + 16))), lam((A_, li, slice(1, 2))),
                          zs((A_, slice(h * 16, h * 16 + 8))), ALU.mult, ALU.add)

        lnm = T("lnm", [128, 512], F32)
        lnr = T("lnr", [128, 512], F32)
        lnt = [T(f"lnt{i}", [128, 512], F32) for i in range(2)]
        sgt = [T(f"sgt{i}", [128, 512], F32) for i in range(2)]

        def layernorm(li, gcol0, bcol0, ttiles, to_y, cols_abs):
            for (t0, n) in ttiles:
                cs = slice(t0, t0 + n)
                nn = slice(0, n)
                pa, pb = PS[2], PS[3]
                pav, pbv = V(pa.t[:, 0:n], pa.b), V(pb.t[:, 0:n], pb.b)
                for c in range(KC):
                    K.mm(pav, ones32, hpre((A_, c, cs), c), start=(c == 0), stop=(c == KC - 1))
                for c in range(KC):
                    t = lnt[c % 2]
                    K.act(t((A_, nn)), hpre((A_, c, cs), c), AF.Square)
                    K.mm(pbv, ones32, t((A_, nn)), start=(c == 0), stop=(c == KC - 1))
                K.ts(lnm((A_, nn)), pav, 1.0 / D, None, ALU.mult)
                K.tt(lnr((A_, nn)), lnm((A_, nn)), lnm((A_, nn)), ALU.mult)
                K.stt(lnr((A_, nn)), pbv, 1.0 / D, lnr((A_, nn)), ALU.mult, ALU.subtract)
                K.act(lnr((A_, nn)), lnr((A_, nn)), AF.Ln, bias=eps5, scale=1.0)
                K.act(lnr((A_, nn)), lnr((A_, nn)), AF.Exp, scale=-0.5)
                for c in range(KC):
                    hv = hpre((A_, c, cs), c)
                    K.tt(hv, hv, lnm((A_, nn)), ALU.subtract)
                    K.tt(hv, hv, lnr((A_, nn)), ALU.mult, eng="pool")
                    K.ts(hv, hv, pcs(li, gcol0 + c), pcs(li, bcol0 + c), ALU.mult, ALU.add)
                    K.cp(xT16((A_, c, cs), c), hv, "pool")
                    if to_y:
                        K.dma(V(yT.ap[c * 128:(c + 1) * 128, cols_abs(t0, n)], yT.b), hv)

        def run_layer(g, li):
            NPG = GSZ[g]
            NTP = GT[g]
            has_s = g == 0
            last = g == NG - 1
            NT = NPG + (NS if has_s else 0)
            ptiles = tok_tiles(NPG)
            stile = [(NPG, NS)] if has_s else []
            ttiles = ptiles + stile
            a0 = GOFF[g]

            def cols_abs(t0, n):
                return slice(a0 + t0, a0 + t0 + n) if t0 < NPG else slice(SEQ, SEQ + NS)

            def rcols(t0, n):
                return slice(t0, t0 + n)

            def wblk(name):
                return wload(w_in.ap[li, BLK[name]], w_in, KC)

            def ev_copy(dst, scale=None):
                def f(t0, n, pv):
                    if scale is None:
                        K.cp(dst((A_, slice(t0, t0 + n))), pv, "act")
                    else:
                        K.act(dst((A_, slice(t0, t0 + n))), pv, AF.Copy, scale=scale)
                return f

            chunks = [(t * 128, 128, t) for t in range(NTP)]
            schunks = [(NPG + b * TS, TS, NTP + b) for b in range(NB_S)] if has_s else []

            w = wblk("gb")
            dense(w, KC, xT16, ttiles, lambda t0, n, pv: K.act(gbT((slice(0, 6), slice(t0, t0 + n))), pv, AF.Sigmoid), M=6)
            w = wblk("ga")
            dense(w, KC, xT16, ttiles,
                  lambda t0, n, pv: K.act(ggT((slice(0, 6), slice(t0, t0 + n))), pv, AF.Exp, bias=pcs(li, 1, 6), scale=1.0), M=6)
            gg6 = ggT((slice(0, 6), slice(0, NT)))
            K.act(gg6, gg6, AF.Ln, bias=sub(one_c, slice(0, 6)), scale=1.0)
            K.ts(gg6, gg6, pcs(li, 0, 6), None, ALU.mult)
            for (c0, Cn, gi) in chunks + schunks:
                rC = slice(0, Cn)
                p = mps(Cn, 6)
                K.tr(p, gbT((slice(0, 6), slice(c0, c0 + Cn))), sub(ident, slice(0, 6), slice(0, 6)))
                K.cp(btok((rC, gi, A_)), p, "act")
                K.ts(nbtok((rC, gi, A_)), p, -1.0, None, ALU.mult)
                p = mps(Cn, 6)
                K.tr(p, ggT((slice(0, 6), slice(c0, c0 + Cn))), sub(ident, slice(0, 6), slice(0, 6)))
                K.cp(gtok((rC, gi, A_)), p, "act")
                p = mps(Cn, 6)
                K.mm(p, sub(Utri, rC, rC), gtok((rC, gi, A_)))
                K.cp(gctok((rC, gi, A_)), p, "dve")

            for h in range(6):
                for (nm, dst, ci) in (("aq", qT, h), ("ak", kT, 6 + h), ("av", vT, 12 + h)):
                    rw = raw[rctr[0] % 2]
                    rws = raws[rctr[0] % 2]
                    rctr[0] += 1
                    K.cp(rw((A_, slice(0, 3))), ctail((A_, li, ci, A_), li * 18 + ci), "pool")
                    if has_s:
                        K.dma(rws((A_, A_, slice(0, 3))), V(st_conv.ap[li, :, :, ci, :].rearrange("b p j -> p b j"), st_conv.b))
                    w = wblk(f"{nm}{h}")

                    def ev(t0, n, pv, rw=rw, rws=rws):
                        if t0 < NPG:
                            K.cp(rw((A_, slice(3 + t0, 3 + t0 + n))), pv, "act")
                        else:
                            K.cp(rws((A_, A_, slice(3, 3 + TS))), V(pv.ap.rearrange("p (b t) -> p b t", b=NB_S), pv.bs), "act")
                    dense(w, KC, xT16, ttiles, ev)
                    K.cp(ctail((A_, li, ci, A_), li * 18 + ci), rw((A_, slice(NPG, NPG + 3))), "pool")
                    if last:
                        K.dma(V(o_conv.ap[li, 0, :, ci, :], o_conv.b), rw((A_, slice(NPG, NPG + 3))))
                    if has_s:
                        K.dma(V(o_conv.ap[li, 1:1 + NB_S, :, ci, :].rearrange("b p j -> p b j"), o_conv.b), rws((A_, A_, slice(TS, TS + 3))))
                    for j in range(4):
                        wj = cw((A_, li, ci, slice(j, j + 1)))
                        if j == 0:
                            K.ts(dst((A_, slice(0, NPG))), rw((A_, slice(0, NPG))), wj, None, ALU.mult)
                        else:
                            K.stt(dst((A_, slice(0, NPG))), rw((A_, slice(j, j + NPG))), wj, dst((A_, slice(0, NPG))), ALU.mult, ALU.add)
                    if has_s:
                        dsv = V(dst.ap[:, NPG:NPG + NS].rearrange("p (b t) -> p b t", b=NB_S), dst.bs)
                        for j in range(4):
                            wj = cw((A_, li, ci, slice(j, j + 1)))
                            if j == 0:
                                K.ts(dsv, rws((A_, A_, slice(0, TS))), wj, None, ALU.mult)
                            else:
                                K.stt(dsv, rws((A_, A_, slice(j, j + TS))), wj, dsv, ALU.mult, ALU.add)
                    K.act(dst((A_, slice(0, NT))), dst((A_, slice(0, NT))), AF.Silu)
                w = wblk(f"az{h}")
                dense(w, KC, xT16, ttiles, lambda t0, n, pv: K.act(zg((A_, slice(t0, t0 + n))), pv, AF.Silu))
                l2norm(qT, ttiles, 128 ** -0.5)
                l2norm(kT, ttiles, 1.0)
                Sv = Sg((A_, li, h, A_), li * 6 + h)
                for ci_, (c0, Cn, gi) in enumerate(chunks):
                    gdn_chunk(li, h, c0, Cn, gi, Sv, ci_ % 2)
                if last:
                    K.dma(V(o_gdn.ap[li, 0, h], o_gdn.b), Sv)
                for b, (c0, Cn, gi) in enumerate(schunks):
                    ss = sst[sctr[0] % 2]
                    sctr[0] += 1
                    K.dma(ss(), V(st_gdn.ap[li, b, h], st_gdn.b))
                    gdn_chunk(li, h, c0, Cn, gi, ss(), b % 2)
                    K.dma(V(o_gdn.ap[li, 1 + b, h], o_gdn.b), ss())
                rmsnorm_gate(oTh, zg, pcs(li, 2), ttiles, h)
            if dbg is not None and dbg.get("stage") == "gdn":
                return "gdn"

            if has_s:
                K.memset(qblk(), 0.0)
            for h in range(4):
                dense(wblk(f"dq{h}"), KC, xT16, ttiles, ev_copy(qT, 0.125))
                dense(wblk(f"dk{h}"), KC, xT16, ttiles, ev_copy(kT))
                dense(wblk(f"dv{h}"), KC, xT16, ttiles, ev_copy(vT))
                for x_t in (qT, kT):
                    for (t0, n) in ttiles:
                        cs = slice(t0, t0 + n)
                        pp = dps()
                        pv = V(pp.t[:, 0:n], pp.b)
                        K.mm(pv, perm, x_t((A_, cs)))
                        K.tt(sq((A_, cs)), pv, rope((A_, 1, cs)), ALU.mult)
                        K.tt(x_t((A_, cs)), x_t((A_, cs)), rope((A_, 0, cs)), ALU.mult, eng="pool")
                        K.tt(x_t((A_, cs)), x_t((A_, cs)), sq((A_, cs)), ALU.add)
                K.dma(V(newk.ap[li, h, :, a0:a0 + NPG], (newk.b[li * 4 + h],)), kT((A_, slice(0, NPG))))
                if has_s:
                    K.dma(V(newk.ap[li, h, :, SEQ:SEQ + NS], (newk.b[li * 4 + h],)), kT((A_, slice(NPG, NT))))
                    K.cp(ks16((A_, h, A_)), kT((A_, slice(NPG, NT))), "pool")
                    K.cp(vs32((A_, h, A_)), vT((A_, slice(NPG, NT))), "pool")
                    for b in range(NB_S):
                        K.cp(qblk((slice(0, 64), b, h, slice(0, 8))), qT((slice(0, 64), slice(NPG + b * TS, NPG + (b + 1) * TS))), "pool")
                        K.cp(qblk((slice(64, 128), b, h, slice(8, 16))), qT((slice(64, 128), slice(NPG + b * TS, NPG + (b + 1) * TS))), "pool")
                K.cp(q16((A_, slice(0, NPG))), qT((A_, slice(0, NPG))), "pool")
                if a0 > 0:
                    K.dma(KTh((A_, slice(0, a0))), V(newk.ap[li, h, :, 0:a0], (newk.b[li * 4 + h],)), q="pool")
                    K.dma(Vh((A_, slice(0, a0 // 128), A_)),
                          V(newv.ap[li, 0:a0, h * 128:(h + 1) * 128].rearrange("(t p) c -> p t c", p=128), (newv.b[li * 4 + h],)), q="pool")
                K.cp(KTh((A_, slice(a0, a0 + NPG))), kT((A_, slice(0, NPG))), "pool")
                for t in range(NTP):
                    p = mps(128, 128)
                    K.tr(p, vT((A_, slice(t * 128, (t + 1) * 128))), ident)
                    vt = vtokp[t % 2]
                    K.cp(vt(), p, "act")
                    K.cp(Vh((A_, a0 // 128 + t, A_)), p, "dve")
                    K.dma(V(newv.ap[li, a0 + t * 128:a0 + (t + 1) * 128, h * 128:(h + 1) * 128], (newv.b[li * 4 + h],)), vt())
                if has_s:
                    p = mps(NS, 128)
                    K.tr(p, vT((A_, slice(NPG, NT))), ident)
                    vt = vtokp[0]
                    K.cp(vt((slice(0, NS), A_)), p, "act")
                    K.dma(V(newv.ap[li, SEQ:SEQ + NS, h * 128:(h + 1) * 128], (newv.b[li * 4 + h],)), vt((slice(0, NS), A_)))
                attn_prompt(li, h, g, ptiles)
                rmsnorm_gate(oTh, None, pcs(li, 3), ptiles, 6 + h, const_gate=1.0 - lam_init[li])
            if has_s:
                attn_sample(li, NPG)
                for h in range(4):
                    K.cp(oTh((A_, slice(NPG, NT))), osamp((A_, h, A_)), "pool")
                    rmsnorm_gate(oTh, None, pcs(li, 3), stile, 6 + h, const_gate=1.0 - lam_init[li])
            if dbg is not None and dbg.get("stage") == "diff":
                return "diff"

            K.memset(clr1((slice(0, 32), slice(0, NT))), 1.0)
            dense(wblk("lr"), KC, xT16, ttiles, lambda t0, n, pv: K.cp(clr1((slice(0, 16), slice(t0, t0 + n))), pv, "act"), M=16)
            for pr in range(3):
                dense(wblk(f"cq{pr}"), KC, xT16, ttiles, ev_copy(qT, 0.125))
                dense(wblk(f"ck{pr}"), KC, xT16, ttiles, ev_copy(kT))
                vts, gts, ots = (vT, vTb), (zg, zgb), (oTh, oThb)
                for hh in range(2):
                    hd = 2 * pr + hh
                    dense(wblk(f"cv{hd}"), KC, xT16, ttiles, ev_copy(vts[hh]))
                    dense(wblk(f"cr{hd}"), KC, xT16, ttiles,
                          lambda t0, n, pv, gt=gts[hh]: K.act(gt((A_, slice(t0, t0 + n))), pv, AF.Silu))
                Sp = Sl((A_, li, pr, A_), li * 3 + pr)
                for ci_, (c0, Cn, gi) in enumerate(chunks):
                    gla_chunk(li, pr, c0, Cn, Sp, ci_ % 2, vts, ots)
                if last:
                    K.dma(V(o_gla.ap[li, 0, pr], o_gla.b), Sp)
                for b, (c0, Cn, gi) in enumerate(schunks):
                    ss = sst[sctr[0] % 2]
                    sctr[0] += 1
                    K.dma(ss(), V(st_gla.ap[li, b, pr], st_gla.b))
                    gla_chunk(li, pr, c0, Cn, ss(), b % 2, vts, ots)
                    K.dma(V(o_gla.ap[li, 1 + b, pr], o_gla.b), ss())
                for hh in range(2):
                    rmsnorm_gate(ots[hh], gts[hh], pcs(li, 4), ttiles, 10 + 2 * pr + hh)
            if dbg is not None and dbg.get("stage") == "gla":
                return "gla"

            for c in range(KC):
                if li == 0:
                    K.dma(hpre((A_, c, slice(0, NPG)), c), V(xpT.ap[c * 128:(c + 1) * 128, a0:a0 + NPG], xpT.b))
                    if has_s:
                        K.dma(hpre((A_, c, slice(NPG, NT)), c), V(xsT.ap[c * 128:(c + 1) * 128, :], xsT.b))
                w = wload(w_out.ap[li, c], w_out, KC)

                def ev_o(t0, n, pv, c=c):
                    cs = slice(t0, t0 + n)
                    src = hpre((A_, c, cs), c) if li == 0 else xT16((A_, c, cs), c)
                    K.stt(hpre((A_, c, cs), c), src, ALPHA, pv, ALU.mult, ALU.add)
                dense(w, KC, mixT, ttiles, ev_o)
            layernorm(li, 8, 8 + KC, ttiles, False, cols_abs)
            for qf in range(NQ):
                for j in range(FQ):
                    jj = qf * FQ + j
                    wg = wload(w_gu.ap[li, 2 * jj], w_gu, KC)
                    wu = wload(w_gu.ap[li, 2 * jj + 1], w_gu, KC)
                    for ti, (t0, n) in enumerate(ttiles):
                        sg = sgt[ti % 2]
                        dense(wg, KC, xT16, [(t0, n)], lambda t0, n, pv, sg=sg: K.act(sg((A_, slice(0, n))), pv, AF.Silu))
                        dense(wu, KC, xT16, [(t0, n)],
                              lambda t0, n, pv, sg=sg, j=j: K.tt(mixT((A_, j, slice(t0, t0 + n)), j), sg((A_, slice(0, n))), pv, ALU.mult))
                for c in range(KC):
                    wd = wload(w_dn.ap[li, qf, c], w_dn, FQ)

                    def ev_d(t0, n, pv, c=c, qf=qf):
                        cs = slice(t0, t0 + n)
                        hv = hpre((A_, c, cs), c)
                        if qf == 0:
                            K.stt(hv, hv, ALPHA, pv, ALU.mult, ALU.add)
                        else:
                            K.tt(hv, hv, pv, ALU.add)
                    dense(wd, FQ, mixT, ttiles, ev_d)
            layernorm(li, 8 + 2 * KC, 8 + 3 * KC, ttiles, li == DEPTH - 1, cols_abs)
            return None

        def load_x(g):
            NPG = GSZ[g]
            a0 = GOFF[g]
            K.dma(rope((A_, A_, slice(0, NPG))), V(ropec.ap[:, :, a0:a0 + NPG].rearrange("a p n -> p a n"), ropec.b))
            if g == 0:
                K.dma(rope((A_, A_, slice(NPG, NPG + NS))), V(ropec.ap[:, :, SEQ:SEQ + NS].rearrange("a p n -> p a n"), ropec.b))
            for c in range(KC):
                K.dma(xT16((A_, c, slice(0, NPG)), c), V(xpT.ap[c * 128:(c + 1) * 128, a0:a0 + NPG], xpT.b), q="pool")
                if g == 0:
                    K.dma(xT16((A_, c, slice(NPG, NPG + NS)), c), V(xsT.ap[c * 128:(c + 1) * 128, :], xsT.b), q="pool")

        stop = None
        for g in range(NG):
            load_x(g)
            for li in range(DEPTH):
                stop = run_layer(g, li)
                if stop:
                    break
            if stop:
                break
        if stop:
            NT0 = GSZ[0] + NS
            hs = {"gdn": range(0, 6), "diff": range(6, 10), "gla": range(10, 16)}[stop]
            for i, h in enumerate(hs):
                K.cp(hpre((A_, i, slice(0, NT0)), i), mixT((A_, h, slice(0, NT0)), h))
                K.dma(V(dbg_out.ap[i, :, 0:NT0], dbg_out.b), hpre((A_, i, slice(0, NT0)), i))
        S.emit(st)
    return nc


def _blk(Wcols, kc):
    n = Wcols.shape[1]
    out = np.zeros((128, kc, 128), np.float32)
    out[:, :, :n] = Wcols.reshape(kc, 128, n).transpose(1, 0, 2)
    return out


def prepare_shared(inp, SEQ, PAST):
    f32 = np.float32
    sh = {}
    NPOOL = inp["cache_k"].shape[1]
    sh["cache_k"] = np.ascontiguousarray(inp["cache_k"]).reshape(DEPTH, NPOOL * 128, 512)
    sh["cache_v"] = np.ascontiguousarray(inp["cache_v"]).reshape(DEPTH, NPOOL * 128, 512)
    blks = in_blocks()
    w_in = np.zeros((DEPTH, N_IN_BLK, 128, KC, 128), f32)
    w_out = np.zeros((DEPTH, KC, 128, KC, 128), f32)
    w_gu = np.zeros((DEPTH, 2 * FC, 128, KC, 128), f32)
    w_dn = np.zeros((DEPTH, NQ, KC, 128, FQ, 128), f32)
    for li in range(DEPTH):
        W = inp["w_in"][li]
        for i, (_, c0, n) in enumerate(blks):
            w_in[li, i] = _blk(W[:, c0:c0 + n], KC)
        W = inp["w_out"][li]
        for c in range(KC):
            w_out[li, c] = _blk(W[:, c * 128:(c + 1) * 128], KC)
        W = inp["w_gate_up"][li]
        for j in range(FC):
            w_gu[li, 2 * j] = _blk(W[:, j * 128:(j + 1) * 128], KC)
            w_gu[li, 2 * j + 1] = _blk(W[:, DFF + j * 128:DFF + (j + 1) * 128], KC)
        W = inp["w_down"][li]
        for hf in range(NQ):
            for c in range(KC):
                w_dn[li, hf, c] = _blk(W[hf * FQ * 128:(hf + 1) * FQ * 128, c * 128:(c + 1) * 128], FQ)
    sh["w_in"], sh["w_out"], sh["w_gu"], sh["w_dn"] = w_in, w_out, w_gu, w_dn
    sh["convw"] = np.ascontiguousarray(inp["gdn_conv_w"].reshape(DEPTH, 4, 18, 128).transpose(0, 3, 2, 1))
    NPC = 8 + 4 * KC
    pcol = np.zeros((DEPTH, 128, NPC), f32)
    pcol[:, 0:6, 0] = inp["gdn_a_log"]
    pcol[:, 0:6, 1] = inp["gdn_dt_bias"]
    pcol[:, :, 2] = inp["gdn_norm_g"]
    pcol[:, :, 3] = inp["diff_norm_g"]
    pcol[:, :, 4] = inp["gla_norm_g"]
    for i, nm in enumerate(("ln1_g", "ln1_b", "ln2_g", "ln2_b")):
        pcol[:, :, 8 + i * KC:8 + (i + 1) * KC] = inp[nm].reshape(DEPTH, KC, 128).transpose(0, 2, 1)
    sh["pcol"] = pcol
    sh["lamv"] = np.ascontiguousarray(inp["diff_lambda"].reshape(1, DEPTH * 256))
    wa2b = np.zeros((DEPTH, 32, 384), f32)
    wa2b[:, 0:16] = inp["gla_wa2"]
    wa2b[:, 16] = inp["gla_ba"]
    sh["wa2b"] = wa2b
    c = np.zeros((128, 6, 128), f32)
    p = np.arange(128)
    c[:, 0] = np.eye(128)
    c[:, 1] = (p[:, None] <= p[None, :])
    c[:, 2] = (p[:, None] < p[None, :])
    c[:, 3] = 1.0
    c[:, 5] = 1.0
    for m0 in (0, 64):
        for d in range(8):
            c[m0 + d + 8, 4, m0 + d] = 1.0
            c[m0 + d, 4, m0 + d + 8] = 1.0
    sh["consts"] = c
    NTOK = SEQ + NS
    pos = np.concatenate([np.arange(SEQ), np.tile(PAST + np.arange(TS), NB_S)]).astype(f32)
    inv = (1.0 / (f32(500000.0) ** (np.arange(8, dtype=f32) * f32(2.0) / f32(16)))).astype(f32)
    ang = (pos[None, :] * inv[:, None]).astype(f32)
    rc = np.zeros((2, 128, NTOK), f32)
    rc[0] = 1.0
    for m0 in (0, 64):
        rc[0, m0:m0 + 8] = np.cos(ang)
        rc[0, m0 + 8:m0 + 16] = np.cos(ang)
        rc[1, m0:m0 + 8] = -np.sin(ang)
        rc[1, m0 + 8:m0 + 16] = np.sin(ang)
    sh["ropec"] = rc
    return sh


def prepare_inputs(inp, SEQ, PAST):
    sh = prepare_shared(inp, SEQ, PAST)
    maps = []
    for c in range(8):
        b = c // 2
        sl = slice(NB_S * c, NB_S * (c + 1))
        m = dict(sh)
        m["xpT"] = np.ascontiguousarray(inp["x_prompt"][b].T)
        m["xsT"] = np.ascontiguousarray(inp["x_sample"][sl].reshape(NS, D).T)
        m["ptab"] = np.ascontiguousarray(inp["page_table"][sl].reshape(1, -1)).astype(np.int32)
        m["st_gdn"] = np.ascontiguousarray(inp["state_gdn"][:, sl])
        m["st_conv"] = np.ascontiguousarray(
            inp["state_gdn_conv"][:, sl].reshape(DEPTH, NB_S, 3, 18, 128).transpose(0, 1, 4, 3, 2))
        m["st_gla"] = np.ascontiguousarray(inp["state_gla"][:, sl].reshape(DEPTH, NB_S, 3, 128, 128))
        maps.append(m)
    return maps


def assemble(results, SEQ):
    f32 = np.float32
    B = 4
    y_p = np.zeros((B, SEQ, D), f32)
    y_s = np.zeros((8 * NB_S, TS, D), f32)
    nk_p = np.zeros((DEPTH, B, SEQ, 4, 128), f32)
    nv_p = np.zeros((DEPTH, B, SEQ, 4, 128), f32)
    gs_p = np.zeros((DEPTH, B, 6, 128, 128), f32)
    gc_p = np.zeros((DEPTH, B, 3, 2304), f32)
    ls_p = np.zeros((DEPTH, B, 6, 64, 128), f32)
    nk_s = np.zeros((DEPTH, 8 * NB_S, TS, 4, 128), f32)
    nv_s = np.zeros((DEPTH, 8 * NB_S, TS, 4, 128), f32)
    gs_s = np.zeros((DEPTH, 8 * NB_S, 6, 128, 128), f32)
    gc_s = np.zeros((DEPTH, 8 * NB_S, 3, 2304), f32)
    ls_s = np.zeros((DEPTH, 8 * NB_S, 6, 64, 128), f32)
    for c in range(8):
        r = results[c]
        sl = slice(NB_S * c, NB_S * (c + 1))
        yT = np.asarray(r["yT"])
        nk = np.asarray(r["newk"])
        nv = np.asarray(r["newv"])
        og = np.asarray(r["o_gdn"])
        oc = np.asarray(r["o_conv"])
        ol = np.asarray(r["o_gla"])
        conv = oc.transpose(0, 1, 4, 3, 2).reshape(DEPTH, 1 + NB_S, 3, 2304)
        if c % 2 == 0:
            b = c // 2
            y_p[b] = yT[:, :SEQ].T
            nk_p[:, b] = nk[:, :, :, :SEQ].transpose(0, 3, 1, 2)
            nv_p[:, b] = nv[:, :SEQ].reshape(DEPTH, SEQ, 4, 128)
            gs_p[:, b] = og[:, 0]
            gc_p[:, b] = conv[:, 0]
            ls_p[:, b] = ol[:, 0].reshape(DEPTH, 6, 64, 128)
        y_s[sl] = yT[:, SEQ:].T.reshape(NB_S, TS, D)
        nk_s[:, sl] = nk[:, :, :, SEQ:].transpose(0, 3, 1, 2).reshape(DEPTH, NB_S, TS, 4, 128)
        nv_s[:, sl] = nv[:, SEQ:].reshape(DEPTH, NB_S, TS, 4, 128)
        gs_s[:, sl] = og[:, 1:]
        gc_s[:, sl] = conv[:, 1:]
        ls_s[:, sl] = ol[:, 1:].reshape(DEPTH, NB_S, 6, 64, 128)
    return (y_p, y_s, nk_p, nv_p, gs_p, gc_p, ls_p, nk_s, nv_s, gs_s, gc_s, ls_s)


def run_step(inputs, SEQ, PAST):
    inp = {k: np.asarray(v) for k, v in inputs.items()}
    NPOOL = inp["cache_k"].shape[1]
    nc = build_program(SEQ, PAST, NPOOL)
    in_maps = prepare_inputs(inp, SEQ, PAST)
    res = run_bass_kernel_spmd(nc, in_maps, core_ids=list(range(8)))
    return assemble(res.results, SEQ)


def kernel(**inputs):
    SEQ = int(np.asarray(inputs["x_prompt"]).shape[1])
    PAST = int(np.asarray(inputs["page_table"]).shape[1]) * int(np.asarray(inputs["cache_k"]).shape[2])
    return run_step(inputs, SEQ, PAST)
```

```python
import math
from contextlib import ExitStack

import numpy as np
import concourse.bass as bass
import concourse.mybir as mybir
from concourse.bass_utils import run_bass_kernel_spmd

F32 = mybir.dt.float32
BF16 = mybir.dt.bfloat16
I32 = mybir.dt.int32
ALU = mybir.AluOpType
AF = mybir.ActivationFunctionType
AX = mybir.AxisListType

D = 2048
KC = 16
DEPTH = 2
DFF = 5632
FC = 44
FH = 22
NQ = 4
FQ = 11
NB_S = 4
TS = 8
NS = NB_S * TS
ALPHA = (2 * DEPTH) ** 0.25
N_IN_BLK = 57
EPOCH = 30000
NSLOT = {"sp": 24, "pool": 12, "act": 8}


class Buf:
    __slots__ = ("name", "lw", "rd")

    def __init__(self, name=""):
        self.name = name
        self.lw = None
        self.rd = {}


class Op:
    __slots__ = ("eng", "fn", "deps", "is_dma", "slot", "use", "needed", "sig", "uid")


class V:
    __slots__ = ("ap", "bs")

    def __init__(self, ap, bs):
        self.ap = ap
        self.bs = tuple(bs)


class Sched:
    def __init__(self, nc):
        self.nc = nc
        self.ops = {e: [] for e in ("pe", "act", "dve", "pool", "sp")}
        self.slot_rr = {q: 0 for q in NSLOT}
        self.slot_last = {q: [None] * NSLOT[q] for q in NSLOT}
        self.slot_use = {q: [0] * NSLOT[q] for q in NSLOT}
        self.uid = 0

    def rec(self, eng, fn, reads, writes, is_dma=False):
        op = Op()
        op.eng = eng
        op.fn = fn
        op.is_dma = is_dma
        op.needed = is_dma
        op.sig = None
        op.uid = self.uid
        self.uid += 1
        deps = {}
        raw = set()
        for b in reads:
            if b.lw is not None:
                deps[b.lw.uid] = b.lw
                raw.add(b.lw.uid)
        for b in writes:
            if b.lw is not None:
                deps[b.lw.uid] = b.lw
            for r in b.rd.values():
                deps[r.uid] = r
        if is_dma:
            s = self.slot_rr[eng]
            self.slot_rr[eng] = (s + 1) % NSLOT[eng]
            prev = self.slot_last[eng][s]
            if prev is not None:
                deps[prev.uid] = prev
            self.slot_last[eng][s] = op
            self.slot_use[eng][s] += 1
            op.slot = s
            op.use = self.slot_use[eng][s]
        final = []
        for d in deps.values():
            if (not d.is_dma) and (not is_dma) and d.eng == eng:
                if eng == "pe":
                    continue
            d.needed = True
            final.append(d)
        op.deps = final
        for b in reads:
            b.rd[("d", op.uid) if is_dma else eng] = op
        for b in writes:
            b.lw = op
            b.rd = {}
        self.ops[eng].append(op)
        return op

    def emit(self, stack):
        nc = self.nc
        nsig = {}
        for e in ("pe", "act", "dve", "pool"):
            c = 0
            for op in self.ops[e]:
                if (not op.is_dma) and op.needed:
                    op.sig = c
                    c += 1
            nsig[e] = c
        sems = {}
        for e in ("pe", "act", "dve", "pool"):
            for k in range(nsig[e] // EPOCH + 1):
                sems[(e, k)] = stack.enter_context(nc.semaphore(f"s_{e}_{k}"))
        final_waits = {}
        for q in NSLOT:
            for s in range(NSLOT[q]):
                if self.slot_use[q][s] > 0:
                    sems[("d", q, s)] = stack.enter_context(nc.semaphore(f"d_{q}_{s}"))
                    final_waits[("d", q, s)] = 16 * self.slot_use[q][s]

        def tok(d):
            if d.is_dma:
                return ("d", d.eng, d.slot), 16 * d.use
            return (d.eng, d.sig // EPOCH), d.sig % EPOCH + 1

        def run(engname, eng):
            waited = {}
            for op in self.ops[engname]:
                need = {}
                for d in op.deps:
                    k, v = tok(d)
                    if need.get(k, 0) < v:
                        need[k] = v
                items = [(k, v) for k, v in need.items() if waited.get(k, 0) < v]
                attach = None
                if items and not op.is_dma:
                    attach = items.pop()
                for k, v in items:
                    eng.wait_ge(sems[k], v)
                    waited[k] = v
                ins = op.fn(eng)
                if attach is not None:
                    ins._wait_ge(sems[attach[0]], attach[1])
                    waited[attach[0]] = attach[1]
                if op.is_dma:
                    ins.then_inc(sems[("d", op.eng, op.slot)], 16)
                elif op.sig is not None:
                    ins.then_inc(sems[(op.eng, op.sig // EPOCH)], 1)
            if engname == "sp":
                for k, v in final_waits.items():
                    if waited.get(k, 0) < v:
                        eng.wait_ge(sems[k], v)

        block = stack.enter_context(nc.Block())

        @block.tensor
        def _(e):
            run("pe", e)

        @block.scalar
        def _(e):
            run("act", e)

        @block.vector
        def _(e):
            run("dve", e)

        @block.gpsimd
        def _(e):
            run("pool", e)

        @block.sync
        def _(e):
            run("sp", e)


class Tile:
    def __init__(self, st, nc, name, shape, dtype, nb=1, psum=False):
        if psum:
            self.t = st.enter_context(nc.psum_tensor(name, shape, dtype))
        else:
            self.t = st.enter_context(nc.sbuf_tensor(name, shape, dtype))
        self.b = [Buf(f"{name}{i}") for i in range(nb)]

    def __call__(self, key=None, bi=0):
        ap = self.t[:] if key is None else self.t[key]
        if bi is None:
            return V(ap, self.b)
        return V(ap, (self.b[bi],))


class DT:
    def __init__(self, nc, name, shape, dtype, kind, nb=1):
        self.t = nc.dram_tensor(name, list(shape), dtype, kind=kind)
        self.ap = self.t.ap()
        self.b = [Buf(f"{name}{i}") for i in range(nb)]

    def __call__(self, ap=None, bi=0):
        return V(self.ap if ap is None else ap, (self.b[bi],))


def _bs(*vs):
    out = []
    for v in vs:
        if isinstance(v, V):
            out.extend(v.bs)
    return out


def _a(x):
    return x.ap if isinstance(x, V) else x


class KB:
    def __init__(self, S):
        self.S = S

    def mm(self, out, lhsT, rhs, start=True, stop=True):
        o, l, r = out.ap, lhsT.ap, rhs.ap
        self.S.rec("pe", lambda e: e.matmul(o, lhsT=l, rhs=r, start=start, stop=stop), _bs(lhsT, rhs), _bs(out))

    def tr(self, out, in_, ident):
        o, i, d = out.ap, in_.ap, ident.ap
        self.S.rec("pe", lambda e: e.transpose(o, i, d), _bs(in_, ident), _bs(out))

    def act(self, out, in_, func, bias=None, scale=None):
        o, i = out.ap, in_.ap
        kw = {}
        if bias is not None:
            kw["bias"] = _a(bias)
        if scale is not None:
            kw["scale"] = _a(scale)
        self.S.rec("act", lambda e: e.activation(out=o, in_=i, func=func, **kw), _bs(in_, bias, scale), _bs(out))

    def ts(self, out, in0, s1, s2=None, op0=ALU.mult, op1=None, eng="dve"):
        o, i = out.ap, in0.ap
        a1, a2 = _a(s1), _a(s2)
        if op1 is None:
            f = lambda e: e.tensor_scalar(out=o, in0=i, scalar1=a1, scalar2=None, op0=op0)
        else:
            f = lambda e: e.tensor_scalar(out=o, in0=i, scalar1=a1, scalar2=a2, op0=op0, op1=op1)
        self.S.rec(eng, f, _bs(in0, s1, s2), _bs(out))

    def tt(self, out, in0, in1, op, eng="dve"):
        o, i0, i1 = out.ap, in0.ap, in1.ap
        self.S.rec(eng, lambda e: e.tensor_tensor(out=o, in0=i0, in1=i1, op=op), _bs(in0, in1), _bs(out))

    def stt(self, out, in0, scalar, in1, op0, op1):
        o, i0, i1, s = out.ap, in0.ap, in1.ap, _a(scalar)
        self.S.rec("dve", lambda e: e.scalar_tensor_tensor(out=o, in0=i0, scalar=s, in1=i1, op0=op0, op1=op1),
                   _bs(in0, in1, scalar), _bs(out))

    def cp(self, out, in_, eng="dve"):
        o, i = out.ap, in_.ap
        if eng == "act":
            self.S.rec("act", lambda e: e.copy(out=o, in_=i), _bs(in_), _bs(out))
        else:
            self.S.rec(eng, lambda e: e.tensor_copy(out=o, in_=i), _bs(in_), _bs(out))

    def memset(self, out, val, eng="pool"):
        o = out.ap
        self.S.rec(eng, lambda e: e.memset(o, val), [], _bs(out))

    def recip(self, out, in_):
        o, i = out.ap, in_.ap
        self.S.rec("dve", lambda e: e.reciprocal(out=o, in_=i), _bs(in_), _bs(out))

    def rsum(self, out, in_):
        o, i = out.ap, in_.ap
        self.S.rec("dve", lambda e: e.reduce_sum(out=o, in_=i, axis=AX.X), _bs(in_), _bs(out))

    def dma(self, out, in_, q="sp"):
        o, i = out.ap, in_.ap
        self.S.rec(q, lambda e: e.dma_start(out=o, in_=i), _bs(in_), _bs(out), is_dma=True)

    def gather(self, out, table, idx):
        o, t, ix = out.ap, table.ap, idx.ap
        self.S.rec("pool", lambda e: e.indirect_dma_start(out=o, out_offset=None, in_=t,
                                                          in_offset=bass.IndirectOffsetOnAxis(ap=ix, axis=0)),
                   _bs(table, idx), _bs(out), is_dma=True)

    def iota_part(self, out):
        o = out.ap
        self.S.rec("pool", lambda e: e.iota(o, pattern=[[0, 1]], base=0, channel_multiplier=1), [], _bs(out))


def in_blocks():
    o_qkv = 0
    o_b = 2304
    o_a = 2310
    o_z = 2316
    o_dq = 3084
    o_dk = 3596
    o_dv = 4108
    o_cq = 4620
    o_ck = 5004
    o_cv = 5388
    o_lr = 6156
    o_cr = 6172
    blks = [("gb", o_b, 6), ("ga", o_a, 6)]
    for h in range(6):
        blks += [(f"aq{h}", o_qkv + h * 128, 128), (f"ak{h}", o_qkv + 768 + h * 128, 128),
                 (f"av{h}", o_qkv + 1536 + h * 128, 128), (f"az{h}", o_z + h * 128, 128)]
    for h in range(4):
        blks += [(f"dq{h}", o_dq + h * 128, 128), (f"dk{h}", o_dk + h * 128, 128), (f"dv{h}", o_dv + h * 128, 128)]
    blks += [("lr", o_lr, 16)]
    for p in range(3):
        blks += [(f"cq{p}", o_cq + p * 128, 128), (f"ck{p}", o_ck + p * 128, 128)]
        for h in (2 * p, 2 * p + 1):
            blks += [(f"cv{h}", o_cv + h * 128, 128), (f"cr{h}", o_cr + h * 128, 128)]
    assert len(blks) == N_IN_BLK
    return blks


BLK = {n: i for i, (n, _, _) in enumerate(in_blocks())}


def tok_tiles(n):
    nt = n // 128
    k = -(-nt // 4)
    out = []
    t = 0
    for i in range(k):
        m = (nt - t + (k - i) - 1) // (k - i)
        out.append((t * 128, m * 128))
        t += m
    return out


def group_tiles(SEQ):
    nt = SEQ // 128
    if nt >= 16:
        a = -(-nt // 3)
        return [a, (nt - a + 1) // 2, (nt - a) // 2]
    return [nt // 2, nt - nt // 2]


class HV:
    def __init__(self, tile, c, rows=None):
        self.ap = tile.t[:, c, :] if rows is None else tile.t[rows, c, :]
        self.bs = (tile.b[c],)

    def __call__(self, key=None):
        return V(self.ap if key is None else self.ap[key], self.bs)


def build_program(SEQ, PAST, NPOOL, dbg=None):
    GT = group_tiles(SEQ)
    NG = len(GT)
    GSZ = [t * 128 for t in GT]
    GOFF = [sum(GSZ[:i]) for i in range(NG)]
    NPGM = max(GSZ)
    NTPM = max(GT)
    NPAGES = PAST // 128
    NTOK = SEQ + NS
    NTmax = NPGM + NS
    nc = bass.Bass("TRN2", target_bir_lowering=False)
    EI, EO = "ExternalInput", "ExternalOutput"
    xpT = DT(nc, "xpT", [D, SEQ], F32, EI)
    xsT = DT(nc, "xsT", [D, NS], F32, EI)
    cache_k = DT(nc, "cache_k", [DEPTH, NPOOL * 128, 512], F32, EI)
    cache_v = DT(nc, "cache_v", [DEPTH, NPOOL * 128, 512], F32, EI)
    ptab = DT(nc, "ptab", [1, NB_S * NPAGES], I32, EI)
    st_gdn = DT(nc, "st_gdn", [DEPTH, NB_S, 6, 128, 128], F32, EI)
    st_conv = DT(nc, "st_conv", [DEPTH, NB_S, 128, 18, 3], F32, EI)
    st_gla = DT(nc, "st_gla", [DEPTH, NB_S, 3, 128, 128], F32, EI)
    w_in = DT(nc, "w_in", [DEPTH, N_IN_BLK, 128, KC, 128], F32, EI)
    w_out = DT(nc, "w_out", [DEPTH, KC, 128, KC, 128], F32, EI)
    w_gu = DT(nc, "w_gu", [DEPTH, 2 * FC, 128, KC, 128], F32, EI)
    w_dn = DT(nc, "w_dn", [DEPTH, NQ, KC, 128, FQ, 128], F32, EI)
    convw = DT(nc, "convw", [DEPTH, 128, 18, 4], F32, EI)
    NPC = 8 + 4 * KC
    pcol = DT(nc, "pcol", [DEPTH, 128, NPC], F32, EI)
    lamv = DT(nc, "lamv", [1, DEPTH * 256], F32, EI)
    wa2b = DT(nc, "wa2b", [DEPTH, 32, 384], F32, EI)
    consts = DT(nc, "consts", [128, 6, 128], F32, EI)
    ropec = DT(nc, "ropec", [2, 128, NTOK], F32, EI)
    yT = DT(nc, "yT", [D, NTOK], F32, EO)
    newk = DT(nc, "newk", [DEPTH, 4, 128, NTOK], F32, EO, nb=DEPTH * 4)
    newv = DT(nc, "newv", [DEPTH, NTOK, 512], F32, EO, nb=DEPTH * 4)
    o_gdn = DT(nc, "o_gdn", [DEPTH, 1 + NB_S, 6, 128, 128], F32, EO)
    o_conv = DT(nc, "o_conv", [DEPTH, 1 + NB_S, 128, 18, 3], F32, EO)
    o_gla = DT(nc, "o_gla", [DEPTH, 1 + NB_S, 3, 128, 128], F32, EO)
    dbg_out = None
    if dbg is not None:
        dbg_out = DT(nc, "dbg", list(dbg["shape"]), F32, EO)

    with ExitStack() as st:
        S = Sched(nc)
        K = KB(S)

        def T(name, shape, dt=F32, nb=1):
            return Tile(st, nc, name, shape, dt, nb=nb)

        A_ = slice(None)
        xT16 = T("xT16", [128, KC, NTmax], BF16, nb=KC)
        mixT = T("mixT", [128, KC, NTmax], BF16, nb=KC)
        hpre = T("hpre", [128, KC, NTmax], F32, nb=KC)
        NWS = 4
        wring = [T(f"wr{i}", [128, KC, 128], BF16) for i in range(NWS)]
        wctr = [0]
        C = T("constsb", [128, 6, 128], F32)
        ident, Utri, mSU, ones32, perm = (C((A_, i, A_)) for i in range(5))
        C16 = T("c16", [128, 2, 128], BF16)
        ones16 = C16((A_, 0, A_))
        Utri16 = C16((A_, 1, A_))
        rope = T("rope", [128, 2, NTmax], F32)
        pc = T("pc", [128, DEPTH, NPC], F32)
        cw = T("cw", [128, DEPTH, 18, 4], F32)
        lam = T("lam", [128, DEPTH, 4], F32)
        lv = T("lv", [128, DEPTH * 256], F32)
        cb = T("cb", [128, 4], F32)
        wab = T("wab", [32, DEPTH, 384], F32)
        Sg = T("Sg", [128, DEPTH, 6, 128], F32, nb=DEPTH * 6)
        Sl = T("Sl", [128, DEPTH, 3, 128], F32, nb=DEPTH * 3)
        ctail = T("ctail", [128, DEPTH, 18, 3], F32, nb=DEPTH * 18)
        KTh = T("KTh", [128, SEQ], BF16)
        Vh = T("Vh", [128, SEQ // 128, 128], BF16)
        IXi = T("IXi", [128, NB_S * NPAGES], I32)
        IXl = [IXi] + [T(f"IXi{l}", [128, NB_S * NPAGES], I32) for l in range(1, DEPTH)]
        IXf = T("IXf", [128, NB_S * NPAGES], F32)
        iot = T("iot", [128, 2], F32)
        ioti = T("ioti", [128, 1], I32)
        PS = [Tile(st, nc, f"ps{i}", [128, 512], F32, nb=1, psum=True) for i in range(8)]
        dctr = [0]

        def dps():
            p = PS[dctr[0] % 2]
            dctr[0] += 1
            return p

        mctr = [0]

        def mps(rows=128, cols=128):
            i = mctr[0] % 6
            mctr[0] += 1
            p = PS[2 + i]
            return V(p.t[0:rows, 0:cols], p.b)

        K.dma(C(), consts())
        K.dma(pc(), V(pcol.ap.rearrange("l p n -> p l n"), pcol.b))
        K.dma(cw(), V(convw.ap.rearrange("l p c j -> p l c j"), convw.b))
        K.dma(lv(), V(lamv.ap.partition_broadcast(128), lamv.b))
        K.dma(wab(), V(wa2b.ap.rearrange("l p n -> p l n"), wa2b.b))
        K.dma(IXi(), V(ptab.ap.partition_broadcast(128), ptab.b))
        K.cp(C16((A_, 0, A_)), ones32)
        K.cp(C16((A_, 1, A_)), Utri)
        for i, v in enumerate((1e-6, 1e-5, 1.0, 0.0)):
            K.memset(cb((A_, slice(i, i + 1))), v)
        eps6, eps5, one_c, zero_c = (cb((A_, slice(i, i + 1))) for i in range(4))
        K.memset(Sg(None, None), 0.0)
        K.memset(Sl(None, None), 0.0)
        K.memset(ctail(None, None), 0.0)
        K.iota_part(ioti())
        K.cp(iot((A_, slice(0, 1))), ioti())
        K.cp(IXf(), IXi())
        K.ts(IXf(), IXf(), 128.0, iot((A_, slice(0, 1))), ALU.mult, ALU.add)
        K.cp(IXi(), IXf())
        for l in range(1, DEPTH):
            K.ts(IXf(), IXf(), float(NPOOL * 128), None, ALU.add)
            K.cp(IXl[l](), IXf())
        lam_init = [0.8 - 0.6 * math.exp(-0.3 * li) for li in range(DEPTH)]
        ltmp = T("ltmp", [128, 64], F32)
        for li in range(DEPTH):
            for j in range(2):
                o = li * 256 + j * 128
                K.tt(ltmp(), lv((A_, slice(o, o + 64))), lv((A_, slice(o + 64, o + 128))), ALU.mult)
                K.rsum(lam((A_, li, slice(2 + j, 3 + j))), ltmp())
            K.act(lam((A_, li, slice(2, 4))), lam((A_, li, slice(2, 4))), AF.Exp)
            K.tt(lam((A_, li, slice(0, 1))), lam((A_, li, slice(2, 3))), lam((A_, li, slice(3, 4))), ALU.subtract)
            K.ts(lam((A_, li, slice(0, 1))), lam((A_, li, slice(0, 1))), lam_init[li], None, ALU.add)
            K.ts(lam((A_, li, slice(1, 2))), lam((A_, li, slice(0, 1))), -1.0, None, ALU.mult)
            K.act(pc((slice(0, 6), li, slice(0, 1))), pc((slice(0, 6), li, slice(0, 1))), AF.Exp)
            K.ts(pc((slice(0, 6), li, slice(0, 1))), pc((slice(0, 6), li, slice(0, 1))), -1.0, None, ALU.mult)

        def pcs(li, col, rows=128):
            return pc((slice(0, rows), li, slice(col, col + 1)))

        def wload(src_ap, src_v, kc):
            w = wring[wctr[0] % NWS]
            wctr[0] += 1
            K.dma(w((A_, slice(0, kc), A_)), V(src_ap, src_v.b), q="pool")
            return w

        def dense(w, kc, rhs_tile, ttiles, evac, M=128):
            for (t0, n) in ttiles:
                p = dps()
                pv = V(p.t[0:M, 0:n], p.b)
                for c in range(kc):
                    K.mm(pv, w((A_, c, slice(0, M))), rhs_tile((A_, c, slice(t0, t0 + n)), c),
                         start=(c == 0), stop=(c == kc - 1))
                evac(t0, n, pv)

        qT, kT, vT, zg, oTh, sq = (HV(hpre, c) for c in range(6))
        raw = [HV(hpre, 6), HV(hpre, 7)]
        vTb, zgb, oThb = HV(hpre, 8), HV(hpre, 9), HV(hpre, 10)
        gbT = HV(hpre, 11, slice(0, 32))
        ggT = HV(hpre, 12, slice(0, 32))
        clr1 = HV(hpre, 6, slice(0, 32))
        at0, at1 = HV(hpre, 12), HV(hpre, 13)
        rz0, rz1 = HV(hpre, 14), HV(hpre, 15)
        raws = [T(f"raws{i}", [128, NB_S, 3 + TS], F32) for i in range(2)]
        rctr = [0]
        rs = T("rs", [128, 512], F32)
        NTL = NTPM + NB_S
        btok = T("btok", [128, NTL, 6], F32)
        nbtok = T("nbtok", [128, NTL, 6], F32)
        gtok = T("gtok", [128, NTL, 6], F32)
        gctok = T("gctok", [128, NTL, 6], F32)
        NSL = 28
        smA = T("smA", [128, NSL, 128], F32, nb=NSL)

        class SV:
            def __init__(self, s0, n=1):
                self.ap = smA.t[:, s0, :] if n == 1 else smA.t[:, s0:s0 + n, :].rearrange("p a b -> p (a b)")
                self.bs = tuple(smA.b[s0:s0 + n])

            def __call__(self, key=None):
                return V(self.ap if key is None else self.ap[key], self.bs)

        names = ("ktok", "vtok", "R", "PT", "qTe", "kTe", "kw", "EG", "E", "ESU", "X", "XT", "X2", "XT2")
        gsm = [{n: SV(par * 14 + i) for i, n in enumerate(names)} for par in range(2)]
        t_r = [T(f"r{i}", [128, 128], F32) for i in range(2)]
        t_u = [T(f"u{i}", [128, 128], F32) for i in range(2)]
        t_Ug = [T(f"Ug{i}", [128, 128], F32) for i in range(2)]
        t_arg = [T(f"arg{i}", [128, 128], F32) for i in range(2)]
        t_gl = [T(f"gl{i}", [128, 2], F32) for i in range(2)]
        sst = [T(f"sst{i}", [128, 128], F32) for i in range(2)]
        sctr = [0]

        def rstd_from(pv, n, eps_v, scale):
            K.act(rs((A_, slice(0, n))), pv, AF.Ln, bias=eps_v, scale=scale)
            K.act(rs((A_, slice(0, n))), rs((A_, slice(0, n))), AF.Exp, scale=-0.5)

        def rmsnorm_gate(o_t, gate_t, gcol_v, ttiles, head, const_gate=None):
            for (t0, n) in ttiles:
                cs = slice(t0, t0 + n)
                K.act(sq((A_, cs)), o_t((A_, cs)), AF.Square)
                pp = dps()
                p = V(pp.t[:, 0:n], pp.b)
                K.mm(p, ones32, sq((A_, cs)))
                rstd_from(p, n, eps6, 1.0 / 128)
                K.tt(rs((A_, slice(0, n))), rs((A_, slice(0, n))), o_t((A_, cs)), ALU.mult)
                dst = mixT((A_, head, cs), head)
                if const_gate is None:
                    K.stt(dst, rs((A_, slice(0, n))), gcol_v, gate_t((A_, cs)), ALU.mult, ALU.mult)
                else:
                    K.ts(dst, rs((A_, slice(0, n))), gcol_v, const_gate, ALU.mult, ALU.mult)

        def l2norm(x_t, ttiles, scale):
            for (t0, n) in ttiles:
                cs = slice(t0, t0 + n)
                K.act(sq((A_, cs)), x_t((A_, cs)), AF.Square)
                pp = dps()
                p = V(pp.t[:, 0:n], pp.b)
                K.mm(p, ones32, sq((A_, cs)))
                rstd_from(p, n, eps6, 1.0)
                K.stt(x_t((A_, cs)), x_t((A_, cs)), scale, rs((A_, slice(0, n))), ALU.mult, ALU.mult)

        def sub(v, rows, cols=None):
            return V(v.ap[rows, :] if cols is None else v.ap[rows, cols], v.bs)

        def gdn_chunk(li, h, col0, Cn, gi, Sv, par):
            cs = slice(col0, col0 + Cn)
            g_ = gsm[par]
            ktok, vtok, R, PT, qTe, kTe, kw, EG, E, ESU, X, XT, X2, XT2 = (g_[n] for n in names)
            r_, u_, Ug, arg, gl = t_r[par], t_u[par], t_Ug[par], t_arg[par], t_gl[par]
            rC = slice(0, Cn)
            gcol = gtok((rC, gi, slice(h, h + 1)))
            gccol = gctok((rC, gi, slice(h, h + 1)))
            p = mps(Cn, 128)
            K.tr(p, kT((A_, cs)), ident)
            K.cp(ktok((rC, A_)), p, "act")
            p = mps(Cn, 128)
            K.tr(p, vT((A_, cs)), ident)
            K.cp(vtok((rC, A_)), p, "act")
            K.ts(Ug((rC, rC)), sub(Utri, rC, rC), gcol, None, ALU.mult, eng="pool")
            pg = mps(128, Cn)
            K.mm(pg, sub(ones32, rC), Ug((rC, rC)))
            K.ts(arg((rC, rC)), sub(pg, rC), gccol, 0.0, ALU.subtract, ALU.min)
            K.act(E((rC, rC)), arg((rC, rC)), AF.Exp)
            K.act(EG((A_, rC)), pg, AF.Exp)
            K.cp(gl((A_, slice(0, 1))), sub(pg, A_, slice(Cn - 1, Cn)))
            K.tt(ESU((rC, rC)), E((rC, rC)), sub(mSU, rC, rC), ALU.mult, eng="pool")
            K.tt(E((rC, rC)), E((rC, rC)), sub(Utri, rC, rC), ALU.mult, eng="pool")
            pk = mps(Cn, Cn)
            K.mm(pk, kT((A_, cs)), kT((A_, cs)))
            K.stt(X((rC, rC)), pk, nbtok((rC, gi, slice(h, h + 1))), ESU((rC, rC)), ALU.mult, ALU.mult)
            pq = mps(Cn, Cn)
            K.mm(pq, kT((A_, cs)), qT((A_, cs)))
            K.tt(PT((rC, rC)), pq, E((rC, rC)), ALU.mult)
            K.tt(qTe((A_, rC)), qT((A_, cs)), EG((A_, rC)), ALU.mult, eng="pool")
            K.tt(kTe((A_, rC)), kT((A_, cs)), EG((A_, rC)), ALU.mult, eng="pool")
            K.act(gl((rC, slice(1, 2))), gccol, AF.Exp, bias=gl((rC, slice(0, 1))), scale=-1.0)
            K.ts(kw((rC, A_)), ktok((rC, A_)), gl((rC, slice(1, 2))), None, ALU.mult, eng="pool")
            p = mps(Cn, Cn)
            K.tr(p, X((rC, rC)), sub(ident, rC, rC))
            K.cp(XT((rC, rC)), p, "act")
            K.tt(R((rC, rC)), X((rC, rC)), sub(ident, rC, rC), ALU.add, eng="pool")
            nst = max(1, int(math.ceil(math.log2(Cn))) - 1)
            Xc, XTc, Xn, XTn = X, XT, X2, XT2
            for k in range(nst):
                p1 = mps(Cn, Cn)
                K.mm(p1, Xc((rC, rC)), XTc((rC, rC)))
                K.cp(XTn((rC, rC)), p1, "act")
                if k < nst - 1:
                    p2 = mps(Cn, Cn)
                    K.mm(p2, XTc((rC, rC)), Xc((rC, rC)))
                    K.cp(Xn((rC, rC)), p2, "dve")
                p3 = mps(Cn, Cn)
                K.mm(p3, XTn((rC, rC)), R((rC, rC)))
                K.tt(R((rC, rC)), R((rC, rC)), p3, ALU.add)
                Xc, XTc, Xn, XTn = Xn, XTn, Xc, XTc
            p = mps(Cn, 128)
            K.mm(p, kTe((A_, rC)), Sv)
            K.tt(r_((rC, A_)), vtok((rC, A_)), p, ALU.subtract)
            p = mps(Cn, 128)
            K.mm(p, R((rC, rC)), r_((rC, A_)))
            K.ts(u_((rC, A_)), p, btok((rC, gi, slice(h, h + 1))), None, ALU.mult)
            p = mps(128, Cn)
            K.mm(p, Sv, qTe((A_, rC)), start=True, stop=False)
            K.mm(p, u_((rC, A_)), PT((rC, rC)), start=False, stop=True)
            K.cp(oTh((A_, cs)), p, "act")
            p = mps(128, 128)
            K.mm(p, kw((rC, A_)), u_((rC, A_)))
            K.stt(Sv, Sv, EG((A_, slice(Cn - 1, Cn))), p, ALU.mult, ALU.add)

        lnames = ("ebc", "enbc", "qtl", "ktl", "ktlt", "vtokA", "PTl")
        lsm = [{n: SV(par * 14 + i) for i, n in enumerate(lnames)} for par in range(2)]
        alogt = [T(f"alogt{i}", [128, 384], F32) for i in range(2)]

        def gla_chunk(li, pr, col0, Cn, Sp, par, vts, ots):
            cs = slice(col0, col0 + Cn)
            rC = slice(0, Cn)
            L = lsm[par]
            ebc, enbc, qtl, ktl, ktlt, vtokA, PTl = (L[n] for n in lnames)
            al = alogt[par]
            pz = dps()
            pzv = V(pz.t[0:Cn, 0:384], pz.b)
            K.mm(pzv, clr1((slice(0, 32), cs)), wab((A_, li, A_)))
            K.act(al((rC, A_)), pzv, AF.Exp, scale=-1.0)
            K.act(al((rC, A_)), al((rC, A_)), AF.Ln, bias=sub(one_c, rC), scale=1.0)
            pb = mps(128, Cn)
            K.mm(pb, al((rC, slice(pr * 128, (pr + 1) * 128))), sub(Utri, rC, rC))
            K.act(ebc((A_, rC)), pb, AF.Exp, scale=-1.0 / 16)
            K.act(enbc((A_, rC)), pb, AF.Exp, scale=1.0 / 16)
            K.tt(qtl((A_, rC)), qT((A_, cs)), ebc((A_, rC)), ALU.mult, eng="pool")
            K.tt(ktl((A_, rC)), kT((A_, cs)), enbc((A_, rC)), ALU.mult, eng="pool")
            p = mps(Cn, 128)
            K.tr(p, ktl((A_, rC)), ident)
            K.cp(ktlt((rC, A_)), p, "act")
            for hh in range(2):
                r64 = slice(hh * 64, hh * 64 + 64)
                p = mps(Cn, 128)
                K.tr(p, vts[hh]((A_, cs)), ident)
                K.cp(vtokA((rC, A_)), p, "act")
                pa = mps(Cn, Cn)
                K.mm(pa, ktl((r64, rC)), qtl((r64, rC)))
                K.tt(PTl((rC, rC)), pa, sub(Utri, rC, rC), ALU.mult)
                po = mps(128, Cn)
                K.mm(po, sub(Sp, r64), qtl((r64, rC)), start=True, stop=False)
                K.mm(po, vtokA((rC, A_)), PTl((rC, rC)), start=False, stop=True)
                K.cp(ots[hh]((A_, cs)), po, "act")
                pf = PS[2 + mctr[0] % 6]
                mctr[0] += 1
                psn = V(pf.t[r64, 0:128], pf.b)
                K.mm(psn, ktlt((rC, r64)), vtokA((rC, A_)))
                K.tt(sub(Sp, r64), sub(Sp, r64), psn, ALU.add)
                K.ts(sub(Sp, r64), sub(Sp, r64), ebc((r64, slice(Cn - 1, Cn))), None, ALU.mult)

        q16 = T("q16", [128, NTmax], BF16)
        eT16 = [T(f"eT16_{i}", [128, 512], BF16) for i in range(2)]
        ks16 = T("ks16", [128, 4, NS], BF16)
        vs32 = T("vs32", [128, 4, NS], F32)
        qblk = T("qblk", [128, NB_S, 4, 16], BF16)
        osamp = T("osamp", [128, 4, NS], F32)
        kpage = [SV(0, 4), SV(4, 4)]
        vpage = [SV(8, 4), SV(12, 4)]
        v16p = [T(f"v16p{i}", [128, 512], BF16) for i in range(2)]
        ktp16 = [T(f"ktp16_{i}", [128, 512], BF16) for i in range(2)]
        eTp16 = [T(f"eTp16_{i}", [128, 64], BF16) for i in range(2)]
        vsn16 = T("vsn16", [8, 4, 128], BF16)
        m8 = T("m8", [8, 64], BF16)
        zs = T("zs", [128, 64], F32)
        for j in range(8):
            K.cp(m8((A_, slice(j * 8, j * 8 + 8))), sub(Utri, slice(0, 8), slice(0, 8)))
        vtokp = [T(f"vtokp{i}", [128, 128], F32) for i in range(2)]

        def attn_prompt(li, h, g, ptiles):
            for (t0, n) in ptiles:
                q0 = GOFF[g] + t0
                nkt = (q0 + n) // 128
                acc = [PS[4], PS[5], PS[6], PS[7]]
                for kt in range(nkt):
                    kabs = kt * 128
                    qoff = max(0, kabs - q0)
                    nq = n - qoff
                    diag = kabs + 128 > q0
                    for m in range(2):
                        r64 = slice(m * 64, m * 64 + 64)
                        pst = PS[2 + (kt * 2 + m) % 2]
                        sv = V(pst.t[:, 0:nq], pst.b)
                        K.mm(sv, KTh((r64, slice(kabs, kabs + 128))), q16((r64, slice(t0 + qoff, t0 + n))))
                        e = eT16[(kt * 2 + m) % 2]
                        K.act(e((A_, slice(0, nq))), sv, AF.Exp)
                        if diag:
                            K.tt(e((A_, slice(0, 128))), e((A_, slice(0, 128))), Utri16, ALU.mult, eng="pool")
                        ov = V(acc[m].t[:, qoff:n], acc[m].b)
                        zv = V(acc[2 + m].t[:, qoff:n], acc[2 + m].b)
                        K.mm(ov, Vh((A_, kt, A_)), e((A_, slice(0, nq))), start=(kt == 0), stop=(kt == nkt - 1))
                        K.mm(zv, ones16, e((A_, slice(0, nq))), start=(kt == 0), stop=(kt == nkt - 1))
                cs = slice(t0, t0 + n)
                nn = slice(0, n)
                K.recip(rz0((A_, nn)), V(acc[2].t[:, 0:n], acc[2].b))
                K.recip(rz1((A_, nn)), V(acc[3].t[:, 0:n], acc[3].b))
                K.tt(at0((A_, nn)), V(acc[0].t[:, 0:n], acc[0].b), rz0((A_, nn)), ALU.mult)
                K.tt(at1((A_, nn)), V(acc[1].t[:, 0:n], acc[1].b), rz1((A_, nn)), ALU.mult)
                K.stt(oTh((A_, cs)), at1((A_, nn)), lam((A_, li, slice(1, 2))), at0((A_, nn)), ALU.mult, ALU.add)

        def attn_sample(li, NPG):
            for b in range(NB_S):
                oacc, zacc = PS[6], PS[7]
                ov = V(oacc.t[:, 0:64], oacc.b)
                zv = V(zacc.t[:, 0:64], zacc.b)
                for h in range(4):
                    p = V(PS[2].t[0:8, 0:128], PS[2].b)
                    K.tr(p, vs32((A_, h, slice(b * TS, (b + 1) * TS))), ident)
                    K.cp(vsn16((A_, h, A_)), p, "act")
                for n in range(NPAGES):
                    par = n % 2
                    col = b * NPAGES + n
                    K.gather(kpage[par](), V(cache_k.ap.rearrange("l n c -> (l n) c"), cache_k.b), IXl[li]((A_, slice(col, col + 1))))
                    K.gather(vpage[par](), V(cache_v.ap.rearrange("l n c -> (l n) c"), cache_v.b), IXl[li]((A_, slice(col, col + 1))))
                    K.cp(v16p[par](), vpage[par](), "pool")
                    pt_ = PS[2 + par]
                    for h in range(4):
                        K.tr(V(pt_.t[:, h * 128:(h + 1) * 128], pt_.b), kpage[par]((A_, slice(h * 128, (h + 1) * 128))), ident)
                    K.cp(ktp16[par](), V(pt_.t[:, :], pt_.b), "act" if par == 0 else "dve")
                    ps_ = PS[4 + par]
                    for h in range(4):
                        K.mm(V(ps_.t[:, h * 16:(h + 1) * 16], ps_.b), ktp16[par]((A_, slice(h * 128, (h + 1) * 128))), qblk((A_, b, h, A_)))
                    K.act(eTp16[par](), V(ps_.t[:, 0:64], ps_.b), AF.Exp)
                    for h in range(4):
                        K.mm(V(oacc.t[:, h * 16:(h + 1) * 16], oacc.b), v16p[par]((A_, slice(h * 128, (h + 1) * 128))),
                             eTp16[par]((A_, slice(h * 16, (h + 1) * 16))), start=(n == 0 and h == 0), stop=False)
                    K.mm(zv, ones16, eTp16[par](), start=(n == 0), stop=False)
                bs_ = slice(b * TS, (b + 1) * TS)
                ps_ = PS[4]
                for h in range(4):
                    K.mm(V(ps_.t[0:8, h * 16:(h + 1) * 16], ps_.b), ks16((A_, h, bs_)), qblk((A_, b, h, A_)))
                e8 = eTp16[0]
                K.act(e8((slice(0, 8), A_)), V(ps_.t[0:8, 0:64], ps_.b), AF.Exp)
                K.tt(e8((slice(0, 8), A_)), e8((slice(0, 8), A_)), m8(), ALU.mult)
                for h in range(4):
                    K.mm(V(oacc.t[:, h * 16:(h + 1) * 16], oacc.b), vsn16((A_, h, A_)), e8((slice(0, 8), slice(h * 16, (h + 1) * 16))),
                         start=False, stop=(h == 3))
                K.mm(zv, sub(ones16, slice(0, 8)), e8((slice(0, 8), A_)), start=False, stop=True)
                K.recip(zs(), zv)
                K.tt(zs(), zs(), ov, ALU.mult)
                for h in range(4):
                    K.stt(osamp((A_, h, bs_)), zs((A_, slice(h * 16 + 8, h * 16 + 16))), lam((A_, li, slice(1, 2))),
                          zs((A_, slice(h * 16, h * 16 + 8))), ALU.mult, ALU.add)

        lnm = T("lnm", [128, 512], F32)
        lnr = T("lnr", [128, 512], F32)
        lnt = [T(f"lnt{i}", [128, 512], F32) for i in range(2)]
        sgt = [T(f"sgt{i}", [128, 512], F32) for i in range(2)]

        def layernorm(li, gcol0, bcol0, ttiles, to_y, cols_abs):
            for (t0, n) in ttiles:
                cs = slice(t0, t0 + n)
                nn = slice(0, n)
                pa, pb = PS[2], PS[3]
                pav, pbv = V(pa.t[:, 0:n], pa.b), V(pb.t[:, 0:n], pb.b)
                for c in range(KC):
                    K.mm(pav, ones32, hpre((A_, c, cs), c), start=(c == 0), stop=(c == KC - 1))
                for c in range(KC):
                    t = lnt[c % 2]
                    K.act(t((A_, nn)), hpre((A_, c, cs), c), AF.Square)
                    K.mm(pbv, ones32, t((A_, nn)), start=(c == 0), stop=(c == KC - 1))
                K.ts(lnm((A_, nn)), pav, 1.0 / D, None, ALU.mult)
                K.tt(lnr((A_, nn)), lnm((A_, nn)), lnm((A_, nn)), ALU.mult)
                K.stt(lnr((A_, nn)), pbv, 1.0 / D, lnr((A_, nn)), ALU.mult, ALU.subtract)
                K.act(lnr((A_, nn)), lnr((A_, nn)), AF.Ln, bias=eps5, scale=1.0)
                K.act(lnr((A_, nn)), lnr((A_, nn)), AF.Exp, scale=-0.5)
                for c in range(KC):
                    hv = hpre((A_, c, cs), c)
                    K.tt(hv, hv, lnm((A_, nn)), ALU.subtract)
                    K.tt(hv, hv, lnr((A_, nn)), ALU.mult, eng="pool")
                    K.ts(hv, hv, pcs(li, gcol0 + c), pcs(li, bcol0 + c), ALU.mult, ALU.add)
                    K.cp(xT16((A_, c, cs), c), hv, "pool")
                    if to_y:
                        K.dma(V(yT.ap[c * 128:(c + 1) * 128, cols_abs(t0, n)], yT.b), hv)

        def run_layer(g, li):
            NPG = GSZ[g]
            NTP = GT[g]
            has_s = g == 0
            last = g == NG - 1
            NT = NPG + (NS if has_s else 0)
            ptiles = tok_tiles(NPG)
            stile = [(NPG, NS)] if has_s else []
            ttiles = ptiles + stile
            dtiles = list(ttiles)
            if has_s and ptiles[-1][1] + NS <= 512:
                dtiles = ptiles[:-1] + [(ptiles[-1][0], ptiles[-1][1] + NS)]
            a0 = GOFF[g]

            def cols_abs(t0, n):
                return slice(a0 + t0, a0 + t0 + n) if t0 < NPG else slice(SEQ, SEQ + NS)

            def rcols(t0, n):
                return slice(t0, t0 + n)

            def wblk(name):
                return wload(w_in.ap[li, BLK[name]], w_in, KC)

            def ev_copy(dst, scale=None):
                def f(t0, n, pv):
                    if scale is None:
                        K.cp(dst((A_, slice(t0, t0 + n))), pv, "act")
                    else:
                        K.act(dst((A_, slice(t0, t0 + n))), pv, AF.Copy, scale=scale)
                return f

            chunks = [(t * 128, 128, t) for t in range(NTP)]
            schunks = [(NPG + b * TS, TS, NTP + b) for b in range(NB_S)] if has_s else []

            w = wblk("gb")
            dense(w, KC, xT16, dtiles, lambda t0, n, pv: K.act(gbT((slice(0, 6), slice(t0, t0 + n))), pv, AF.Sigmoid), M=6)
            w = wblk("ga")
            dense(w, KC, xT16, dtiles,
                  lambda t0, n, pv: K.act(ggT((slice(0, 6), slice(t0, t0 + n))), pv, AF.Exp, bias=pcs(li, 1, 6), scale=1.0), M=6)
            gg6 = ggT((slice(0, 6), slice(0, NT)))
            K.act(gg6, gg6, AF.Ln, bias=sub(one_c, slice(0, 6)), scale=1.0)
            K.ts(gg6, gg6, pcs(li, 0, 6), None, ALU.mult)
            for (c0, Cn, gi) in chunks + schunks:
                rC = slice(0, Cn)
                p = mps(Cn, 6)
                K.tr(p, gbT((slice(0, 6), slice(c0, c0 + Cn))), sub(ident, slice(0, 6), slice(0, 6)))
                K.cp(btok((rC, gi, A_)), p, "act")
                K.ts(nbtok((rC, gi, A_)), p, -1.0, None, ALU.mult)
                p = mps(Cn, 6)
                K.tr(p, ggT((slice(0, 6), slice(c0, c0 + Cn))), sub(ident, slice(0, 6), slice(0, 6)))
                K.cp(gtok((rC, gi, A_)), p, "act")
                p = mps(Cn, 6)
                K.mm(p, sub(Utri, rC, rC), gtok((rC, gi, A_)))
                K.cp(gctok((rC, gi, A_)), p, "dve")

            for h in range(6):
                for (nm, dst, ci) in (("aq", qT, h), ("ak", kT, 6 + h), ("av", vT, 12 + h)):
                    rw = raw[rctr[0] % 2]
                    rws = raws[rctr[0] % 2]
                    rctr[0] += 1
                    K.cp(rw((A_, slice(0, 3))), ctail((A_, li, ci, A_), li * 18 + ci), "pool")
                    if has_s:
                        K.dma(rws((A_, A_, slice(0, 3))), V(st_conv.ap[li, :, :, ci, :].rearrange("b p j -> p b j"), st_conv.b))
                    w = wblk(f"{nm}{h}")

                    def ev(t0, n, pv, rw=rw, rws=rws):
                        npr = max(0, min(n, NPG - t0))
                        if npr > 0:
                            K.cp(rw((A_, slice(3 + t0, 3 + t0 + npr))), V(pv.ap[:, 0:npr], pv.bs), "act")
                        if npr < n:
                            K.cp(rws((A_, A_, slice(3, 3 + TS))), V(pv.ap[:, npr:n].rearrange("p (b t) -> p b t", b=NB_S), pv.bs), "act")
                    dense(w, KC, xT16, dtiles, ev)
                    K.cp(ctail((A_, li, ci, A_), li * 18 + ci), rw((A_, slice(NPG, NPG + 3))), "pool")
                    if last:
                        K.dma(V(o_conv.ap[li, 0, :, ci, :], o_conv.b), rw((A_, slice(NPG, NPG + 3))))
                    if has_s:
                        K.dma(V(o_conv.ap[li, 1:1 + NB_S, :, ci, :].rearrange("b p j -> p b j"), o_conv.b), rws((A_, A_, slice(TS, TS + 3))))
                    for j in range(4):
                        wj = cw((A_, li, ci, slice(j, j + 1)))
                        if j == 0:
                            K.ts(dst((A_, slice(0, NPG))), rw((A_, slice(0, NPG))), wj, None, ALU.mult)
                        else:
                            K.stt(dst((A_, slice(0, NPG))), rw((A_, slice(j, j + NPG))), wj, dst((A_, slice(0, NPG))), ALU.mult, ALU.add)
                    if has_s:
                        dsv = V(dst.ap[:, NPG:NPG + NS].rearrange("p (b t) -> p b t", b=NB_S), dst.bs)
                        for j in range(4):
                            wj = cw((A_, li, ci, slice(j, j + 1)))
                            if j == 0:
                                K.ts(dsv, rws((A_, A_, slice(0, TS))), wj, None, ALU.mult)
                            else:
                                K.stt(dsv, rws((A_, A_, slice(j, j + TS))), wj, dsv, ALU.mult, ALU.add)
                    K.act(dst((A_, slice(0, NT))), dst((A_, slice(0, NT))), AF.Silu)
                w = wblk(f"az{h}")
                dense(w, KC, xT16, dtiles, lambda t0, n, pv: K.act(zg((A_, slice(t0, t0 + n))), pv, AF.Silu))
                l2norm(qT, dtiles, 128 ** -0.5)
                l2norm(kT, dtiles, 1.0)
                Sv = Sg((A_, li, h, A_), li * 6 + h)
                for ci_, (c0, Cn, gi) in enumerate(chunks):
                    gdn_chunk(li, h, c0, Cn, gi, Sv, ci_ % 2)
                if last:
                    K.dma(V(o_gdn.ap[li, 0, h], o_gdn.b), Sv)
                for b, (c0, Cn, gi) in enumerate(schunks):
                    ss = sst[sctr[0] % 2]
                    sctr[0] += 1
                    K.dma(ss(), V(st_gdn.ap[li, b, h], st_gdn.b))
                    gdn_chunk(li, h, c0, Cn, gi, ss(), b % 2)
                    K.dma(V(o_gdn.ap[li, 1 + b, h], o_gdn.b), ss())
                rmsnorm_gate(oTh, zg, pcs(li, 2), dtiles, h)
            if dbg is not None and dbg.get("stage") == "gdn":
                return "gdn"

            if has_s:
                K.memset(qblk(), 0.0)
            for h in range(4):
                dense(wblk(f"dq{h}"), KC, xT16, dtiles, ev_copy(qT, 0.125))
                dense(wblk(f"dk{h}"), KC, xT16, dtiles, ev_copy(kT))
                dense(wblk(f"dv{h}"), KC, xT16, dtiles, ev_copy(vT))
                for x_t in (qT, kT):
                    for (t0, n) in dtiles:
                        cs = slice(t0, t0 + n)
                        pp = dps()
                        pv = V(pp.t[:, 0:n], pp.b)
                        K.mm(pv, perm, x_t((A_, cs)))
                        K.tt(sq((A_, cs)), pv, rope((A_, 1, cs)), ALU.mult)
                        K.tt(x_t((A_, cs)), x_t((A_, cs)), rope((A_, 0, cs)), ALU.mult, eng="pool")
                        K.tt(x_t((A_, cs)), x_t((A_, cs)), sq((A_, cs)), ALU.add)
                K.dma(V(newk.ap[li, h, :, a0:a0 + NPG], (newk.b[li * 4 + h],)), kT((A_, slice(0, NPG))))
                if has_s:
                    K.dma(V(newk.ap[li, h, :, SEQ:SEQ + NS], (newk.b[li * 4 + h],)), kT((A_, slice(NPG, NT))))
                    K.cp(ks16((A_, h, A_)), kT((A_, slice(NPG, NT))), "pool")
                    K.cp(vs32((A_, h, A_)), vT((A_, slice(NPG, NT))), "pool")
                    for b in range(NB_S):
                        K.cp(qblk((slice(0, 64), b, h, slice(0, 8))), qT((slice(0, 64), slice(NPG + b * TS, NPG + (b + 1) * TS))), "pool")
                        K.cp(qblk((slice(64, 128), b, h, slice(8, 16))), qT((slice(64, 128), slice(NPG + b * TS, NPG + (b + 1) * TS))), "pool")
                K.cp(q16((A_, slice(0, NPG))), qT((A_, slice(0, NPG))), "pool")
                if a0 > 0:
                    K.dma(KTh((A_, slice(0, a0))), V(newk.ap[li, h, :, 0:a0], (newk.b[li * 4 + h],)), q="pool")
                    K.dma(Vh((A_, slice(0, a0 // 128), A_)),
                          V(newv.ap[li, 0:a0, h * 128:(h + 1) * 128].rearrange("(t p) c -> p t c", p=128), (newv.b[li * 4 + h],)), q="pool")
                K.cp(KTh((A_, slice(a0, a0 + NPG))), kT((A_, slice(0, NPG))), "pool")
                for t in range(NTP):
                    p = mps(128, 128)
                    K.tr(p, vT((A_, slice(t * 128, (t + 1) * 128))), ident)
                    vt = vtokp[t % 2]
                    K.cp(vt(), p, "act")
                    K.cp(Vh((A_, a0 // 128 + t, A_)), p, "dve")
                    K.dma(V(newv.ap[li, a0 + t * 128:a0 + (t + 1) * 128, h * 128:(h + 1) * 128], (newv.b[li * 4 + h],)), vt())
                if has_s:
                    p = mps(NS, 128)
                    K.tr(p, vT((A_, slice(NPG, NT))), ident)
                    vt = vtokp[0]
                    K.cp(vt((slice(0, NS), A_)), p, "act")
                    K.dma(V(newv.ap[li, SEQ:SEQ + NS, h * 128:(h + 1) * 128], (newv.b[li * 4 + h],)), vt((slice(0, NS), A_)))
                attn_prompt(li, h, g, ptiles)
                rmsnorm_gate(oTh, None, pcs(li, 3), ptiles, 6 + h, const_gate=1.0 - lam_init[li])
            if has_s:
                attn_sample(li, NPG)
                for h in range(4):
                    K.cp(oTh((A_, slice(NPG, NT))), osamp((A_, h, A_)), "pool")
                    rmsnorm_gate(oTh, None, pcs(li, 3), stile, 6 + h, const_gate=1.0 - lam_init[li])
            if dbg is not None and dbg.get("stage") == "diff":
                return "diff"

            K.memset(clr1((slice(0, 32), slice(0, NT))), 1.0)
            dense(wblk("lr"), KC, xT16, dtiles, lambda t0, n, pv: K.cp(clr1((slice(0, 16), slice(t0, t0 + n))), pv, "act"), M=16)
            for pr in range(3):
                dense(wblk(f"cq{pr}"), KC, xT16, dtiles, ev_copy(qT, 0.125))
                dense(wblk(f"ck{pr}"), KC, xT16, dtiles, ev_copy(kT))
                vts, gts, ots = (vT, vTb), (zg, zgb), (oTh, oThb)
                for hh in range(2):
                    hd = 2 * pr + hh
                    dense(wblk(f"cv{hd}"), KC, xT16, dtiles, ev_copy(vts[hh]))
                    dense(wblk(f"cr{hd}"), KC, xT16, dtiles,
                          lambda t0, n, pv, gt=gts[hh]: K.act(gt((A_, slice(t0, t0 + n))), pv, AF.Silu))
                Sp = Sl((A_, li, pr, A_), li * 3 + pr)
                for ci_, (c0, Cn, gi) in enumerate(chunks):
                    gla_chunk(li, pr, c0, Cn, Sp, ci_ % 2, vts, ots)
                if last:
                    K.dma(V(o_gla.ap[li, 0, pr], o_gla.b), Sp)
                for b, (c0, Cn, gi) in enumerate(schunks):
                    ss = sst[sctr[0] % 2]
                    sctr[0] += 1
                    K.dma(ss(), V(st_gla.ap[li, b, pr], st_gla.b))
                    gla_chunk(li, pr, c0, Cn, ss(), b % 2, vts, ots)
                    K.dma(V(o_gla.ap[li, 1 + b, pr], o_gla.b), ss())
                for hh in range(2):
                    rmsnorm_gate(ots[hh], gts[hh], pcs(li, 4), dtiles, 10 + 2 * pr + hh)
            if dbg is not None and dbg.get("stage") == "gla":
                return "gla"

            for c in range(KC):
                if li == 0:
                    K.dma(hpre((A_, c, slice(0, NPG)), c), V(xpT.ap[c * 128:(c + 1) * 128, a0:a0 + NPG], xpT.b))
                    if has_s:
                        K.dma(hpre((A_, c, slice(NPG, NT)), c), V(xsT.ap[c * 128:(c + 1) * 128, :], xsT.b))
                w = wload(w_out.ap[li, c], w_out, KC)

                def ev_o(t0, n, pv, c=c):
                    cs = slice(t0, t0 + n)
                    src = hpre((A_, c, cs), c) if li == 0 else xT16((A_, c, cs), c)
                    K.stt(hpre((A_, c, cs), c), src, ALPHA, pv, ALU.mult, ALU.add)
                dense(w, KC, mixT, dtiles, ev_o)
            layernorm(li, 8, 8 + KC, ttiles, False, cols_abs)
            for qf in range(NQ):
                for j in range(FQ):
                    jj = qf * FQ + j
                    wg = wload(w_gu.ap[li, 2 * jj], w_gu, KC)
                    wu = wload(w_gu.ap[li, 2 * jj + 1], w_gu, KC)
                    for ti, (t0, n) in enumerate(dtiles):
                        sg = sgt[ti % 2]
                        dense(wg, KC, xT16, [(t0, n)], lambda t0, n, pv, sg=sg: K.act(sg((A_, slice(0, n))), pv, AF.Silu))
                        dense(wu, KC, xT16, [(t0, n)],
                              lambda t0, n, pv, sg=sg, j=j: K.tt(mixT((A_, j, slice(t0, t0 + n)), j), sg((A_, slice(0, n))), pv, ALU.mult))
                for c in range(KC):
                    wd = wload(w_dn.ap[li, qf, c], w_dn, FQ)

                    def ev_d(t0, n, pv, c=c, qf=qf):
                        cs = slice(t0, t0 + n)
                        hv = hpre((A_, c, cs), c)
                        if qf == 0:
                            K.stt(hv, hv, ALPHA, pv, ALU.mult, ALU.add)
                        else:
                            K.tt(hv, hv, pv, ALU.add)
                    dense(wd, FQ, mixT, dtiles, ev_d)
            layernorm(li, 8 + 2 * KC, 8 + 3 * KC, ttiles, li == DEPTH - 1, cols_abs)
            return None

        def load_x(g):
            NPG = GSZ[g]
            a0 = GOFF[g]
            K.dma(rope((A_, A_, slice(0, NPG))), V(ropec.ap[:, :, a0:a0 + NPG].rearrange("a p n -> p a n"), ropec.b))
            if g == 0:
                K.dma(rope((A_, A_, slice(NPG, NPG + NS))), V(ropec.ap[:, :, SEQ:SEQ + NS].rearrange("a p n -> p a n"), ropec.b))
            for c in range(KC):
                K.dma(xT16((A_, c, slice(0, NPG)), c), V(xpT.ap[c * 128:(c + 1) * 128, a0:a0 + NPG], xpT.b), q="pool")
                if g == 0:
                    K.dma(xT16((A_, c, slice(NPG, NPG + NS)), c), V(xsT.ap[c * 128:(c + 1) * 128, :], xsT.b), q="pool")

        stop = None
        for g in range(NG):
            load_x(g)
            for li in range(DEPTH):
                stop = run_layer(g, li)
                if stop:
                    break
            if stop:
                break
        if stop:
            NT0 = GSZ[0] + NS
            hs = {"gdn": range(0, 6), "diff": range(6, 10), "gla": range(10, 16)}[stop]
            for i, h in enumerate(hs):
                K.cp(hpre((A_, i, slice(0, NT0)), i), mixT((A_, h, slice(0, NT0)), h))
                K.dma(V(dbg_out.ap[i, :, 0:NT0], dbg_out.b), hpre((A_, i, slice(0, NT0)), i))
        S.emit(st)
    return nc


def _blk(Wcols, kc):
    n = Wcols.shape[1]
    out = np.zeros((128, kc, 128), np.float32)
    out[:, :, :n] = Wcols.reshape(kc, 128, n).transpose(1, 0, 2)
    return out


def prepare_shared(inp, SEQ, PAST):
    f32 = np.float32
    sh = {}
    NPOOL = inp["cache_k"].shape[1]
    sh["cache_k"] = np.ascontiguousarray(inp["cache_k"]).reshape(DEPTH, NPOOL * 128, 512)
    sh["cache_v"] = np.ascontiguousarray(inp["cache_v"]).reshape(DEPTH, NPOOL * 128, 512)
    blks = in_blocks()
    w_in = np.zeros((DEPTH, N_IN_BLK, 128, KC, 128), f32)
    w_out = np.zeros((DEPTH, KC, 128, KC, 128), f32)
    w_gu = np.zeros((DEPTH, 2 * FC, 128, KC, 128), f32)
    w_dn = np.zeros((DEPTH, NQ, KC, 128, FQ, 128), f32)
    for li in range(DEPTH):
        W = inp["w_in"][li]
        for i, (_, c0, n) in enumerate(blks):
            w_in[li, i] = _blk(W[:, c0:c0 + n], KC)
        W = inp["w_out"][li]
        for c in range(KC):
            w_out[li, c] = _blk(W[:, c * 128:(c + 1) * 128], KC)
        W = inp["w_gate_up"][li]
        for j in range(FC):
            w_gu[li, 2 * j] = _blk(W[:, j * 128:(j + 1) * 128], KC)
            w_gu[li, 2 * j + 1] = _blk(W[:, DFF + j * 128:DFF + (j + 1) * 128], KC)
        W = inp["w_down"][li]
        for hf in range(NQ):
            for c in range(KC):
                w_dn[li, hf, c] = _blk(W[hf * FQ * 128:(hf + 1) * FQ * 128, c * 128:(c + 1) * 128], FQ)
    sh["w_in"], sh["w_out"], sh["w_gu"], sh["w_dn"] = w_in, w_out, w_gu, w_dn
    sh["convw"] = np.ascontiguousarray(inp["gdn_conv_w"].reshape(DEPTH, 4, 18, 128).transpose(0, 3, 2, 1))
    NPC = 8 + 4 * KC
    pcol = np.zeros((DEPTH, 128, NPC), f32)
    pcol[:, 0:6, 0] = inp["gdn_a_log"]
    pcol[:, 0:6, 1] = inp["gdn_dt_bias"]
    pcol[:, :, 2] = inp["gdn_norm_g"]
    pcol[:, :, 3] = inp["diff_norm_g"]
    pcol[:, :, 4] = inp["gla_norm_g"]
    for i, nm in enumerate(("ln1_g", "ln1_b", "ln2_g", "ln2_b")):
        pcol[:, :, 8 + i * KC:8 + (i + 1) * KC] = inp[nm].reshape(DEPTH, KC, 128).transpose(0, 2, 1)
    sh["pcol"] = pcol
    sh["lamv"] = np.ascontiguousarray(inp["diff_lambda"].reshape(1, DEPTH * 256))
    wa2b = np.zeros((DEPTH, 32, 384), f32)
    wa2b[:, 0:16] = inp["gla_wa2"]
    wa2b[:, 16] = inp["gla_ba"]
    sh["wa2b"] = wa2b
    c = np.zeros((128, 6, 128), f32)
    p = np.arange(128)
    c[:, 0] = np.eye(128)
    c[:, 1] = (p[:, None] <= p[None, :])
    c[:, 2] = (p[:, None] < p[None, :])
    c[:, 3] = 1.0
    c[:, 5] = 1.0
    for m0 in (0, 64):
        for d in range(8):
            c[m0 + d + 8, 4, m0 + d] = 1.0
            c[m0 + d, 4, m0 + d + 8] = 1.0
    sh["consts"] = c
    NTOK = SEQ + NS
    pos = np.concatenate([np.arange(SEQ), np.tile(PAST + np.arange(TS), NB_S)]).astype(f32)
    inv = (1.0 / (f32(500000.0) ** (np.arange(8, dtype=f32) * f32(2.0) / f32(16)))).astype(f32)
    ang = (pos[None, :] * inv[:, None]).astype(f32)
    rc = np.zeros((2, 128, NTOK), f32)
    rc[0] = 1.0
    for m0 in (0, 64):
        rc[0, m0:m0 + 8] = np.cos(ang)
        rc[0, m0 + 8:m0 + 16] = np.cos(ang)
        rc[1, m0:m0 + 8] = -np.sin(ang)
        rc[1, m0 + 8:m0 + 16] = np.sin(ang)
    sh["ropec"] = rc
    return sh


def prepare_inputs(inp, SEQ, PAST):
    sh = prepare_shared(inp, SEQ, PAST)
    maps = []
    for c in range(8):
        b = c // 2
        sl = slice(NB_S * c, NB_S * (c + 1))
        m = dict(sh)
        m["xpT"] = np.ascontiguousarray(inp["x_prompt"][b].T)
        m["xsT"] = np.ascontiguousarray(inp["x_sample"][sl].reshape(NS, D).T)
        m["ptab"] = np.ascontiguousarray(inp["page_table"][sl].reshape(1, -1)).astype(np.int32)
        m["st_gdn"] = np.ascontiguousarray(inp["state_gdn"][:, sl])
        m["st_conv"] = np.ascontiguousarray(
            inp["state_gdn_conv"][:, sl].reshape(DEPTH, NB_S, 3, 18, 128).transpose(0, 1, 4, 3, 2))
        m["st_gla"] = np.ascontiguousarray(inp["state_gla"][:, sl].reshape(DEPTH, NB_S, 3, 128, 128))
        maps.append(m)
    return maps


def assemble(results, SEQ):
    f32 = np.float32
    B = 4
    y_p = np.zeros((B, SEQ, D), f32)
    y_s = np.zeros((8 * NB_S, TS, D), f32)
    nk_p = np.zeros((DEPTH, B, SEQ, 4, 128), f32)
    nv_p = np.zeros((DEPTH, B, SEQ, 4, 128), f32)
    gs_p = np.zeros((DEPTH, B, 6, 128, 128), f32)
    gc_p = np.zeros((DEPTH, B, 3, 2304), f32)
    ls_p = np.zeros((DEPTH, B, 6, 64, 128), f32)
    nk_s = np.zeros((DEPTH, 8 * NB_S, TS, 4, 128), f32)
    nv_s = np.zeros((DEPTH, 8 * NB_S, TS, 4, 128), f32)
    gs_s = np.zeros((DEPTH, 8 * NB_S, 6, 128, 128), f32)
    gc_s = np.zeros((DEPTH, 8 * NB_S, 3, 2304), f32)
    ls_s = np.zeros((DEPTH, 8 * NB_S, 6, 64, 128), f32)
    for c in range(8):
        r = results[c]
        sl = slice(NB_S * c, NB_S * (c + 1))
        yT = np.asarray(r["yT"])
        nk = np.asarray(r["newk"])
        nv = np.asarray(r["newv"])
        og = np.asarray(r["o_gdn"])
        oc = np.asarray(r["o_conv"])
        ol = np.asarray(r["o_gla"])
        conv = oc.transpose(0, 1, 4, 3, 2).reshape(DEPTH, 1 + NB_S, 3, 2304)
        if c % 2 == 0:
            b = c // 2
            y_p[b] = yT[:, :SEQ].T
            nk_p[:, b] = nk[:, :, :, :SEQ].transpose(0, 3, 1, 2)
            nv_p[:, b] = nv[:, :SEQ].reshape(DEPTH, SEQ, 4, 128)
            gs_p[:, b] = og[:, 0]
            gc_p[:, b] = conv[:, 0]
            ls_p[:, b] = ol[:, 0].reshape(DEPTH, 6, 64, 128)
        y_s[sl] = yT[:, SEQ:].T.reshape(NB_S, TS, D)
        nk_s[:, sl] = nk[:, :, :, SEQ:].transpose(0, 3, 1, 2).reshape(DEPTH, NB_S, TS, 4, 128)
        nv_s[:, sl] = nv[:, SEQ:].reshape(DEPTH, NB_S, TS, 4, 128)
        gs_s[:, sl] = og[:, 1:]
        gc_s[:, sl] = conv[:, 1:]
        ls_s[:, sl] = ol[:, 1:].reshape(DEPTH, NB_S, 6, 64, 128)
    return (y_p, y_s, nk_p, nv_p, gs_p, gc_p, ls_p, nk_s, nv_s, gs_s, gc_s, ls_s)


def run_step(inputs, SEQ, PAST):
    inp = {k: np.asarray(v) for k, v in inputs.items()}
    NPOOL = inp["cache_k"].shape[1]
    nc = build_program(SEQ, PAST, NPOOL)
    in_maps = prepare_inputs(inp, SEQ, PAST)
    res = run_bass_kernel_spmd(nc, in_maps, core_ids=list(range(8)))
    return assemble(res.results, SEQ)


def kernel(**inputs):
    SEQ = int(np.asarray(inputs["x_prompt"]).shape[1])
    PAST = int(np.asarray(inputs["page_table"]).shape[1]) * int(np.asarray(inputs["cache_k"]).shape[2])
    return run_step(inputs, SEQ, PAST)
```

```python
import math
from contextlib import ExitStack

import numpy as np
import concourse.bass as bass
import concourse.mybir as mybir
from concourse.bass_utils import run_bass_kernel_spmd

F32 = mybir.dt.float32
BF16 = mybir.dt.bfloat16
I32 = mybir.dt.int32
ALU = mybir.AluOpType
AF = mybir.ActivationFunctionType
AX = mybir.AxisListType

D = 2048
KC = 16
DEPTH = 2
DFF = 5632
FC = 44
FH = 22
NQ = 4
FQ = 11
NB_S = 4
TS = 8
NS = NB_S * TS
ALPHA = (2 * DEPTH) ** 0.25
N_IN_BLK = 57
EPOCH = 30000
NSLOT = {"sp": 24, "pool": 12, "act": 8}


class Buf:
    __slots__ = ("name", "lw", "rd")

    def __init__(self, name=""):
        self.name = name
        self.lw = None
        self.rd = {}


class Op:
    __slots__ = ("eng", "fn", "deps", "is_dma", "slot", "use", "needed", "sig", "uid")


class V:
    __slots__ = ("ap", "bs")

    def __init__(self, ap, bs):
        self.ap = ap
        self.bs = tuple(bs)


class Sched:
    def __init__(self, nc):
        self.nc = nc
        self.ops = {e: [] for e in ("pe", "act", "dve", "pool", "sp")}
        self.slot_rr = {q: 0 for q in NSLOT}
        self.slot_last = {q: [None] * NSLOT[q] for q in NSLOT}
        self.slot_use = {q: [0] * NSLOT[q] for q in NSLOT}
        self.uid = 0

    def rec(self, eng, fn, reads, writes, is_dma=False):
        op = Op()
        op.eng = eng
        op.fn = fn
        op.is_dma = is_dma
        op.needed = is_dma
        op.sig = None
        op.uid = self.uid
        self.uid += 1
        deps = {}
        raw = set()
        for b in reads:
            if b.lw is not None:
                deps[b.lw.uid] = b.lw
                raw.add(b.lw.uid)
        for b in writes:
            if b.lw is not None:
                deps[b.lw.uid] = b.lw
            for r in b.rd.values():
                deps[r.uid] = r
        if is_dma:
            s = self.slot_rr[eng]
            self.slot_rr[eng] = (s + 1) % NSLOT[eng]
            prev = self.slot_last[eng][s]
            if prev is not None:
                deps[prev.uid] = prev
            self.slot_last[eng][s] = op
            self.slot_use[eng][s] += 1
            op.slot = s
            op.use = self.slot_use[eng][s]
        final = []
        for d in deps.values():
            if (not d.is_dma) and (not is_dma) and d.eng == eng:
                if eng == "pe":
                    continue
            d.needed = True
            final.append(d)
        op.deps = final
        for b in reads:
            b.rd[("d", op.uid) if is_dma else eng] = op
        for b in writes:
            b.lw = op
            b.rd = {}
        self.ops[eng].append(op)
        return op

    def emit(self, stack):
        nc = self.nc
        nsig = {}
        for e in ("pe", "act", "dve", "pool"):
            c = 0
            for op in self.ops[e]:
                if (not op.is_dma) and op.needed:
                    op.sig = c
                    c += 1
            nsig[e] = c
        sems = {}
        for e in ("pe", "act", "dve", "pool"):
            for k in range(nsig[e] // EPOCH + 1):
                sems[(e, k)] = stack.enter_context(nc.semaphore(f"s_{e}_{k}"))
        final_waits = {}
        for q in NSLOT:
            for s in range(NSLOT[q]):
                if self.slot_use[q][s] > 0:
                    sems[("d", q, s)] = stack.enter_context(nc.semaphore(f"d_{q}_{s}"))
                    final_waits[("d", q, s)] = 16 * self.slot_use[q][s]

        def tok(d):
            if d.is_dma:
                return ("d", d.eng, d.slot), 16 * d.use
            return (d.eng, d.sig // EPOCH), d.sig % EPOCH + 1

        def run(engname, eng):
            waited = {}
            for op in self.ops[engname]:
                need = {}
                for d in op.deps:
                    k, v = tok(d)
                    if need.get(k, 0) < v:
                        need[k] = v
                items = [(k, v) for k, v in need.items() if waited.get(k, 0) < v]
                attach = None
                if items and not op.is_dma:
                    attach = items.pop()
                for k, v in items:
                    eng.wait_ge(sems[k], v)
                    waited[k] = v
                ins = op.fn(eng)
                if attach is not None:
                    ins._wait_ge(sems[attach[0]], attach[1])
                    waited[attach[0]] = attach[1]
                if op.is_dma:
                    ins.then_inc(sems[("d", op.eng, op.slot)], 16)
                elif op.sig is not None:
                    ins.then_inc(sems[(op.eng, op.sig // EPOCH)], 1)
            if engname == "sp":
                for k, v in final_waits.items():
                    if waited.get(k, 0) < v:
                        eng.wait_ge(sems[k], v)

        block = stack.enter_context(nc.Block())

        @block.tensor
        def _(e):
            run("pe", e)

        @block.scalar
        def _(e):
            run("act", e)

        @block.vector
        def _(e):
            run("dve", e)

        @block.gpsimd
        def _(e):
            run("pool", e)

        @block.sync
        def _(e):
            run("sp", e)


class Tile:
    def __init__(self, st, nc, name, shape, dtype, nb=1, psum=False):
        if psum:
            self.t = st.enter_context(nc.psum_tensor(name, shape, dtype))
        else:
            self.t = st.enter_context(nc.sbuf_tensor(name, shape, dtype))
        self.b = [Buf(f"{name}{i}") for i in range(nb)]

    def __call__(self, key=None, bi=0):
        ap = self.t[:] if key is None else self.t[key]
        if bi is None:
            return V(ap, self.b)
        return V(ap, (self.b[bi],))


class DT:
    def __init__(self, nc, name, shape, dtype, kind, nb=1):
        self.t = nc.dram_tensor(name, list(shape), dtype, kind=kind)
        self.ap = self.t.ap()
        self.b = [Buf(f"{name}{i}") for i in range(nb)]

    def __call__(self, ap=None, bi=0):
        return V(self.ap if ap is None else ap, (self.b[bi],))


def _bs(*vs):
    out = []
    for v in vs:
        if isinstance(v, V):
            out.extend(v.bs)
    return out


def _a(x):
    return x.ap if isinstance(x, V) else x


class KB:
    def __init__(self, S):
        self.S = S

    def mm(self, out, lhsT, rhs, start=True, stop=True):
        o, l, r = out.ap, lhsT.ap, rhs.ap
        self.S.rec("pe", lambda e: e.matmul(o, lhsT=l, rhs=r, start=start, stop=stop), _bs(lhsT, rhs), _bs(out))

    def tr(self, out, in_, ident):
        o, i, d = out.ap, in_.ap, ident.ap
        self.S.rec("pe", lambda e: e.transpose(o, i, d), _bs(in_, ident), _bs(out))

    def act(self, out, in_, func, bias=None, scale=None):
        o, i = out.ap, in_.ap
        kw = {}
        if bias is not None:
            kw["bias"] = _a(bias)
        if scale is not None:
            kw["scale"] = _a(scale)
        self.S.rec("act", lambda e: e.activation(out=o, in_=i, func=func, **kw), _bs(in_, bias, scale), _bs(out))

    def ts(self, out, in0, s1, s2=None, op0=ALU.mult, op1=None, eng="dve"):
        o, i = out.ap, in0.ap
        a1, a2 = _a(s1), _a(s2)
        if op1 is None:
            f = lambda e: e.tensor_scalar(out=o, in0=i, scalar1=a1, scalar2=None, op0=op0)
        else:
            f = lambda e: e.tensor_scalar(out=o, in0=i, scalar1=a1, scalar2=a2, op0=op0, op1=op1)
        self.S.rec(eng, f, _bs(in0, s1, s2), _bs(out))

    def tt(self, out, in0, in1, op, eng="dve"):
        o, i0, i1 = out.ap, in0.ap, in1.ap
        self.S.rec(eng, lambda e: e.tensor_tensor(out=o, in0=i0, in1=i1, op=op), _bs(in0, in1), _bs(out))

    def stt(self, out, in0, scalar, in1, op0, op1):
        o, i0, i1, s = out.ap, in0.ap, in1.ap, _a(scalar)
        self.S.rec("dve", lambda e: e.scalar_tensor_tensor(out=o, in0=i0, scalar=s, in1=i1, op0=op0, op1=op1),
                   _bs(in0, in1, scalar), _bs(out))

    def cp(self, out, in_, eng="dve"):
        o, i = out.ap, in_.ap
        if eng == "act":
            self.S.rec("act", lambda e: e.copy(out=o, in_=i), _bs(in_), _bs(out))
        else:
            self.S.rec(eng, lambda e: e.tensor_copy(out=o, in_=i), _bs(in_), _bs(out))

    def memset(self, out, val, eng="pool"):
        o = out.ap
        self.S.rec(eng, lambda e: e.memset(o, val), [], _bs(out))

    def recip(self, out, in_):
        o, i = out.ap, in_.ap
        self.S.rec("dve", lambda e: e.reciprocal(out=o, in_=i), _bs(in_), _bs(out))

    def rsum(self, out, in_):
        o, i = out.ap, in_.ap
        self.S.rec("dve", lambda e: e.reduce_sum(out=o, in_=i, axis=AX.X), _bs(in_), _bs(out))

    def dma(self, out, in_, q="sp"):
        o, i = out.ap, in_.ap
        self.S.rec(q, lambda e: e.dma_start(out=o, in_=i), _bs(in_), _bs(out), is_dma=True)

    def gather(self, out, table, idx):
        o, t, ix = out.ap, table.ap, idx.ap
        self.S.rec("pool", lambda e: e.indirect_dma_start(out=o, out_offset=None, in_=t,
                                                          in_offset=bass.IndirectOffsetOnAxis(ap=ix, axis=0)),
                   _bs(table, idx), _bs(out), is_dma=True)

    def iota_part(self, out):
        o = out.ap
        self.S.rec("pool", lambda e: e.iota(o, pattern=[[0, 1]], base=0, channel_multiplier=1), [], _bs(out))


def in_blocks():
    o_qkv = 0
    o_b = 2304
    o_a = 2310
    o_z = 2316
    o_dq = 3084
    o_dk = 3596
    o_dv = 4108
    o_cq = 4620
    o_ck = 5004
    o_cv = 5388
    o_lr = 6156
    o_cr = 6172
    blks = [("gb", o_b, 6), ("ga", o_a, 6)]
    for h in range(6):
        blks += [(f"aq{h}", o_qkv + h * 128, 128), (f"ak{h}", o_qkv + 768 + h * 128, 128),
                 (f"av{h}", o_qkv + 1536 + h * 128, 128), (f"az{h}", o_z + h * 128, 128)]
    for h in range(4):
        blks += [(f"dq{h}", o_dq + h * 128, 128), (f"dk{h}", o_dk + h * 128, 128), (f"dv{h}", o_dv + h * 128, 128)]
    blks += [("lr", o_lr, 16)]
    for p in range(3):
        blks += [(f"cq{p}", o_cq + p * 128, 128), (f"ck{p}", o_ck + p * 128, 128)]
        for h in (2 * p, 2 * p + 1):
            blks += [(f"cv{h}", o_cv + h * 128, 128), (f"cr{h}", o_cr + h * 128, 128)]
    assert len(blks) == N_IN_BLK
    return blks


BLK = {n: i for i, (n, _, _) in enumerate(in_blocks())}


def tok_tiles(n):
    nt = n // 128
    k = -(-nt // 4)
    out = []
    t = 0
    for i in range(k):
        m = (nt - t + (k - i) - 1) // (k - i)
        out.append((t * 128, m * 128))
        t += m
    return out


def group_tiles(SEQ):
    nt = SEQ // 128
    if nt >= 16:
        a = -(-nt // 3)
        return [a, (nt - a + 1) // 2, (nt - a) // 2]
    return [nt // 2, nt - nt // 2]


class HV:
    def __init__(self, tile, c, rows=None):
        self.ap = tile.t[:, c, :] if rows is None else tile.t[rows, c, :]
        self.bs = (tile.b[c],)

    def __call__(self, key=None):
        return V(self.ap if key is None else self.ap[key], self.bs)


def build_program(SEQ, PAST, NPOOL, dbg=None):
    GT = group_tiles(SEQ)
    NG = len(GT)
    GSZ = [t * 128 for t in GT]
    GOFF = [sum(GSZ[:i]) for i in range(NG)]
    NPGM = max(GSZ)
    NTPM = max(GT)
    NPAGES = PAST // 128
    NTOK = SEQ + NS
    NTmax = NPGM + NS
    nc = bass.Bass("TRN2", target_bir_lowering=False)
    EI, EO = "ExternalInput", "ExternalOutput"
    xpT = DT(nc, "xpT", [D, SEQ], F32, EI)
    xsT = DT(nc, "xsT", [D, NS], F32, EI)
    cache_k = DT(nc, "cache_k", [DEPTH, NPOOL * 128, 512], F32, EI)
    cache_v = DT(nc, "cache_v", [DEPTH, NPOOL * 128, 512], F32, EI)
    ptab = DT(nc, "ptab", [1, NB_S * NPAGES], I32, EI)
    st_gdn = DT(nc, "st_gdn", [DEPTH, NB_S, 6, 128, 128], F32, EI)
    st_conv = DT(nc, "st_conv", [DEPTH, NB_S, 128, 18, 3], F32, EI)
    st_gla = DT(nc, "st_gla", [DEPTH, NB_S, 3, 128, 128], F32, EI)
    w_in = DT(nc, "w_in", [DEPTH, N_IN_BLK, 128, KC, 128], F32, EI)
    w_out = DT(nc, "w_out", [DEPTH, KC, 128, KC, 128], F32, EI)
    w_gu = DT(nc, "w_gu", [DEPTH, 2 * FC, 128, KC, 128], F32, EI)
    w_dn = DT(nc, "w_dn", [DEPTH, NQ, KC, 128, FQ, 128], F32, EI)
    convw = DT(nc, "convw", [DEPTH, 128, 18, 4], F32, EI)
    NPC = 8 + 4 * KC
    pcol = DT(nc, "pcol", [DEPTH, 128, NPC], F32, EI)
    lamv = DT(nc, "lamv", [1, DEPTH * 256], F32, EI)
    wa2b = DT(nc, "wa2b", [DEPTH, 32, 384], F32, EI)
    consts = DT(nc, "consts", [128, 6, 128], F32, EI)
    ropec = DT(nc, "ropec", [2, 128, NTOK], F32, EI)
    yT = DT(nc, "yT", [D, NTOK], F32, EO)
    newk = DT(nc, "newk", [DEPTH, 4, 128, NTOK], F32, EO, nb=DEPTH * 4)
    newv = DT(nc, "newv", [DEPTH, NTOK, 512], F32, EO, nb=DEPTH * 4)
    o_gdn = DT(nc, "o_gdn", [DEPTH, 1 + NB_S, 6, 128, 128], F32, EO)
    o_conv = DT(nc, "o_conv", [DEPTH, 1 + NB_S, 128, 18, 3], F32, EO)
    o_gla = DT(nc, "o_gla", [DEPTH, 1 + NB_S, 3, 128, 128], F32, EO)
    dbg_out = None
    if dbg is not None:
        dbg_out = DT(nc, "dbg", list(dbg["shape"]), F32, EO)

    with ExitStack() as st:
        S = Sched(nc)
        K = KB(S)

        def T(name, shape, dt=F32, nb=1):
            return Tile(st, nc, name, shape, dt, nb=nb)

        A_ = slice(None)
        xT16 = T("xT16", [128, KC, NTmax], BF16, nb=KC)
        mixT = T("mixT", [128, KC, NTmax], BF16, nb=KC)
        hpre = T("hpre", [128, KC, NTmax], F32, nb=KC)
        NWS = 4
        wring = [T(f"wr{i}", [128, KC, 128], BF16) for i in range(NWS)]
        wctr = [0]
        C = T("constsb", [128, 6, 128], F32)
        ident, Utri, mSU, ones32, perm = (C((A_, i, A_)) for i in range(5))
        C16 = T("c16", [128, 2, 128], BF16)
        ones16 = C16((A_, 0, A_))
        Utri16 = C16((A_, 1, A_))
        rope = T("rope", [128, 2, NTmax], F32)
        pc = T("pc", [128, DEPTH, NPC], F32)
        cw = T("cw", [128, DEPTH, 18, 4], F32)
        lam = T("lam", [128, DEPTH, 4], F32)
        lv = T("lv", [128, DEPTH * 256], F32)
        cb = T("cb", [128, 4], F32)
        wab = T("wab", [32, DEPTH, 384], F32)
        Sg = T("Sg", [128, DEPTH, 6, 128], F32, nb=DEPTH * 6)
        Sl = T("Sl", [128, DEPTH, 3, 128], F32, nb=DEPTH * 3)
        ctail = T("ctail", [128, DEPTH, 18, 3], F32, nb=DEPTH * 18)
        KTh = T("KTh", [128, SEQ], BF16)
        Vh = T("Vh", [128, SEQ // 128, 128], BF16)
        IXi = T("IXi", [128, NB_S * NPAGES], I32)
        IXl = [IXi] + [T(f"IXi{l}", [128, NB_S * NPAGES], I32) for l in range(1, DEPTH)]
        IXf = T("IXf", [128, NB_S * NPAGES], F32)
        iot = T("iot", [128, 2], F32)
        ioti = T("ioti", [128, 1], I32)
        PS = [Tile(st, nc, f"ps{i}", [128, 512], F32, nb=1, psum=True) for i in range(8)]
        dctr = [0]

        def dps():
            p = PS[dctr[0] % 2]
            dctr[0] += 1
            return p

        mctr = [0]

        def mps(rows=128, cols=128):
            i = mctr[0] % 6
            mctr[0] += 1
            p = PS[2 + i]
            return V(p.t[0:rows, 0:cols], p.b)

        K.dma(C(), consts())
        K.dma(pc(), V(pcol.ap.rearrange("l p n -> p l n"), pcol.b))
        K.dma(cw(), V(convw.ap.rearrange("l p c j -> p l c j"), convw.b))
        K.dma(lv(), V(lamv.ap.partition_broadcast(128), lamv.b))
        K.dma(wab(), V(wa2b.ap.rearrange("l p n -> p l n"), wa2b.b))
        K.dma(IXi(), V(ptab.ap.partition_broadcast(128), ptab.b))
        K.cp(C16((A_, 0, A_)), ones32)
        K.cp(C16((A_, 1, A_)), Utri)
        for i, v in enumerate((1e-6, 1e-5, 1.0, 0.0)):
            K.memset(cb((A_, slice(i, i + 1))), v)
        eps6, eps5, one_c, zero_c = (cb((A_, slice(i, i + 1))) for i in range(4))
        K.memset(Sg(None, None), 0.0)
        K.memset(Sl(None, None), 0.0)
        K.memset(ctail(None, None), 0.0)
        K.iota_part(ioti())
        K.cp(iot((A_, slice(0, 1))), ioti())
        K.cp(IXf(), IXi())
        K.ts(IXf(), IXf(), 128.0, iot((A_, slice(0, 1))), ALU.mult, ALU.add)
        K.cp(IXi(), IXf())
        for l in range(1, DEPTH):
            K.ts(IXf(), IXf(), float(NPOOL * 128), None, ALU.add)
            K.cp(IXl[l](), IXf())
        lam_init = [0.8 - 0.6 * math.exp(-0.3 * li) for li in range(DEPTH)]
        ltmp = T("ltmp", [128, 64], F32)
        for li in range(DEPTH):
            for j in range(2):
                o = li * 256 + j * 128
                K.tt(ltmp(), lv((A_, slice(o, o + 64))), lv((A_, slice(o + 64, o + 128))), ALU.mult)
                K.rsum(lam((A_, li, slice(2 + j, 3 + j))), ltmp())
            K.act(lam((A_, li, slice(2, 4))), lam((A_, li, slice(2, 4))), AF.Exp)
            K.tt(lam((A_, li, slice(0, 1))), lam((A_, li, slice(2, 3))), lam((A_, li, slice(3, 4))), ALU.subtract)
            K.ts(lam((A_, li, slice(0, 1))), lam((A_, li, slice(0, 1))), lam_init[li], None, ALU.add)
            K.ts(lam((A_, li, slice(1, 2))), lam((A_, li, slice(0, 1))), -1.0, None, ALU.mult)
            K.act(pc((slice(0, 6), li, slice(0, 1))), pc((slice(0, 6), li, slice(0, 1))), AF.Exp)
            K.ts(pc((slice(0, 6), li, slice(0, 1))), pc((slice(0, 6), li, slice(0, 1))), -1.0, None, ALU.mult)

        def pcs(li, col, rows=128):
            return pc((slice(0, rows), li, slice(col, col + 1)))

        def wload(src_ap, src_v, kc):
            w = wring[wctr[0] % NWS]
            wctr[0] += 1
            K.dma(w((A_, slice(0, kc), A_)), V(src_ap, src_v.b), q="pool")
            return w

        def dense(w, kc, rhs_tile, ttiles, evac, M=128):
            for (t0, n) in ttiles:
                p = dps()
                pv = V(p.t[0:M, 0:n], p.b)
                for c in range(kc):
                    K.mm(pv, w((A_, c, slice(0, M))), rhs_tile((A_, c, slice(t0, t0 + n)), c),
                         start=(c == 0), stop=(c == kc - 1))
                evac(t0, n, pv)

        qT, kT, vT, zg, oTh, sq = (HV(hpre, c) for c in range(6))
        raw = [HV(hpre, 6), HV(hpre, 7)]
        vTb, zgb, oThb = HV(hpre, 8), HV(hpre, 9), HV(hpre, 10)
        gbT = HV(hpre, 11, slice(0, 32))
        ggT = HV(hpre, 12, slice(0, 32))
        clr1 = HV(hpre, 6, slice(0, 32))
        at0, at1 = HV(hpre, 12), HV(hpre, 13)
        rz0, rz1 = HV(hpre, 14), HV(hpre, 15)
        raws = [T(f"raws{i}", [128, NB_S, 3 + TS], F32) for i in range(2)]
        rctr = [0]
        rs = T("rs", [128, 512], F32)
        NTL = NTPM + NB_S
        btok = T("btok", [128, NTL, 6], F32)
        nbtok = T("nbtok", [128, NTL, 6], F32)
        gtok = T("gtok", [128, NTL, 6], F32)
        gctok = T("gctok", [128, NTL, 6], F32)
        NSL = 28
        smA = T("smA", [128, NSL, 128], F32, nb=NSL)

        class SV:
            def __init__(self, s0, n=1):
                self.ap = smA.t[:, s0, :] if n == 1 else smA.t[:, s0:s0 + n, :].rearrange("p a b -> p (a b)")
                self.bs = tuple(smA.b[s0:s0 + n])

            def __call__(self, key=None):
                return V(self.ap if key is None else self.ap[key], self.bs)

        names = ("ktok", "vtok", "R", "PT", "qTe", "kTe", "kw", "EG", "E", "ESU", "X", "XT", "X2", "XT2")
        gsm = [{n: SV(par * 14 + i) for i, n in enumerate(names)} for par in range(2)]
        t_r = [T(f"r{i}", [128, 128], F32) for i in range(2)]
        t_u = [T(f"u{i}", [128, 128], F32) for i in range(2)]
        t_Ug = [T(f"Ug{i}", [128, 128], F32) for i in range(2)]
        t_arg = [T(f"arg{i}", [128, 128], F32) for i in range(2)]
        t_gl = [T(f"gl{i}", [128, 2], F32) for i in range(2)]
        sst = [T(f"sst{i}", [128, 128], F32) for i in range(2)]
        sctr = [0]

        def rstd_from(pv, n, eps_v, scale):
            K.act(rs((A_, slice(0, n))), pv, AF.Ln, bias=eps_v, scale=scale)
            K.act(rs((A_, slice(0, n))), rs((A_, slice(0, n))), AF.Exp, scale=-0.5)

        def rmsnorm_gate(o_t, gate_t, gcol_v, ttiles, head, const_gate=None):
            for (t0, n) in ttiles:
                cs = slice(t0, t0 + n)
                K.act(sq((A_, cs)), o_t((A_, cs)), AF.Square)
                pp = dps()
                p = V(pp.t[:, 0:n], pp.b)
                K.mm(p, ones32, sq((A_, cs)))
                rstd_from(p, n, eps6, 1.0 / 128)
                K.tt(rs((A_, slice(0, n))), rs((A_, slice(0, n))), o_t((A_, cs)), ALU.mult)
                dst = mixT((A_, head, cs), head)
                if const_gate is None:
                    K.stt(dst, rs((A_, slice(0, n))), gcol_v, gate_t((A_, cs)), ALU.mult, ALU.mult)
                else:
                    K.ts(dst, rs((A_, slice(0, n))), gcol_v, const_gate, ALU.mult, ALU.mult)

        def l2norm(x_t, ttiles, scale):
            for (t0, n) in ttiles:
                cs = slice(t0, t0 + n)
                K.act(sq((A_, cs)), x_t((A_, cs)), AF.Square)
                pp = dps()
                p = V(pp.t[:, 0:n], pp.b)
                K.mm(p, ones32, sq((A_, cs)))
                rstd_from(p, n, eps6, 1.0)
                K.stt(x_t((A_, cs)), x_t((A_, cs)), scale, rs((A_, slice(0, n))), ALU.mult, ALU.mult)

        def sub(v, rows, cols=None):
            return V(v.ap[rows, :] if cols is None else v.ap[rows, cols], v.bs)

        def gdn_chunk(li, h, col0, Cn, gi, Sv, par):
            cs = slice(col0, col0 + Cn)
            g_ = gsm[par]
            ktok, vtok, R, PT, qTe, kTe, kw, EG, E, ESU, X, XT, X2, XT2 = (g_[n] for n in names)
            r_, u_, Ug, arg, gl = t_r[par], t_u[par], t_Ug[par], t_arg[par], t_gl[par]
            rC = slice(0, Cn)
            gcol = gtok((rC, gi, slice(h, h + 1)))
            gccol = gctok((rC, gi, slice(h, h + 1)))
            p = mps(Cn, 128)
            K.tr(p, kT((A_, cs)), ident)
            yield None
            K.cp(ktok((rC, A_)), p, "act")
            yield None
            p = mps(Cn, 128)
            K.tr(p, vT((A_, cs)), ident)
            yield None
            K.cp(vtok((rC, A_)), p, "act")
            yield None
            K.ts(Ug((rC, rC)), sub(Utri, rC, rC), gcol, None, ALU.mult, eng="pool")
            yield None
            pg = mps(128, Cn)
            K.mm(pg, sub(ones32, rC), Ug((rC, rC)))
            yield None
            K.ts(arg((rC, rC)), sub(pg, rC), gccol, 0.0, ALU.subtract, ALU.min)
            yield None
            K.act(E((rC, rC)), arg((rC, rC)), AF.Exp)
            yield None
            K.act(EG((A_, rC)), pg, AF.Exp)
            yield None
            K.cp(gl((A_, slice(0, 1))), sub(pg, A_, slice(Cn - 1, Cn)))
            yield None
            K.tt(ESU((rC, rC)), E((rC, rC)), sub(mSU, rC, rC), ALU.mult, eng="pool")
            yield None
            K.tt(E((rC, rC)), E((rC, rC)), sub(Utri, rC, rC), ALU.mult, eng="pool")
            yield None
            pk = mps(Cn, Cn)
            K.mm(pk, kT((A_, cs)), kT((A_, cs)))
            yield None
            K.stt(X((rC, rC)), pk, nbtok((rC, gi, slice(h, h + 1))), ESU((rC, rC)), ALU.mult, ALU.mult)
            yield None
            pq = mps(Cn, Cn)
            K.mm(pq, kT((A_, cs)), qT((A_, cs)))
            yield None
            K.tt(PT((rC, rC)), pq, E((rC, rC)), ALU.mult)
            yield None
            K.tt(qTe((A_, rC)), qT((A_, cs)), EG((A_, rC)), ALU.mult, eng="pool")
            yield None
            K.tt(kTe((A_, rC)), kT((A_, cs)), EG((A_, rC)), ALU.mult, eng="pool")
            yield None
            K.act(gl((rC, slice(1, 2))), gccol, AF.Exp, bias=gl((rC, slice(0, 1))), scale=-1.0)
            yield None
            K.ts(kw((rC, A_)), ktok((rC, A_)), gl((rC, slice(1, 2))), None, ALU.mult, eng="pool")
            yield None
            p = mps(Cn, Cn)
            K.tr(p, X((rC, rC)), sub(ident, rC, rC))
            yield None
            K.cp(XT((rC, rC)), p, "act")
            yield None
            K.tt(R((rC, rC)), X((rC, rC)), sub(ident, rC, rC), ALU.add, eng="pool")
            yield None
            nst = max(1, int(math.ceil(math.log2(Cn))) - 1)
            Xc, XTc, Xn, XTn = X, XT, X2, XT2
            for k in range(nst):
                p1 = mps(Cn, Cn)
                K.mm(p1, Xc((rC, rC)), XTc((rC, rC)))
                yield None
                K.cp(XTn((rC, rC)), p1, "act")
                yield None
                if k < nst - 1:
                    p2 = mps(Cn, Cn)
                    K.mm(p2, XTc((rC, rC)), Xc((rC, rC)))
                    yield None
                    K.cp(Xn((rC, rC)), p2, "dve")
                    yield None
                p3 = mps(Cn, Cn)
                K.mm(p3, XTn((rC, rC)), R((rC, rC)))
                yield None
                K.tt(R((rC, rC)), R((rC, rC)), p3, ALU.add)
                yield None
                Xc, XTc, Xn, XTn = Xn, XTn, Xc, XTc
            yield "SEQ"
            p = mps(Cn, 128)
            K.mm(p, kTe((A_, rC)), Sv)
            K.tt(r_((rC, A_)), vtok((rC, A_)), p, ALU.subtract)
            p = mps(Cn, 128)
            K.mm(p, R((rC, rC)), r_((rC, A_)))
            K.ts(u_((rC, A_)), p, btok((rC, gi, slice(h, h + 1))), None, ALU.mult)
            p = mps(128, Cn)
            K.mm(p, Sv, qTe((A_, rC)), start=True, stop=False)
            K.mm(p, u_((rC, A_)), PT((rC, rC)), start=False, stop=True)
            K.cp(oTh((A_, cs)), p, "act")
            p = mps(128, 128)
            K.mm(p, kw((rC, A_)), u_((rC, A_)))
            K.stt(Sv, Sv, EG((A_, slice(Cn - 1, Cn))), p, ALU.mult, ALU.add)

        def run_gens(gens):
            active = list(gens)
            while active:
                for gen in list(active):
                    if next(gen) == "SEQ":
                        active.remove(gen)
            for gen in gens:
                for _ in gen:
                    pass

        lnames = ("ebc", "enbc", "qtl", "ktl", "ktlt", "vtokA", "PTl")
        lsm = [{n: SV(par * 14 + i) for i, n in enumerate(lnames)} for par in range(2)]
        alogt = [T(f"alogt{i}", [128, 384], F32) for i in range(2)]

        def gla_chunk(li, pr, col0, Cn, Sp, par, vts, ots):
            cs = slice(col0, col0 + Cn)
            rC = slice(0, Cn)
            L = lsm[par]
            ebc, enbc, qtl, ktl, ktlt, vtokA, PTl = (L[n] for n in lnames)
            al = alogt[par]
            pz = dps()
            pzv = V(pz.t[0:Cn, 0:384], pz.b)
            K.mm(pzv, clr1((slice(0, 32), cs)), wab((A_, li, A_)))
            K.act(al((rC, A_)), pzv, AF.Exp, scale=-1.0)
            K.act(al((rC, A_)), al((rC, A_)), AF.Ln, bias=sub(one_c, rC), scale=1.0)
            pb = mps(128, Cn)
            K.mm(pb, al((rC, slice(pr * 128, (pr + 1) * 128))), sub(Utri, rC, rC))
            K.act(ebc((A_, rC)), pb, AF.Exp, scale=-1.0 / 16)
            K.act(enbc((A_, rC)), pb, AF.Exp, scale=1.0 / 16)
            K.tt(qtl((A_, rC)), qT((A_, cs)), ebc((A_, rC)), ALU.mult, eng="pool")
            K.tt(ktl((A_, rC)), kT((A_, cs)), enbc((A_, rC)), ALU.mult, eng="pool")
            p = mps(Cn, 128)
            K.tr(p, ktl((A_, rC)), ident)
            K.cp(ktlt((rC, A_)), p, "act")
            for hh in range(2):
                r64 = slice(hh * 64, hh * 64 + 64)
                p = mps(Cn, 128)
                K.tr(p, vts[hh]((A_, cs)), ident)
                K.cp(vtokA((rC, A_)), p, "act")
                pa = mps(Cn, Cn)
                K.mm(pa, ktl((r64, rC)), qtl((r64, rC)))
                K.tt(PTl((rC, rC)), pa, sub(Utri, rC, rC), ALU.mult)
                po = mps(128, Cn)
                K.mm(po, sub(Sp, r64), qtl((r64, rC)), start=True, stop=False)
                K.mm(po, vtokA((rC, A_)), PTl((rC, rC)), start=False, stop=True)
                K.cp(ots[hh]((A_, cs)), po, "act")
                pf = PS[2 + mctr[0] % 6]
                mctr[0] += 1
                psn = V(pf.t[r64, 0:128], pf.b)
                K.mm(psn, ktlt((rC, r64)), vtokA((rC, A_)))
                K.tt(sub(Sp, r64), sub(Sp, r64), psn, ALU.add)
                K.ts(sub(Sp, r64), sub(Sp, r64), ebc((r64, slice(Cn - 1, Cn))), None, ALU.mult)

        q16 = T("q16", [128, NTmax], BF16)
        eT16 = [T(f"eT16_{i}", [128, 512], BF16) for i in range(2)]
        ks16 = T("ks16", [128, 4, NS], BF16)
        vs32 = T("vs32", [128, 4, NS], F32)
        qblk = T("qblk", [128, NB_S, 4, 16], BF16)
        osamp = T("osamp", [128, 4, NS], F32)
        kpage = [SV(0, 4), SV(4, 4)]
        vpage = [SV(8, 4), SV(12, 4)]
        v16p = [T(f"v16p{i}", [128, 512], BF16) for i in range(2)]
        ktp16 = [T(f"ktp16_{i}", [128, 512], BF16) for i in range(2)]
        eTp16 = [T(f"eTp16_{i}", [128, 64], BF16) for i in range(2)]
        vsn16 = T("vsn16", [8, 4, 128], BF16)
        m8 = T("m8", [8, 64], BF16)
        zs = T("zs", [128, 64], F32)
        for j in range(8):
            K.cp(m8((A_, slice(j * 8, j * 8 + 8))), sub(Utri, slice(0, 8), slice(0, 8)))
        vtokp = [T(f"vtokp{i}", [128, 128], F32) for i in range(2)]

        def attn_prompt(li, h, g, ptiles):
            for (t0, n) in ptiles:
                q0 = GOFF[g] + t0
                nkt = (q0 + n) // 128
                acc = [PS[4], PS[5], PS[6], PS[7]]
                for kt in range(nkt):
                    kabs = kt * 128
                    qoff = max(0, kabs - q0)
                    nq = n - qoff
                    diag = kabs + 128 > q0
                    for m in range(2):
                        r64 = slice(m * 64, m * 64 + 64)
                        pst = PS[2 + (kt * 2 + m) % 2]
                        sv = V(pst.t[:, 0:nq], pst.b)
                        K.mm(sv, KTh((r64, slice(kabs, kabs + 128))), q16((r64, slice(t0 + qoff, t0 + n))))
                        e = eT16[(kt * 2 + m) % 2]
                        K.act(e((A_, slice(0, nq))), sv, AF.Exp)
                        if diag:
                            K.tt(e((A_, slice(0, 128))), e((A_, slice(0, 128))), Utri16, ALU.mult, eng="pool")
                        ov = V(acc[m].t[:, qoff:n], acc[m].b)
                        zv = V(acc[2 + m].t[:, qoff:n], acc[2 + m].b)
                        K.mm(ov, Vh((A_, kt, A_)), e((A_, slice(0, nq))), start=(kt == 0), stop=(kt == nkt - 1))
                        K.mm(zv, ones16, e((A_, slice(0, nq))), start=(kt == 0), stop=(kt == nkt - 1))
                cs = slice(t0, t0 + n)
                nn = slice(0, n)
                K.recip(rz0((A_, nn)), V(acc[2].t[:, 0:n], acc[2].b))
                K.recip(rz1((A_, nn)), V(acc[3].t[:, 0:n], acc[3].b))
                K.tt(at0((A_, nn)), V(acc[0].t[:, 0:n], acc[0].b), rz0((A_, nn)), ALU.mult)
                K.tt(at1((A_, nn)), V(acc[1].t[:, 0:n], acc[1].b), rz1((A_, nn)), ALU.mult)
                K.stt(oTh((A_, cs)), at1((A_, nn)), lam((A_, li, slice(1, 2))), at0((A_, nn)), ALU.mult, ALU.add)

        def attn_sample(li, NPG):
            for b in range(NB_S):
                oacc, zacc = PS[6], PS[7]
                ov = V(oacc.t[:, 0:64], oacc.b)
                zv = V(zacc.t[:, 0:64], zacc.b)
                for h in range(4):
                    p = V(PS[2].t[0:8, 0:128], PS[2].b)
                    K.tr(p, vs32((A_, h, slice(b * TS, (b + 1) * TS))), ident)
                    K.cp(vsn16((A_, h, A_)), p, "act")
                for n in range(NPAGES):
                    par = n % 2
                    col = b * NPAGES + n
                    K.gather(kpage[par](), V(cache_k.ap.rearrange("l n c -> (l n) c"), cache_k.b), IXl[li]((A_, slice(col, col + 1))))
                    K.gather(vpage[par](), V(cache_v.ap.rearrange("l n c -> (l n) c"), cache_v.b), IXl[li]((A_, slice(col, col + 1))))
                    K.cp(v16p[par](), vpage[par](), "pool")
                    pt_ = PS[2 + par]
                    for h in range(4):
                        K.tr(V(pt_.t[:, h * 128:(h + 1) * 128], pt_.b), kpage[par]((A_, slice(h * 128, (h + 1) * 128))), ident)
                    K.cp(ktp16[par](), V(pt_.t[:, :], pt_.b), "act" if par == 0 else "dve")
                    ps_ = PS[4 + par]
                    for h in range(4):
                        K.mm(V(ps_.t[:, h * 16:(h + 1) * 16], ps_.b), ktp16[par]((A_, slice(h * 128, (h + 1) * 128))), qblk((A_, b, h, A_)))
                    K.act(eTp16[par](), V(ps_.t[:, 0:64], ps_.b), AF.Exp)
                    for h in range(4):
                        K.mm(V(oacc.t[:, h * 16:(h + 1) * 16], oacc.b), v16p[par]((A_, slice(h * 128, (h + 1) * 128))),
                             eTp16[par]((A_, slice(h * 16, (h + 1) * 16))), start=(n == 0 and h == 0), stop=False)
                    K.mm(zv, ones16, eTp16[par](), start=(n == 0), stop=False)
                bs_ = slice(b * TS, (b + 1) * TS)
                ps_ = PS[4]
                for h in range(4):
                    K.mm(V(ps_.t[0:8, h * 16:(h + 1) * 16], ps_.b), ks16((A_, h, bs_)), qblk((A_, b, h, A_)))
                e8 = eTp16[0]
                K.act(e8((slice(0, 8), A_)), V(ps_.t[0:8, 0:64], ps_.b), AF.Exp)
                K.tt(e8((slice(0, 8), A_)), e8((slice(0, 8), A_)), m8(), ALU.mult)
                for h in range(4):
                    K.mm(V(oacc.t[:, h * 16:(h + 1) * 16], oacc.b), vsn16((A_, h, A_)), e8((slice(0, 8), slice(h * 16, (h + 1) * 16))),
                         start=False, stop=(h == 3))
                K.mm(zv, sub(ones16, slice(0, 8)), e8((slice(0, 8), A_)), start=False, stop=True)
                K.recip(zs(), zv)
                K.tt(zs(), zs(), ov, ALU.mult)
                for h in range(4):
                    K.stt(osamp((A_, h, bs_)), zs((A_, slice(h * 16 + 8, h * 16 + 16))), lam((A_, li, slice(1, 2))),
                          zs((A_, slice(h * 16, h * 16 + 8))), ALU.mult, ALU.add)

        lnm = T("lnm", [128, 512], F32)
        lnr = T("lnr", [128, 512], F32)
        lnt = [T(f"lnt{i}", [128, 512], F32) for i in range(2)]
        sgt = [T(f"sgt{i}", [128, 512], F32) for i in range(2)]

        def layernorm(li, gcol0, bcol0, ttiles, to_y, cols_abs):
            for (t0, n) in ttiles:
                cs = slice(t0, t0 + n)
                nn = slice(0, n)
                pa, pb = PS[2], PS[3]
                pav, pbv = V(pa.t[:, 0:n], pa.b), V(pb.t[:, 0:n], pb.b)
                for c in range(KC):
                    K.mm(pav, ones32, hpre((A_, c, cs), c), start=(c == 0), stop=(c == KC - 1))
                for c in range(KC):
                    t = lnt[c % 2]
                    K.act(t((A_, nn)), hpre((A_, c, cs), c), AF.Square)
                    K.mm(pbv, ones32, t((A_, nn)), start=(c == 0), stop=(c == KC - 1))
                K.ts(lnm((A_, nn)), pav, 1.0 / D, None, ALU.mult)
                K.tt(lnr((A_, nn)), lnm((A_, nn)), lnm((A_, nn)), ALU.mult)
                K.stt(lnr((A_, nn)), pbv, 1.0 / D, lnr((A_, nn)), ALU.mult, ALU.subtract)
                K.act(lnr((A_, nn)), lnr((A_, nn)), AF.Ln, bias=eps5, scale=1.0)
                K.act(lnr((A_, nn)), lnr((A_, nn)), AF.Exp, scale=-0.5)
                for c in range(KC):
                    hv = hpre((A_, c, cs), c)
                    K.tt(hv, hv, lnm((A_, nn)), ALU.subtract)
                    K.tt(hv, hv, lnr((A_, nn)), ALU.mult, eng="pool")
                    K.ts(hv, hv, pcs(li, gcol0 + c), pcs(li, bcol0 + c), ALU.mult, ALU.add)
                    K.cp(xT16((A_, c, cs), c), hv, "pool")
                    if to_y:
                        K.dma(V(yT.ap[c * 128:(c + 1) * 128, cols_abs(t0, n)], yT.b), hv)

        def run_layer(g, li):
            NPG = GSZ[g]
            NTP = GT[g]
            has_s = g == 0
            last = g == NG - 1
            NT = NPG + (NS if has_s else 0)
            ptiles = tok_tiles(NPG)
            stile = [(NPG, NS)] if has_s else []
            ttiles = ptiles + stile
            dtiles = list(ttiles)
            if has_s and ptiles[-1][1] + NS <= 512:
                dtiles = ptiles[:-1] + [(ptiles[-1][0], ptiles[-1][1] + NS)]
            a0 = GOFF[g]

            def cols_abs(t0, n):
                return slice(a0 + t0, a0 + t0 + n) if t0 < NPG else slice(SEQ, SEQ + NS)

            def rcols(t0, n):
                return slice(t0, t0 + n)

            def wblk(name):
                return wload(w_in.ap[li, BLK[name]], w_in, KC)

            def ev_copy(dst, scale=None):
                def f(t0, n, pv):
                    if scale is None:
                        K.cp(dst((A_, slice(t0, t0 + n))), pv, "act")
                    else:
                        K.act(dst((A_, slice(t0, t0 + n))), pv, AF.Copy, scale=scale)
                return f

            chunks = [(t * 128, 128, t) for t in range(NTP)]
            schunks = [(NPG + b * TS, TS, NTP + b) for b in range(NB_S)] if has_s else []

            w = wblk("gb")
            dense(w, KC, xT16, dtiles, lambda t0, n, pv: K.act(gbT((slice(0, 6), slice(t0, t0 + n))), pv, AF.Sigmoid), M=6)
            w = wblk("ga")
            dense(w, KC, xT16, dtiles,
                  lambda t0, n, pv: K.act(ggT((slice(0, 6), slice(t0, t0 + n))), pv, AF.Exp, bias=pcs(li, 1, 6), scale=1.0), M=6)
            gg6 = ggT((slice(0, 6), slice(0, NT)))
            K.act(gg6, gg6, AF.Ln, bias=sub(one_c, slice(0, 6)), scale=1.0)
            K.ts(gg6, gg6, pcs(li, 0, 6), None, ALU.mult)
            for (c0, Cn, gi) in chunks + schunks:
                rC = slice(0, Cn)
                p = mps(Cn, 6)
                K.tr(p, gbT((slice(0, 6), slice(c0, c0 + Cn))), sub(ident, slice(0, 6), slice(0, 6)))
                K.cp(btok((rC, gi, A_)), p, "act")
                K.ts(nbtok((rC, gi, A_)), p, -1.0, None, ALU.mult)
                p = mps(Cn, 6)
                K.tr(p, ggT((slice(0, 6), slice(c0, c0 + Cn))), sub(ident, slice(0, 6), slice(0, 6)))
                K.cp(gtok((rC, gi, A_)), p, "act")
                p = mps(Cn, 6)
                K.mm(p, sub(Utri, rC, rC), gtok((rC, gi, A_)))
                K.cp(gctok((rC, gi, A_)), p, "dve")

            for h in range(6):
                for (nm, dst, ci) in (("aq", qT, h), ("ak", kT, 6 + h), ("av", vT, 12 + h)):
                    rw = raw[rctr[0] % 2]
                    rws = raws[rctr[0] % 2]
                    rctr[0] += 1
                    K.cp(rw((A_, slice(0, 3))), ctail((A_, li, ci, A_), li * 18 + ci), "pool")
                    if has_s:
                        K.dma(rws((A_, A_, slice(0, 3))), V(st_conv.ap[li, :, :, ci, :].rearrange("b p j -> p b j"), st_conv.b))
                    w = wblk(f"{nm}{h}")

                    def ev(t0, n, pv, rw=rw, rws=rws):
                        npr = max(0, min(n, NPG - t0))
                        if npr > 0:
                            K.cp(rw((A_, slice(3 + t0, 3 + t0 + npr))), V(pv.ap[:, 0:npr], pv.bs), "act")
                        if npr < n:
                            K.cp(rws((A_, A_, slice(3, 3 + TS))), V(pv.ap[:, npr:n].rearrange("p (b t) -> p b t", b=NB_S), pv.bs), "act")
                    dense(w, KC, xT16, dtiles, ev)
                    K.cp(ctail((A_, li, ci, A_), li * 18 + ci), rw((A_, slice(NPG, NPG + 3))), "pool")
                    if last:
                        K.dma(V(o_conv.ap[li, 0, :, ci, :], o_conv.b), rw((A_, slice(NPG, NPG + 3))))
                    if has_s:
                        K.dma(V(o_conv.ap[li, 1:1 + NB_S, :, ci, :].rearrange("b p j -> p b j"), o_conv.b), rws((A_, A_, slice(TS, TS + 3))))
                    for j in range(4):
                        wj = cw((A_, li, ci, slice(j, j + 1)))
                        if j == 0:
                            K.ts(dst((A_, slice(0, NPG))), rw((A_, slice(0, NPG))), wj, None, ALU.mult)
                        else:
                            K.stt(dst((A_, slice(0, NPG))), rw((A_, slice(j, j + NPG))), wj, dst((A_, slice(0, NPG))), ALU.mult, ALU.add)
                    if has_s:
                        dsv = V(dst.ap[:, NPG:NPG + NS].rearrange("p (b t) -> p b t", b=NB_S), dst.bs)
                        for j in range(4):
                            wj = cw((A_, li, ci, slice(j, j + 1)))
                            if j == 0:
                                K.ts(dsv, rws((A_, A_, slice(0, TS))), wj, None, ALU.mult)
                            else:
                                K.stt(dsv, rws((A_, A_, slice(j, j + TS))), wj, dsv, ALU.mult, ALU.add)
                    K.act(dst((A_, slice(0, NT))), dst((A_, slice(0, NT))), AF.Silu)
                w = wblk(f"az{h}")
                dense(w, KC, xT16, dtiles, lambda t0, n, pv: K.act(zg((A_, slice(t0, t0 + n))), pv, AF.Silu))
                l2norm(qT, dtiles, 128 ** -0.5)
                l2norm(kT, dtiles, 1.0)
                Sv = Sg((A_, li, h, A_), li * 6 + h)
                for i0 in range(0, len(chunks), 2):
                    run_gens([gdn_chunk(li, h, c0, Cn, gi, Sv, j) for j, (c0, Cn, gi) in enumerate(chunks[i0:i0 + 2])])
                if last:
                    K.dma(V(o_gdn.ap[li, 0, h], o_gdn.b), Sv)
                for b0 in range(0, len(schunks), 2):
                    gens = []
                    for j, (c0, Cn, gi) in enumerate(schunks[b0:b0 + 2]):
                        K.dma(sst[j](), V(st_gdn.ap[li, b0 + j, h], st_gdn.b))
                        gens.append(gdn_chunk(li, h, c0, Cn, gi, sst[j](), j))
                    run_gens(gens)
                    for j in range(len(gens)):
                        K.dma(V(o_gdn.ap[li, 1 + b0 + j, h], o_gdn.b), sst[j]())
                rmsnorm_gate(oTh, zg, pcs(li, 2), dtiles, h)
            if dbg is not None and dbg.get("stage") == "gdn":
                return "gdn"

            if has_s:
                K.memset(qblk(), 0.0)
            for h in range(4):
                dense(wblk(f"dq{h}"), KC, xT16, dtiles, ev_copy(qT, 0.125))
                dense(wblk(f"dk{h}"), KC, xT16, dtiles, ev_copy(kT))
                dense(wblk(f"dv{h}"), KC, xT16, dtiles, ev_copy(vT))
                for x_t in (qT, kT):
                    for (t0, n) in dtiles:
                        cs = slice(t0, t0 + n)
                        pp = dps()
                        pv = V(pp.t[:, 0:n], pp.b)
                        K.mm(pv, perm, x_t((A_, cs)))
                        K.tt(sq((A_, cs)), pv, rope((A_, 1, cs)), ALU.mult)
                        K.tt(x_t((A_, cs)), x_t((A_, cs)), rope((A_, 0, cs)), ALU.mult, eng="pool")
                        K.tt(x_t((A_, cs)), x_t((A_, cs)), sq((A_, cs)), ALU.add)
                K.dma(V(newk.ap[li, h, :, a0:a0 + NPG], (newk.b[li * 4 + h],)), kT((A_, slice(0, NPG))))
                if has_s:
                    K.dma(V(newk.ap[li, h, :, SEQ:SEQ + NS], (newk.b[li * 4 + h],)), kT((A_, slice(NPG, NT))))
                    K.cp(ks16((A_, h, A_)), kT((A_, slice(NPG, NT))), "pool")
                    K.cp(vs32((A_, h, A_)), vT((A_, slice(NPG, NT))), "pool")
                    for b in range(NB_S):
                        K.cp(qblk((slice(0, 64), b, h, slice(0, 8))), qT((slice(0, 64), slice(NPG + b * TS, NPG + (b + 1) * TS))), "pool")
                        K.cp(qblk((slice(64, 128), b, h, slice(8, 16))), qT((slice(64, 128), slice(NPG + b * TS, NPG + (b + 1) * TS))), "pool")
                K.cp(q16((A_, slice(0, NPG))), qT((A_, slice(0, NPG))), "pool")
                if a0 > 0:
                    K.dma(KTh((A_, slice(0, a0))), V(newk.ap[li, h, :, 0:a0], (newk.b[li * 4 + h],)), q="pool")
                    K.dma(Vh((A_, slice(0, a0 // 128), A_)),
                          V(newv.ap[li, 0:a0, h * 128:(h + 1) * 128].rearrange("(t p) c -> p t c", p=128), (newv.b[li * 4 + h],)), q="pool")
                K.cp(KTh((A_, slice(a0, a0 + NPG))), kT((A_, slice(0, NPG))), "pool")
                for t in range(NTP):
                    p = mps(128, 128)
                    K.tr(p, vT((A_, slice(t * 128, (t + 1) * 128))), ident)
                    vt = vtokp[t % 2]
                    K.cp(vt(), p, "act")
                    K.cp(Vh((A_, a0 // 128 + t, A_)), p, "dve")
                    K.dma(V(newv.ap[li, a0 + t * 128:a0 + (t + 1) * 128, h * 128:(h + 1) * 128], (newv.b[li * 4 + h],)), vt())
                if has_s:
                    p = mps(NS, 128)
                    K.tr(p, vT((A_, slice(NPG, NT))), ident)
                    vt = vtokp[0]
                    K.cp(vt((slice(0, NS), A_)), p, "act")
                    K.dma(V(newv.ap[li, SEQ:SEQ + NS, h * 128:(h + 1) * 128], (newv.b[li * 4 + h],)), vt((slice(0, NS), A_)))
                attn_prompt(li, h, g, ptiles)
                rmsnorm_gate(oTh, None, pcs(li, 3), ptiles, 6 + h, const_gate=1.0 - lam_init[li])
            if has_s:
                attn_sample(li, NPG)
                for h in range(4):
                    K.cp(oTh((A_, slice(NPG, NT))), osamp((A_, h, A_)), "pool")
                    rmsnorm_gate(oTh, None, pcs(li, 3), stile, 6 + h, const_gate=1.0 - lam_init[li])
            if dbg is not None and dbg.get("stage") == "diff":
                return "diff"

            K.memset(clr1((slice(0, 32), slice(0, NT))), 1.0)
            dense(wblk("lr"), KC, xT16, dtiles, lambda t0, n, pv: K.cp(clr1((slice(0, 16), slice(t0, t0 + n))), pv, "act"), M=16)
            for pr in range(3):
                dense(wblk(f"cq{pr}"), KC, xT16, dtiles, ev_copy(qT, 0.125))
                dense(wblk(f"ck{pr}"), KC, xT16, dtiles, ev_copy(kT))
                vts, gts, ots = (vT, vTb), (zg, zgb), (oTh, oThb)
                for hh in range(2):
                    hd = 2 * pr + hh
                    dense(wblk(f"cv{hd}"), KC, xT16, dtiles, ev_copy(vts[hh]))
                    dense(wblk(f"cr{hd}"), KC, xT16, dtiles,
                          lambda t0, n, pv, gt=gts[hh]: K.act(gt((A_, slice(t0, t0 + n))), pv, AF.Silu))
                Sp = Sl((A_, li, pr, A_), li * 3 + pr)
                for ci_, (c0, Cn, gi) in enumerate(chunks):
                    gla_chunk(li, pr, c0, Cn, Sp, ci_ % 2, vts, ots)
                if last:
                    K.dma(V(o_gla.ap[li, 0, pr], o_gla.b), Sp)
                for b, (c0, Cn, gi) in enumerate(schunks):
                    ss = sst[sctr[0] % 2]
                    sctr[0] += 1
                    K.dma(ss(), V(st_gla.ap[li, b, pr], st_gla.b))
                    gla_chunk(li, pr, c0, Cn, ss(), b % 2, vts, ots)
                    K.dma(V(o_gla.ap[li, 1 + b, pr], o_gla.b), ss())
                for hh in range(2):
                    rmsnorm_gate(ots[hh], gts[hh], pcs(li, 4), dtiles, 10 + 2 * pr + hh)
            if dbg is not None and dbg.get("stage") == "gla":
                return "gla"

            for c in range(KC):
                if li == 0:
                    K.dma(hpre((A_, c, slice(0, NPG)), c), V(xpT.ap[c * 128:(c + 1) * 128, a0:a0 + NPG], xpT.b))
                    if has_s:
                        K.dma(hpre((A_, c, slice(NPG, NT)), c), V(xsT.ap[c * 128:(c + 1) * 128, :], xsT.b))
                w = wload(w_out.ap[li, c], w_out, KC)

                def ev_o(t0, n, pv, c=c):
                    cs = slice(t0, t0 + n)
                    src = hpre((A_, c, cs), c) if li == 0 else xT16((A_, c, cs), c)
                    K.stt(hpre((A_, c, cs), c), src, ALPHA, pv, ALU.mult, ALU.add)
                dense(w, KC, mixT, dtiles, ev_o)
            layernorm(li, 8, 8 + KC, ttiles, False, cols_abs)
            for qf in range(NQ):
                for j in range(FQ):
                    jj = qf * FQ + j
                    wg = wload(w_gu.ap[li, 2 * jj], w_gu, KC)
                    wu = wload(w_gu.ap[li, 2 * jj + 1], w_gu, KC)
                    for ti, (t0, n) in enumerate(dtiles):
                        sg = sgt[ti % 2]
                        dense(wg, KC, xT16, [(t0, n)], lambda t0, n, pv, sg=sg: K.act(sg((A_, slice(0, n))), pv, AF.Silu))
                        dense(wu, KC, xT16, [(t0, n)],
                              lambda t0, n, pv, sg=sg, j=j: K.tt(mixT((A_, j, slice(t0, t0 + n)), j), sg((A_, slice(0, n))), pv, ALU.mult))
                for c in range(KC):
                    wd = wload(w_dn.ap[li, qf, c], w_dn, FQ)

                    def ev_d(t0, n, pv, c=c, qf=qf):
                        cs = slice(t0, t0 + n)
                        hv = hpre((A_, c, cs), c)
                        if qf == 0:
                            K.stt(hv, hv, ALPHA, pv, ALU.mult, ALU.add)
                        else:
                            K.tt(hv, hv, pv, ALU.add)
                    dense(wd, FQ, mixT, dtiles, ev_d)
            layernorm(li, 8 + 2 * KC, 8 + 3 * KC, ttiles, li == DEPTH - 1, cols_abs)
            return None

        def load_x(g):
            NPG = GSZ[g]
            a0 = GOFF[g]
            K.dma(rope((A_, A_, slice(0, NPG))), V(ropec.ap[:, :, a0:a0 + NPG].rearrange("a p n -> p a n"), ropec.b))
            if g == 0:
                K.dma(rope((A_, A_, slice(NPG, NPG + NS))), V(ropec.ap[:, :, SEQ:SEQ + NS].rearrange("a p n -> p a n"), ropec.b))
            for c in range(KC):
                K.dma(xT16((A_, c, slice(0, NPG)), c), V(xpT.ap[c * 128:(c + 1) * 128, a0:a0 + NPG], xpT.b), q="pool")
                if g == 0:
                    K.dma(xT16((A_, c, slice(NPG, NPG + NS)), c), V(xsT.ap[c * 128:(c + 1) * 128, :], xsT.b), q="pool")

        stop = None
        for g in range(NG):
            load_x(g)
            for li in range(DEPTH):
                stop = run_layer(g, li)
                if stop:
                    break
            if stop:
                break
        if stop:
            NT0 = GSZ[0] + NS
            hs = {"gdn": range(0, 6), "diff": range(6, 10), "gla": range(10, 16)}[stop]
            for i, h in enumerate(hs):
                K.cp(hpre((A_, i, slice(0, NT0)), i), mixT((A_, h, slice(0, NT0)), h))
                K.dma(V(dbg_out.ap[i, :, 0:NT0], dbg_out.b), hpre((A_, i, slice(0, NT0)), i))
        S.emit(st)
    return nc


def _blk(Wcols, kc):
    n = Wcols.shape[1]
    out = np.zeros((128, kc, 128), np.float32)
    out[:, :, :n] = Wcols.reshape(kc, 128, n).transpose(1, 0, 2)
    return out


def prepare_shared(inp, SEQ, PAST):
    f32 = np.float32
    sh = {}
    NPOOL = inp["cache_k"].shape[1]
    sh["cache_k"] = np.ascontiguousarray(inp["cache_k"]).reshape(DEPTH, NPOOL * 128, 512)
    sh["cache_v"] = np.ascontiguousarray(inp["cache_v"]).reshape(DEPTH, NPOOL * 128, 512)
    blks = in_blocks()
    w_in = np.zeros((DEPTH, N_IN_BLK, 128, KC, 128), f32)
    w_out = np.zeros((DEPTH, KC, 128, KC, 128), f32)
    w_gu = np.zeros((DEPTH, 2 * FC, 128, KC, 128), f32)
    w_dn = np.zeros((DEPTH, NQ, KC, 128, FQ, 128), f32)
    for li in range(DEPTH):
        W = inp["w_in"][li]
        for i, (_, c0, n) in enumerate(blks):
            w_in[li, i] = _blk(W[:, c0:c0 + n], KC)
        W = inp["w_out"][li]
        for c in range(KC):
            w_out[li, c] = _blk(W[:, c * 128:(c + 1) * 128], KC)
        W = inp["w_gate_up"][li]
        for j in range(FC):
            w_gu[li, 2 * j] = _blk(W[:, j * 128:(j + 1) * 128], KC)
            w_gu[li, 2 * j + 1] = _blk(W[:, DFF + j * 128:DFF + (j + 1) * 128], KC)
        W = inp["w_down"][li]
        for hf in range(NQ):
            for c in range(KC):
                w_dn[li, hf, c] = _blk(W[hf * FQ * 128:(hf + 1) * FQ * 128, c * 128:(c + 1) * 128], FQ)
    sh["w_in"], sh["w_out"], sh["w_gu"], sh["w_dn"] = w_in, w_out, w_gu, w_dn
    sh["convw"] = np.ascontiguousarray(inp["gdn_conv_w"].reshape(DEPTH, 4, 18, 128).transpose(0, 3, 2, 1))
    NPC = 8 + 4 * KC
    pcol = np.zeros((DEPTH, 128, NPC), f32)
    pcol[:, 0:6, 0] = inp["gdn_a_log"]
    pcol[:, 0:6, 1] = inp["gdn_dt_bias"]
    pcol[:, :, 2] = inp["gdn_norm_g"]
    pcol[:, :, 3] = inp["diff_norm_g"]
    pcol[:, :, 4] = inp["gla_norm_g"]
    for i, nm in enumerate(("ln1_g", "ln1_b", "ln2_g", "ln2_b")):
        pcol[:, :, 8 + i * KC:8 + (i + 1) * KC] = inp[nm].reshape(DEPTH, KC, 128).transpose(0, 2, 1)
    sh["pcol"] = pcol
    sh["lamv"] = np.ascontiguousarray(inp["diff_lambda"].reshape(1, DEPTH * 256))
    wa2b = np.zeros((DEPTH, 32, 384), f32)
    wa2b[:, 0:16] = inp["gla_wa2"]
    wa2b[:, 16] = inp["gla_ba"]
    sh["wa2b"] = wa2b
    c = np.zeros((128, 6, 128), f32)
    p = np.arange(128)
    c[:, 0] = np.eye(128)
    c[:, 1] = (p[:, None] <= p[None, :])
    c[:, 2] = (p[:, None] < p[None, :])
    c[:, 3] = 1.0
    c[:, 5] = 1.0
    for m0 in (0, 64):
        for d in range(8):
            c[m0 + d + 8, 4, m0 + d] = 1.0
            c[m0 + d, 4, m0 + d + 8] = 1.0
    sh["consts"] = c
    NTOK = SEQ + NS
    pos = np.concatenate([np.arange(SEQ), np.tile(PAST + np.arange(TS), NB_S)]).astype(f32)
    inv = (1.0 / (f32(500000.0) ** (np.arange(8, dtype=f32) * f32(2.0) / f32(16)))).astype(f32)
    ang = (pos[None, :] * inv[:, None]).astype(f32)
    rc = np.zeros((2, 128, NTOK), f32)
    rc[0] = 1.0
    for m0 in (0, 64):
        rc[0, m0:m0 + 8] = np.cos(ang)
        rc[0, m0 + 8:m0 + 16] = np.cos(ang)
        rc[1, m0:m0 + 8] = -np.sin(ang)
        rc[1, m0 + 8:m0 + 16] = np.sin(ang)
    sh["ropec"] = rc
    return sh


def prepare_inputs(inp, SEQ, PAST):
    sh = prepare_shared(inp, SEQ, PAST)
    maps = []
    for c in range(8):
        b = c // 2
        sl = slice(NB_S * c, NB_S * (c + 1))
        m = dict(sh)
        m["xpT"] = np.ascontiguousarray(inp["x_prompt"][b].T)
        m["xsT"] = np.ascontiguousarray(inp["x_sample"][sl].reshape(NS, D).T)
        m["ptab"] = np.ascontiguousarray(inp["page_table"][sl].reshape(1, -1)).astype(np.int32)
        m["st_gdn"] = np.ascontiguousarray(inp["state_gdn"][:, sl])
        m["st_conv"] = np.ascontiguousarray(
            inp["state_gdn_conv"][:, sl].reshape(DEPTH, NB_S, 3, 18, 128).transpose(0, 1, 4, 3, 2))
        m["st_gla"] = np.ascontiguousarray(inp["state_gla"][:, sl].reshape(DEPTH, NB_S, 3, 128, 128))
        maps.append(m)
    return maps


def assemble(results, SEQ):
    f32 = np.float32
    B = 4
    y_p = np.zeros((B, SEQ, D), f32)
    y_s = np.zeros((8 * NB_S, TS, D), f32)
    nk_p = np.zeros((DEPTH, B, SEQ, 4, 128), f32)
    nv_p = np.zeros((DEPTH, B, SEQ, 4, 128), f32)
    gs_p = np.zeros((DEPTH, B, 6, 128, 128), f32)
    gc_p = np.zeros((DEPTH, B, 3, 2304), f32)
    ls_p = np.zeros((DEPTH, B, 6, 64, 128), f32)
    nk_s = np.zeros((DEPTH, 8 * NB_S, TS, 4, 128), f32)
    nv_s = np.zeros((DEPTH, 8 * NB_S, TS, 4, 128), f32)
    gs_s = np.zeros((DEPTH, 8 * NB_S, 6, 128, 128), f32)
    gc_s = np.zeros((DEPTH, 8 * NB_S, 3, 2304), f32)
    ls_s = np.zeros((DEPTH, 8 * NB_S, 6, 64, 128), f32)
    for c in range(8):
        r = results[c]
        sl = slice(NB_S * c, NB_S * (c + 1))
        yT = np.asarray(r["yT"])
        nk = np.asarray(r["newk"])
        nv = np.asarray(r["newv"])
        og = np.asarray(r["o_gdn"])
        oc = np.asarray(r["o_conv"])
        ol = np.asarray(r["o_gla"])
        conv = oc.transpose(0, 1, 4, 3, 2).reshape(DEPTH, 1 + NB_S, 3, 2304)
        if c % 2 == 0:
            b = c // 2
            y_p[b] = yT[:, :SEQ].T
            nk_p[:, b] = nk[:, :, :, :SEQ].transpose(0, 3, 1, 2)
            nv_p[:, b] = nv[:, :SEQ].reshape(DEPTH, SEQ, 4, 128)
            gs_p[:, b] = og[:, 0]
            gc_p[:, b] = conv[:, 0]
            ls_p[:, b] = ol[:, 0].reshape(DEPTH, 6, 64, 128)
        y_s[sl] = yT[:, SEQ:].T.reshape(NB_S, TS, D)
        nk_s[:, sl] = nk[:, :, :, SEQ:].transpose(0, 3, 1, 2).reshape(DEPTH, NB_S, TS, 4, 128)
        nv_s[:, sl] = nv[:, SEQ:].reshape(DEPTH, NB_S, TS, 4, 128)
        gs_s[:, sl] = og[:, 1:]
        gc_s[:, sl] = conv[:, 1:]
        ls_s[:, sl] = ol[:, 1:].reshape(DEPTH, NB_S, 6, 64, 128)
    return (y_p, y_s, nk_p, nv_p, gs_p, gc_p, ls_p, nk_s, nv_s, gs_s, gc_s, ls_s)


def run_step(inputs, SEQ, PAST):
    inp = {k: np.asarray(v) for k, v in inputs.items()}
    NPOOL = inp["cache_k"].shape[1]
    nc = build_program(SEQ, PAST, NPOOL)
    in_maps = prepare_inputs(inp, SEQ, PAST)
    res = run_bass_kernel_spmd(nc, in_maps, core_ids=list(range(8)))
    return assemble(res.results, SEQ)


def kernel(**inputs):
    SEQ = int(np.asarray(inputs["x_prompt"]).shape[1])
    PAST = int(np.asarray(inputs["page_table"]).shape[1]) * int(np.asarray(inputs["cache_k"]).shape[2])
    return run_step(inputs, SEQ, PAST)
```
